# Optimizing a Trainium2 kernel written in Bass

```python
import math
import jax, jax.numpy as jnp
from jax import lax
import numpy as np

D_MODEL = 1024
BATCH = 4
SEQ = 4096
DEPTH = 1
DEC_BATCH = 128
DEC_SEQ = 8
PAST_LEN = 8192
PAGE_SIZE = 128

ATTN_GROUPS = ((128, 1), (512, 4), (2048, 16))
N_GROUPS = 3
HEADS_PER_GROUP = 4
ATTN_HEAD_DIM = 128
N_ATTN_HEADS = N_GROUPS * HEADS_PER_GROUP
ATTN_QKV_W = N_ATTN_HEADS * ATTN_HEAD_DIM
ATTN_OUT_W = HEADS_PER_GROUP * ATTN_HEAD_DIM
BAND_BLOCK = 128
HG_EXPAND = 128
HG_HEADS = D_MODEL // HG_EXPAND
HG_DK = HG_EXPAND
HG_DV = D_MODEL // HG_HEADS
HG_W = HG_HEADS * HG_DK
HG_CHUNK = 64
D_FF = 4 * D_MODEL
SPLITS = [ATTN_QKV_W, 2 * ATTN_QKV_W, 3 * ATTN_QKV_W,
          3 * ATTN_QKV_W + HG_W, 3 * ATTN_QKV_W + 2 * HG_W, 3 * ATTN_QKV_W + 3 * HG_W,
          3 * ATTN_QKV_W + 4 * HG_W, 3 * ATTN_QKV_W + 4 * HG_W + D_MODEL]
N_IN = 3 * ATTN_QKV_W + 4 * HG_W + 2 * D_MODEL
DN_ALPHA = (2 * DEPTH) ** 0.25
DN_BETA = (8 * DEPTH) ** -0.25
LN_EPS = 1e-5
RMS_EPS = 1e-6

kernel_name = "hybrid_dilated_attn_hgrn2_decoder_step"

F32 = jnp.float32


def _alibi_slopes():
    h = jnp.arange(1, N_ATTN_HEADS + 1, dtype=F32)
    return (2.0 ** (-8.0 * h / N_ATTN_HEADS)).reshape(N_GROUPS, HEADS_PER_GROUP)


def _layer_norm(h, g, b, dtype):
    h = h.astype(F32)
    mu = jnp.mean(h, axis=-1, keepdims=True)
    var = jnp.mean(jnp.square(h - mu), axis=-1, keepdims=True)
    return ((h - mu) * lax.rsqrt(var + LN_EPS) * g.astype(F32) + b.astype(F32)).astype(dtype)


def _dilated_prompt(q, k, v, window, dil, slopes):
    b, s, h, dh = q.shape
    n_steps = window // dil
    m = s // dil
    nb = -(-m // BAND_BLOCK)
    mp = nb * BAND_BLOCK

    def to_res(a):
        a = a.astype(F32).reshape(b, m, dil, h, dh).transpose(0, 2, 1, 3, 4).reshape(b * dil, m, h, dh)
        a = jnp.pad(a, ((0, 0), (0, mp - m), (0, 0), (0, 0)))
        return a.reshape(b * dil, nb, BAND_BLOCK, h, dh)

    def band(a):
        prev = jnp.pad(a[:, :-1], ((0, 0), (1, 0), (0, 0), (0, 0), (0, 0)))
        return jnp.concatenate([prev, a], axis=2)

    qb = to_res(q)
    kk = band(to_res(k))
    vv = band(to_res(v))
    scores = jnp.einsum('bnqhd,bnkhd->bnhqk', qb, kk) * (dh ** -0.5)
    qi = jnp.arange(BAND_BLOCK)[:, None]
    kj = jnp.arange(2 * BAND_BLOCK)[None, :]
    steps = BAND_BLOCK + qi - kj
    key_m = (jnp.arange(nb)[:, None] - 1) * BAND_BLOCK + kj
    valid = ((steps >= 0) & (steps <= n_steps))[None] & (key_m >= 0)[:, None, :]
    alibi = -slopes[:, None, None] * (dil * steps).astype(F32)[None]
    scores = jnp.where(valid[None, :, None], scores + alibi[None, None], -jnp.inf)
    lse = jax.nn.logsumexp(scores, axis=-1)
    p = jnp.exp(scores - lse[..., None])
    o = jnp.einsum('bnhqk,bnkhd->bnqhd', p, vv)
    o = o.reshape(b, dil, mp, h, dh)[:, :, :m].transpose(0, 2, 1, 3, 4).reshape(b, s, h, dh)
    lse = lse.transpose(0, 1, 3, 2).reshape(b, dil, mp, h)[:, :, :m].transpose(0, 2, 1, 3).reshape(b, s, h)
    return o, lse


def _dilated_sample(q, k, v, buf, window, dil, slopes):
    bd, t, h, dh = q.shape
    L = buf.shape[1]
    n_steps = window // dil
    ext = jnp.concatenate([buf, jnp.stack([k, v], axis=2).astype(buf.dtype)], axis=1)
    st = jnp.arange(n_steps + 1)[None, :]
    idx = L + jnp.arange(t)[:, None] - dil * st
    valid = idx >= 0
    g = jnp.take(ext, jnp.clip(idx, 0), axis=1).astype(F32)
    scores = jnp.einsum('bthd,btkhd->bhtk', q.astype(F32), g[:, :, :, 0]) * (dh ** -0.5)
    scores = scores - slopes[:, None, None] * (dil * st).astype(F32)
    scores = jnp.where(valid[None, None], scores, -jnp.inf)
    lse = jax.nn.logsumexp(scores, axis=-1)
    p = jnp.exp(scores - lse[..., None])
    o = jnp.einsum('bhtk,btkhd->bthd', p, g[:, :, :, 1])
    new_len = min(window, L + t)
    return o, lse.transpose(0, 2, 1), ext[:, L + t - new_len:]


def _hgrn2_chunked(q, k, v, logf, s0):
    b, l, h, dk = q.shape
    dv = v.shape[-1]
    c = min(HG_CHUNK, l)
    nc = -(-l // c)
    lp = nc * c

    def chunks(a):
        a = jnp.pad(a, ((0, 0), (0, lp - l), (0, 0), (0, 0)))
        return a.reshape(b, nc, c, h, a.shape[-1]).transpose(1, 0, 3, 2, 4)

    causal = jnp.tril(jnp.ones((c, c), dtype=bool))

    def step(state, inp):
        qc, kc, vc, gc = inp
        gcum = jnp.cumsum(gc, axis=2)
        diff = gcum[:, :, :, None, :] - gcum[:, :, None, :, :]
        decay = jnp.exp(jnp.where(causal[:, :, None], diff, -jnp.inf))
        attn = jnp.einsum('bhtk,bhsk,bhtsk->bhts', qc, kc, decay)
        o = (jnp.einsum('bhts,bhsv->bhtv', attn, vc)
             + jnp.einsum('bhtk,bhkv->bhtv', qc * jnp.exp(gcum), state))
        g_last = gcum[:, :, -1:, :]
        new_state = (jnp.exp(g_last[:, :, 0, :, None]) * state
                     + jnp.einsum('bhsk,bhsv->bhkv', kc * jnp.exp(g_last - gcum), vc))
        return new_state, o

    s_fin, o = lax.scan(step, s0, (chunks(q), chunks(k), chunks(v), chunks(logf)))
    o = o.transpose(1, 0, 3, 2, 4).reshape(b, lp, h, dv)[:, :l]
    return o, s_fin


def _layer(x, c, kv_bufs, hg_state, lb, w_ada, b_ada, w_in, hg_norm_w, w_branch_a, w_branch_b,
           w_out, ln1_g, ln1_b, w_up, b_up, w_down, b_down, ln2_g, ln2_b):
    b, s, _ = x.shape
    dtype = x.dtype
    mod = jnp.einsum('bd,dn->bn', jax.nn.silu(c), w_ada) + b_ada
    sh1, sc1, g1, sh2, sc2, g2 = jnp.split(mod[:, None, :], 6, axis=-1)
    u = x * (1 + sc1) + sh1
    proj = jnp.einsum('bsd,dn->bsn', u, w_in)
    qa, ka, va, qh, fh, ih, ogh, gate_a, gate_b = jnp.split(proj, SPLITS, axis=-1)

    def heads(a):
        return a.reshape(b, s, N_GROUPS, HEADS_PER_GROUP, ATTN_HEAD_DIM)
    qa, ka, va = heads(qa), heads(ka), heads(va)
    slopes = _alibi_slopes()
    outs, lses, new_bufs = [], [], []
    for gi, (win, dil) in enumerate(ATTN_GROUPS):
        q_g, k_g, v_g = qa[:, :, gi], ka[:, :, gi], va[:, :, gi]
        if kv_bufs is None:
            o_g, lse_g = _dilated_prompt(q_g, k_g, v_g, win, dil, slopes[gi])
            buf = jnp.stack([k_g, v_g], axis=2)[:, s - min(win, s):]
        else:
            o_g, lse_g, buf = _dilated_sample(q_g, k_g, v_g, kv_bufs[gi], win, dil, slopes[gi])
        outs.append(o_g)
        lses.append(lse_g)
        new_bufs.append(buf)
    w_mix = jax.nn.softmax(jnp.stack(lses), axis=0)
    attn = jnp.sum(w_mix[..., None] * jnp.stack(outs), axis=0).reshape(b, s, ATTN_OUT_W)

    hq = jax.nn.silu(qh.astype(F32)).reshape(b, s, HG_HEADS, HG_DK)
    f = lb + (1.0 - lb) * jax.nn.sigmoid(fh.astype(F32))
    logf = jnp.log(f).reshape(b, s, HG_HEADS, HG_DK)
    hk = (1.0 - f).reshape(b, s, HG_HEADS, HG_DK)
    hv = ih.astype(F32).reshape(b, s, HG_HEADS, HG_DV)
    s0 = (jnp.zeros((b, HG_HEADS, HG_DK, HG_DV), F32) if hg_state is None else hg_state.astype(F32))
    o_h, s_fin = _hgrn2_chunked(hq, hk, hv, logf, s0)
    o_h = o_h * lax.rsqrt(jnp.mean(jnp.square(o_h), axis=-1, keepdims=True) + RMS_EPS) * hg_norm_w.astype(F32)
    hg = o_h.reshape(b, s, HG_W) * jax.nn.silu(ogh.astype(F32))

    branch_a = jnp.einsum('bsn,nd->bsd', attn, w_branch_a)
    branch_b = jnp.einsum('bsn,nd->bsd', hg, w_branch_b)
    merged = jax.nn.sigmoid(gate_a.astype(F32)) * branch_a + jax.nn.sigmoid(gate_b.astype(F32)) * branch_b
    mix = jnp.einsum('bsd,de->bse', merged, w_out)
    x1 = _layer_norm(DN_ALPHA * x + g1 * mix, ln1_g, ln1_b, dtype)

    u2 = x1 * (1 + sc2) + sh2
    hid = jnp.square(jax.nn.relu(jnp.einsum('bsd,df->bsf', u2, w_up) + b_up))
    ff = jnp.einsum('bsf,fd->bsd', hid, w_down) + b_down
    x2 = _layer_norm(DN_ALPHA * x1 + g2 * ff, ln2_g, ln2_b, dtype)
    st_dtype = dtype if hg_state is None else hg_state.dtype
    return x2, new_bufs, s_fin.astype(st_dtype)


def setup_inputs(seed: int = 0) -> dict:
    key = jax.random.key(seed)
    ks = jax.random.split(key, 32)
    nrm = lambda k, shape, sc: jax.random.normal(k, shape, F32) * sc
    L = DEPTH
    return {
        "x_prompt": nrm(ks[0], (BATCH, SEQ, D_MODEL), 1.0),
        "x_sample": nrm(ks[1], (DEC_BATCH, DEC_SEQ, D_MODEL), 1.0),
        "c_prompt": nrm(ks[2], (BATCH, D_MODEL), 1.0),
        "c_sample": nrm(ks[3], (DEC_BATCH, D_MODEL), 1.0),
        "cache_kv_w128": nrm(ks[4], (L, DEC_BATCH, min(ATTN_GROUPS[0][0], PAST_LEN), 2, HEADS_PER_GROUP, ATTN_HEAD_DIM), 1.0),
        "cache_kv_w512": nrm(ks[5], (L, DEC_BATCH, min(ATTN_GROUPS[1][0], PAST_LEN), 2, HEADS_PER_GROUP, ATTN_HEAD_DIM), 1.0),
        "cache_kv_w2048": nrm(ks[6], (L, DEC_BATCH, min(ATTN_GROUPS[2][0], PAST_LEN), 2, HEADS_PER_GROUP, ATTN_HEAD_DIM), 1.0),
        "state_hgrn": nrm(ks[7], (L, DEC_BATCH, HG_HEADS, HG_DK, HG_DV), 0.5),
        "w_ada": nrm(ks[8], (L, D_MODEL, 6 * D_MODEL), 0.5 * D_MODEL ** -0.5),
        "b_ada": nrm(ks[9], (L, 6 * D_MODEL), 0.02),
        "w_in": nrm(ks[10], (L, D_MODEL, N_IN), D_MODEL ** -0.5),
        "lb_param": nrm(ks[11], (L + 1, HG_W), 0.5),
        "hg_norm_w": 1.0 + nrm(ks[12], (L, HG_DV), 0.02),
        "w_branch_a": nrm(ks[13], (L, ATTN_OUT_W, D_MODEL), ATTN_OUT_W ** -0.5),
        "w_branch_b": nrm(ks[14], (L, HG_W, D_MODEL), HG_W ** -0.5),
        "w_out": nrm(ks[15], (L, D_MODEL, D_MODEL), DN_BETA * D_MODEL ** -0.5),
        "ln1_g": 1.0 + nrm(ks[16], (L, D_MODEL), 0.02),
        "ln1_b": nrm(ks[17], (L, D_MODEL), 0.02),
        "w_up": nrm(ks[18], (L, D_MODEL, D_FF), D_MODEL ** -0.5),
        "b_up": nrm(ks[19], (L, D_FF), 0.02),
        "w_down": nrm(ks[20], (L, D_FF, D_MODEL), DN_BETA * D_FF ** -0.5),
        "b_down": nrm(ks[21], (L, D_MODEL), 0.02),
        "ln2_g": 1.0 + nrm(ks[22], (L, D_MODEL), 0.02),
        "ln2_b": nrm(ks[23], (L, D_MODEL), 0.02),
    }


def reference(x_prompt, x_sample, c_prompt, c_sample, cache_kv_w128, cache_kv_w512, cache_kv_w2048,
              state_hgrn, w_ada, b_ada, w_in, lb_param, hg_norm_w, w_branch_a, w_branch_b, w_out,
              ln1_g, ln1_b, w_up, b_up, w_down, b_down, ln2_g, ln2_b):
    lower_bounds = jnp.cumsum(jax.nn.softmax(lb_param.astype(F32), axis=0), axis=0)
    yp, ys = x_prompt, x_sample
    p128, p512, p2048, php = [], [], [], []
    s128, s512, s2048, shs = [], [], [], []
    for l in range(DEPTH):
        yp, bufs_p, st_p = _layer(yp, c_prompt, None, None, lower_bounds[l], w_ada[l], b_ada[l], w_in[l],
                                  hg_norm_w[l], w_branch_a[l], w_branch_b[l], w_out[l], ln1_g[l], ln1_b[l],
                                  w_up[l], b_up[l], w_down[l], b_down[l], ln2_g[l], ln2_b[l])
        ys, bufs_s, st_s = _layer(ys, c_sample, (cache_kv_w128[l], cache_kv_w512[l], cache_kv_w2048[l]),
                                  state_hgrn[l], lower_bounds[l], w_ada[l], b_ada[l], w_in[l],
                                  hg_norm_w[l], w_branch_a[l], w_branch_b[l], w_out[l], ln1_g[l], ln1_b[l],
                                  w_up[l], b_up[l], w_down[l], b_down[l], ln2_g[l], ln2_b[l])
        p128.append(bufs_p[0]); p512.append(bufs_p[1]); p2048.append(bufs_p[2]); php.append(st_p)
        s128.append(bufs_s[0]); s512.append(bufs_s[1]); s2048.append(bufs_s[2]); shs.append(st_s)
    return (yp, ys, jnp.stack(p128), jnp.stack(p512), jnp.stack(p2048), jnp.stack(php),
            jnp.stack(s128), jnp.stack(s512), jnp.stack(s2048), jnp.stack(shs))
```

```python
import contextlib
import numpy as np
import concourse.bass as bass
import concourse.mybir as mybir
from concourse.bass_utils import run_bass_kernel_spmd

F32 = mybir.dt.float32
BF16 = mybir.dt.bfloat16
ALU = mybir.AluOpType
AF = mybir.ActivationFunctionType

SAME_ENGINE_SYNC = "raw"


def _need_sync(p, o, d):
    if p.is_dma or p.q != o.q:
        return True
    if p.q == "tensor":
        return False
    if SAME_ENGINE_SYNC is True:
        return True
    if SAME_ENGINE_SYNC == "raw":
        return d in o.raw
    return False
COMPUTE = ("tensor", "vector", "scalar", "gpsimd")

NCORES = 8
D = 1024
KC = 8
HALO = 2048
MAIN = 2048
SAMP = 128
NT = HALO + MAIN + SAMP
M0 = HALO
S0 = HALO + MAIN
NO = MAIN + SAMP
GROUPS = ((128, 1), (512, 4), (2048, 16))
ALPHA = 2.0 ** 0.25
LN_EPS = 1e-5 / (ALPHA * ALPHA)
RMS_EPS = 1e-6
QSCALE = 128.0 ** -0.5
C_Q, C_K, C_V = 0, 1536, 3072
C_QH, C_FH, C_IH, C_OG = 4608, 5632, 6656, 7680
C_GA, C_GB = 8704, 9728


def slope(g, h):
    return 2.0 ** (-8.0 * (g * 4 + h + 1) / 12.0)


class Buf:
    __slots__ = ("name", "t", "last_w", "readers", "excl")

    def __init__(self, name, t=None, excl=False):
        self.name = name
        self.t = t
        self.excl = excl
        self.last_w = None
        self.readers = []

    def __getitem__(self, k):
        return self.t[k]


class Op:
    __slots__ = ("q", "fn", "deps", "raw", "is_dma", "signal", "sigval", "sem", "idx", "prev")

    def __init__(self, q, fn, is_dma):
        self.q = q
        self.fn = fn
        self.deps = set()
        self.raw = set()
        self.is_dma = is_dma
        self.signal = False
        self.sigval = None
        self.sem = None
        self.prev = 0


class Prog:
    def __init__(self, nc):
        self.nc = nc
        self.ops = []

    def op(self, q, fn, reads=(), writes=(), dma=False):
        o = Op(q, fn, dma)
        o.idx = len(self.ops)
        for b in reads:
            if b.last_w is not None:
                o.deps.add(b.last_w)
                o.raw.add(b.last_w)
            if b.excl:
                for r in b.readers:
                    if self.ops[r].q != q:
                        o.deps.add(r)
        for b in writes:
            if b.last_w is not None:
                o.deps.add(b.last_w)
            for r in b.readers:
                o.deps.add(r)
        for b in reads:
            b.readers.append(o.idx)
        for b in writes:
            b.last_w = o.idx
            b.readers = []
        o.deps.discard(o.idx)
        self.ops.append(o)
        return o

    def barrier(self):
        self.ops.append(None)

    def emit(self, final_wait_q="sync"):
        nc = self.nc
        ops = self.ops
        lastq = {}
        for o in ops:
            if o is None:
                for q_, lo in lastq.items():
                    lo.signal = True
                continue
            if not o.is_dma:
                lastq[o.q] = o
        for o in ops:
            if o is None:
                continue
            for d in o.deps:
                p = ops[d]
                if _need_sync(p, o, d):
                    p.signal = True
        for o in ops:
            if o is not None and o.is_dma:
                o.signal = True
        stack = contextlib.ExitStack()
        qsem = {q: stack.enter_context(nc.semaphore("s_" + q)) for q in COMPUTE}
        pool_sizes = {"sync": 16, "gpsimd": 8, "scalar": 8}
        dpool = {q: [stack.enter_context(nc.semaphore("d_%s%d" % (q, i))) for i in range(n)]
                 for q, n in pool_sizes.items()}
        dcount = {q: [0] * n for q, n in pool_sizes.items()}
        dnext = {q: 0 for q in pool_sizes}
        qcount = {q: 0 for q in COMPUTE}
        snaps = {}
        for oi, o in enumerate(ops):
            if o is None:
                snaps[oi] = (dict(qcount), {q: list(v) for q, v in dcount.items()})
                continue
            if not o.signal:
                continue
            if o.is_dma:
                i = dnext[o.q]
                dnext[o.q] = (i + 1) % pool_sizes[o.q]
                o.sem = dpool[o.q][i]
                o.prev = dcount[o.q][i]
                dcount[o.q][i] += 16
                o.sigval = dcount[o.q][i]
            else:
                o.sem = qsem[o.q]
                qcount[o.q] += 1
                o.sigval = qcount[o.q]
        with nc.Block() as block:
            def make(qname):
                def body(eng):
                    known = {}
                    for oi, o in enumerate(ops):
                        if o is None:
                            qc, dc = snaps[oi]
                            for q2 in COMPUTE:
                                if qc[q2] > known.get(id(qsem[q2]), 0):
                                    eng.wait_ge(qsem[q2], qc[q2])
                                    known[id(qsem[q2])] = qc[q2]
                            for q2, n in pool_sizes.items():
                                for i in range(n):
                                    if dc[q2][i] > known.get(id(dpool[q2][i]), 0):
                                        eng.wait_ge(dpool[q2][i], dc[q2][i])
                                        known[id(dpool[q2][i])] = dc[q2][i]
                            continue
                        if o.q != qname:
                            continue
                        for d in sorted(o.deps):
                            p = ops[d]
                            if not p.signal:
                                continue
                            if not _need_sync(p, o, d):
                                continue
                            key = id(p.sem)
                            if known.get(key, 0) >= p.sigval:
                                continue
                            eng.wait_ge(p.sem, p.sigval)
                            known[key] = p.sigval
                        if o.is_dma and o.prev > 0 and known.get(id(o.sem), 0) < o.prev:
                            eng.wait_ge(o.sem, o.prev)
                            known[id(o.sem)] = o.prev
                        inst = o.fn(eng)
                        if o.signal:
                            inst.then_inc(o.sem, 16 if o.is_dma else 1)
                    if qname == final_wait_q:
                        for q2, n in pool_sizes.items():
                            for i in range(n):
                                if dcount[q2][i] > 0:
                                    eng.wait_ge(dpool[q2][i], dcount[q2][i])
                        for q in COMPUTE:
                            if qcount[q] > 0:
                                eng.wait_ge(qsem[q], qcount[q])
                return body
            block.sync(make("sync"))
            block.scalar(make("scalar"))
            block.vector(make("vector"))
            block.gpsimd(make("gpsimd"))
            block.tensor(make("tensor"))
        stack.close()


def const_tables():
    j = np.arange(128)[:, None].astype(np.float64)
    i = np.arange(128)[None, :].astype(np.float64)
    masks = np.zeros((12, 128, 256), np.float32)
    smask = np.zeros((12, 128, 128), np.float32)
    nmask = np.zeros((12, 128, 128), np.float32)
    for g, (win, dil) in enumerate(GROUPS):
        for h in range(4):
            sl = slope(g, h)
            prev = np.where(i <= j, np.exp(-sl * dil * (128 + i - j)), 0.0)
            own = np.where(i >= j, np.exp(-sl * dil * (i - j)), 0.0)
            masks[g * 4 + h, :, :128] = prev
            masks[g * 4 + h, :, 128:] = own
            nb = win // 128
            p = np.arange(128)[:, None]
            for blk in range(nb):
                t = np.arange(8)[None, :]
                idx = blk * 128 + p
                diff = win + t - idx
                ok = (diff % dil == 0) & (diff >= 0) & (diff <= win)
                smask[g * 4 + h, :, blk * 8:(blk + 1) * 8] = np.where(ok, np.exp(-sl * diff), 0.0)
            kp = np.arange(128)[:, None]
            qp = np.arange(128)[None, :]
            dj = (qp % 8) - (kp % 8)
            ok = (kp // 8 == qp // 8) & (dj >= 0) & (dj % dil == 0)
            nmask[g * 4 + h] = np.where(ok, np.exp(-sl * dj), 0.0)
    s = np.arange(128)[:, None]
    t = np.arange(128)[None, :]
    hmask = np.zeros((2, 128, 128), np.float32)
    hmask[0] = ((s // 64 == t // 64) & (s <= t)).astype(np.float32)
    hmask[1] = ((s // 8 == t // 8) & (s <= t)).astype(np.float32)
    reset = np.ones((2, 128, 512), np.float32)
    reset[0, :, ::64] = 0.0
    reset[1, :, ::8] = 0.0
    oh = (np.arange(128)[:, None] // 8 == np.arange(16)[None, :]).astype(np.float32)
    return dict(masks=masks, smask=smask, nmask=nmask, hmask=hmask, reset=reset, oh=oh)


ARENA_WORDS = 53100


class Region:
    def __init__(self, big, lo, hi):
        self.big, self.lo, self.hi, self.top = big, lo, hi, lo

    def alloc(self, name, shape, dt=F32):
        assert shape[0] == 128
        n = 1
        for s_ in shape[1:]:
            n *= s_
        words = n if dt == F32 else (n + 1) // 2
        words = (words + 7) // 8 * 8
        assert self.top + words <= self.hi, (name, self.top, words, self.hi)
        ap = self.big[:, self.top:self.top + words]
        if dt != F32:
            ap = ap.bitcast(dt)
        ap = ap[:, 0:n]
        if len(shape) == 3:
            ap = ap.rearrange("p (a b) -> p a b", b=shape[2])
        self.top += words
        return Buf(name, ap)


def build_nc(kstop=99, nsc=16):
    nc = bass.Bass("TRN2", target_bir_lowering=False)
    P = Prog(nc)
    st = contextlib.ExitStack()

    def finish():
        P.emit()
        st.close()
        return nc

    def din(name, shape):
        return nc.dram_tensor(name, list(shape), F32, kind="ExternalInput").ap()

    def dout(name, shape):
        return nc.dram_tensor(name, list(shape), F32, kind="ExternalOutput").ap()

    xT = din("xT", [KC, 128, NT])
    cT = din("cT", [128, KC, 17])
    flag_d = din("flag", [128, 1])
    w_ada = din("w_ada", [D, 6144])
    b_ada = din("b_ada", [128, 48])
    w_in = din("w_in", [D, 10752])
    lbp = din("lbp", [128, 2, 8])
    hgw_d = din("hgw", [128, 1])
    w_ba = din("w_ba", [512, D])
    w_bb = din("w_bb", [D, D])
    w_out = din("w_out", [D, D])
    ln1g_d = din("ln1g", [128, 8]); ln1b_d = din("ln1b", [128, 8])
    ln2g_d = din("ln2g", [128, 8]); ln2b_d = din("ln2b", [128, 8])
    w_up = din("w_up", [D, 4096]); bup_d = din("bup", [128, 32])
    w_down = din("w_down", [4096, D]); bdown_d = din("bdown", [128, 8])
    masks_d = din("masks", [12, 128, 256])
    smask_d = din("smask", [12, 128, 128])
    nmask_d = din("nmask", [12, 128, 128])
    hmask_d = din("hmask", [2, 128, 128])
    reset_d = din("reset", [2, 128, 512])
    oh_d = din("oh", [128, 16])
    ident_d = din("ident", [128, 128])
    kvc = [din("kv%d" % w, [nsc, w, 1024]) for (w, _) in GROUPS]
    kTc = [din("kT%d" % w, [nsc, 4, 128, w]) for (w, _) in GROUPS]
    state_d = din("state", [16, 8, 128, 128])

    yT = dout("yT", [KC, 128, NO])
    kTo = [dout("kTo%d" % w, [4, 128, w]) for (w, _) in GROUPS]
    vo = [dout("vo%d" % w, [w, 4, 128]) for (w, _) in GROUPS]
    s_out = dout("s_out", [8, 128, 128])
    kvo = [dout("kvo%d" % w, [nsc, w, 1024]) for (w, _) in GROUPS]
    state_o = dout("state_o", [16, 8, 128, 128])

    big = st.enter_context(nc.sbuf_tensor("big", [128, ARENA_WORDS], F32))

    def psum(name, shape, dt=F32):
        return Buf(name, st.enter_context(nc.psum_tensor(name, list(shape), dt)), excl=True)

    def DMA(out, in_, reads=(), writes=(), q="sync"):
        P.op(q, lambda e: e.dma_start(out=out, in_=in_), reads, writes, dma=True)

    def ACT(out, in_, func, reads, writes, scale=None, bias=None):
        kw = {}
        if scale is not None:
            kw["scale"] = scale
        if bias is not None:
            kw["bias"] = bias
        P.op("scalar", lambda e: e.activation(out=out, in_=in_, func=func, **kw), reads, writes)

    def TT(q, out, in0, in1, op, reads, writes):
        P.op(q, lambda e: e.tensor_tensor(out=out, in0=in0, in1=in1, op=op), reads, writes)

    def TS(q, out, in0, s1, s2, op0, op1, reads, writes):
        if op1 is None:
            P.op(q, lambda e: e.tensor_scalar(out=out, in0=in0, scalar1=s1, scalar2=None, op0=op0), reads, writes)
        else:
            P.op(q, lambda e: e.tensor_scalar(out=out, in0=in0, scalar1=s1, scalar2=s2, op0=op0, op1=op1), reads, writes)

    def STT(out, in0, scalar, in1, op0, op1, reads, writes):
        P.op("vector", lambda e: e.scalar_tensor_tensor(out=out, in0=in0, scalar=scalar, in1=in1, op0=op0, op1=op1),
             reads, writes)

    def COPY(q, out, in_, reads, writes):
        if q == "scalar":
            P.op(q, lambda e: e.activation(out=out, in_=in_, func=AF.Copy), reads, writes)
        else:
            P.op(q, lambda e: e.tensor_copy(out=out, in_=in_), reads, writes)

    def RECIP(out, in_, reads, writes):
        P.op("vector", lambda e: e.reciprocal(out=out, in_=in_), reads, writes)

    def MEMSET(q, out, val, writes):
        P.op(q, lambda e: e.memset(out, val), (), writes)

    def MMG(mms, reads, writes):
        mms = list(mms)

        def fn(e):
            inst = None
            for (o, l, r, s_, p_) in mms:
                inst = e.matmul(o, lhsT=l, rhs=r, start=s_, stop=p_, skip_group_check=True)
            return inst
        P.op("tensor", fn, reads, writes)

    def TRANSPOSE(out, in_, ident, reads, writes):
        P.op("tensor", lambda e: e.transpose(out, in_, ident), reads, writes)

    psb = [psum("ps%d" % i, [128, 512]) for i in range(6)]
    psacc = psum("psacc", [128, 512])
    pst = psum("pst", [128, 1024], BF16)
    ps_i = [0]

    def PS():
        b = psb[ps_i[0] % 6]
        ps_i[0] += 1
        return b

    RA = Region(big, 0, 4608)
    o_uTm = 4608
    o_attn = o_uTm + 8704
    o_uTh = o_attn + 4352
    o_hi = o_uTh + 8192
    RB = Region(big, o_uTm, o_hi)
    uTm = RB.alloc("uTm", [128, KC, NO], BF16)
    attnT = RB.alloc("attnT", [128, 4, NO], BF16)
    uTh = RB.alloc("uTh", [128, KC, HALO], BF16)
    assert RB.top == o_hi

    def U(kc, t0, n, step=1):
        if t0 < M0:
            assert t0 + (n - 1) * step < M0
            return uTh.t[:, kc, t0: t0 + (n - 1) * step + 1: step]
        a = t0 - M0
        return uTm.t[:, kc, a: a + (n - 1) * step + 1: step]

    def sb(name, shape, dt=F32):
        return RA.alloc(name, shape, dt)

    modT = sb("modT", [128, 48, 17])
    A1 = sb("A1", [128, 8, 17]); A2 = sb("A2", [128, 8, 17])
    G1a = sb("G1a", [128, 8, 17]); G2a = sb("G2a", [128, 8, 17])
    c1b = sb("c1b", [128, 8, 17]); g1t = sb("g1t", [128, 8, 17])
    gA2 = sb("gA2", [128, 8, 17]); bA2 = sb("bA2", [128, 8, 17])
    g2t = sb("g2t", [128, 8, 17]); b2t = sb("b2t", [128, 8, 17])
    flag = sb("flag_s", [128, 1])
    lbT = sb("lbT", [128, 8]); lb1 = sb("lb1", [128, 8]); nlb1 = sb("nlb1", [128, 8])
    hgw = sb("hgw_s", [128, 1])
    ln1g = sb("ln1g_s", [128, 8]); ln1b = sb("ln1b_s", [128, 8])
    ln2g = sb("ln2g_s", [128, 8]); ln2b = sb("ln2b_s", [128, 8])
    bup = sb("bup_s", [128, 32]); bdown = sb("bdown_s", [128, 8])
    ones_bf = sb("ones_bf", [128, 128], BF16)
    ones_m = sb("ones_m", [128, 128])
    ones_r = sb("ones_r", [128, 128])
    ident_bf = sb("ident_bf", [128, 128], BF16)
    hmask = sb("hmask_s", [128, 2, 128])
    reset = sb("reset_s", [128, 2, 512])
    oh = sb("oh_s", [128, 16])
    mg_s = sb("mg_s", [128, KC, 128], BF16)
    ones_f = sb("ones_f", [128, 128])

    class Rings:
        def __init__(self, R, nst, nbf):
            self.st = [R.alloc("wst%d" % i, [128, 4096]) for i in range(nst)]
            self.bf = [R.alloc("wbf%d" % i, [128, 4096], BF16) for i in range(nbf)]
            self.i = 0
            self.j = 0

        def stage(self):
            s_ = self.st[self.i % len(self.st)]
            self.i += 1
            return s_

        def load(self, srcs, castq=None):
            s_ = self.stage()
            d_ = self.bf[self.j % len(self.bf)]
            self.j += 1
            tot = 0
            for (off, a, b, ap) in srcs:
                DMA(s_.t[:, off:off + a * b].rearrange("p (a b) -> p a b", b=b), ap, writes=[s_])
                tot = max(tot, off + a * b)
            q = castq or ("scalar" if (self.j % 2 == 0) else "vector")
            COPY(q, d_.t[:, 0:tot], s_.t[:, 0:tot], [s_], [d_])
            return d_

    def run_stream(items, lookahead):
        loaded = {}
        n = len(items)
        for i in range(min(lookahead, n)):
            loaded[i] = items[i][0]()
        for i in range(n):
            if i + lookahead < n:
                loaded[i + lookahead] = items[i + lookahead][0]()
            items[i][1](loaded.pop(i))
        assert not loaded

    def wsrc(w, c0, ncols, kc=KC, r0=0):
        return w[r0:r0 + kc * 128, c0:c0 + ncols].rearrange("(k p) n -> p k n", p=128)

    def wblocks(w, cols, kc=KC, width=128):
        nb_ = len(cols)
        return [(0, None, None, None)] and [
            (bi, c) for bi, c in enumerate(cols)], nb_ * width

    def load_blocks(rings, w, cols, kc=KC, width=128, castq=None):
        s_ = rings.stage()
        d_ = rings.bf[rings.j % len(rings.bf)]
        rings.j += 1
        row = len(cols) * width
        tot = kc * row
        for bi, c in enumerate(cols):
            DMA(s_.t[:, 0:tot].rearrange("p (a b) -> p a b", b=row)[:, :, bi * width:(bi + 1) * width],
                wsrc(w, c, width, kc=kc), writes=[s_])
        q = castq or ("scalar" if (rings.j % 2 == 0) else "vector")
        COPY(q, d_.t[:, 0:tot], s_.t[:, 0:tot], [s_], [d_])
        return d_

    def proj_fm(wb, row, woff, t0, wd):
        pb = PS()
        MMG([(pb.t[:, 0:wd], wb.t[:, kc * row + woff: kc * row + woff + 128], U(kc, t0, wd), kc == 0, kc == KC - 1)
             for kc in range(KC)], [wb, uTm, uTh], [pb])
        return pb

    def bc(tab, j, lo, n, rep):
        return tab.t[:, j, lo:lo + n].unsqueeze(2).broadcast_to([128, n, rep])

    def v3(ap, rep):
        return ap.rearrange("p (s t) -> p s t", t=rep)

    def affine(q2, out, in_, stab, sj, btab, bj, sample, reads, writes, tmp=None):
        if not sample:
            ACT(out, in_, AF.Identity, list(reads) + [stab, btab], writes, scale=stab.t[:, sj, 0:1], bias=btab.t[:, bj, 0:1])
        else:
            TT("vector", v3(tmp.t[:, 0:128], 8), v3(in_, 8), bc(stab, sj, 1, 16, 8), ALU.mult, list(reads) + [stab], [tmp])
            TT(q2, v3(out, 8), v3(tmp.t[:, 0:128], 8), bc(btab, bj, 1, 16, 8), ALU.add, [tmp, btab], writes)

    MAIN_TILES = [(M0 + i * 512, 512) for i in range(4)]
    HALO_TILES = [(i * 512, 512) for i in range(4)]
    SAMP_TILE = (S0, 128)

    R0 = Region(big, o_hi, ARENA_WORDS)
    for (dst, src) in ((flag, flag_d), (hgw, hgw_d), (ln1g, ln1g_d), (ln1b, ln1b_d), (ln2g, ln2g_d),
                       (ln2b, ln2b_d), (bup, bup_d), (bdown, bdown_d), (oh, oh_d)):
        DMA(dst.t[:], src, writes=[dst])
    DMA(hmask.t[:], hmask_d.rearrange("a p n -> p a n"), writes=[hmask])
    DMA(reset.t[:], reset_d.rearrange("a p n -> p a n"), writes=[reset])
    MEMSET("vector", ones_bf.t[:], 1.0, [ones_bf])
    MEMSET("vector", ones_m.t[:], 1.0 / 1024.0, [ones_m])
    MEMSET("vector", ones_r.t[:], 1.0 / 128.0, [ones_r])
    MEMSET("vector", ones_f.t[:], 1.0, [ones_f])
    identf = R0.alloc("identf", [128, 128])
    DMA(identf.t[:], ident_d, writes=[identf])
    COPY("vector", ident_bf.t[:], identf.t[:], [identf], [ident_bf])

    kvo_bufs = [Buf("kvo%d" % g) for g in range(3)]
    for g, (w, _) in enumerate(GROUPS):
        for b4 in range(0, nsc, 4):
            b5 = min(nsc, b4 + 4)
            DMA(kvo[g][b4:b5, 0:w - 8, :], kvc[g][b4:b5, 8:w, :], writes=[kvo_bufs[g]], q="scalar")

    lbs = R0.alloc("lbs", [128, 2, 8])
    DMA(lbs.t[:], lbp, writes=[lbs])
    TT("vector", lbT.t[:], lbs.t[:, 1, :], lbs.t[:, 0, :], ALU.subtract, [lbs], [lbT])
    ACT(lbT.t[:], lbT.t[:], AF.Exp, [lbT], [lbT])
    TS("vector", lbT.t[:], lbT.t[:], 1.0, None, ALU.add, None, [lbT], [lbT])
    RECIP(lbT.t[:], lbT.t[:], [lbT], [lbT])
    TS("vector", lb1.t[:], lbT.t[:], -1.0, 1.0, ALU.mult, ALU.add, [lbT], [lb1])
    TS("vector", nlb1.t[:], lb1.t[:], -1.0, None, ALU.mult, None, [lb1], [nlb1])

    scT = R0.alloc("scT", [128, KC, 17])
    sct = R0.alloc("sct", [128, KC, 17])
    DMA(scT.t[:], cT, writes=[scT])
    ACT(sct.t[:], scT.t[:], AF.Exp, [scT], [sct], scale=-1.0)
    TS("vector", sct.t[:], sct.t[:], 1.0, None, ALU.add, None, [sct], [sct])
    RECIP(sct.t[:], sct.t[:], [sct], [sct])
    TT("vector", scT.t[:], scT.t[:], sct.t[:], ALU.mult, [scT, sct], [scT])
    bada = R0.alloc("bada", [128, 48])
    DMA(bada.t[:], b_ada, writes=[bada])
    wada = [R0.alloc("wada%d" % i, [128, 4096]) for i in range(2)]
    for ng in range(12):
        s_ = wada[ng % 2]
        DMA(s_.t[:, 0:4096].rearrange("p (a b) -> p a b", b=512), wsrc(w_ada, ng * 512, 512), writes=[s_])
        pb = PS()
        mms = []
        for jj in range(4):
            for kc in range(KC):
                mms.append((pb.t[:, jj * 17:(jj + 1) * 17], s_.t[:, kc * 512 + jj * 128: kc * 512 + (jj + 1) * 128],
                            scT.t[:, kc, :], kc == 0 and jj == 0, kc == KC - 1))
        MMG(mms, [s_, scT], [pb])
        for jj in range(4):
            j = ng * 4 + jj
            TS("vector", modT.t[:, j, :], pb.t[:, jj * 17:(jj + 1) * 17], bada.t[:, j:j + 1], None, ALU.add, None,
               [pb, bada], [modT])
    TS("vector", A1.t[:], modT.t[:, 8:16, :], 1.0, None, ALU.add, None, [modT], [A1])
    TS("vector", A2.t[:], modT.t[:, 32:40, :], 1.0, None, ALU.add, None, [modT], [A2])
    TS("vector", G1a.t[:], modT.t[:, 16:24, :], 1.0 / ALPHA, None, ALU.mult, None, [modT], [G1a])
    TS("vector", G2a.t[:], modT.t[:, 40:48, :], 1.0 / ALPHA, None, ALU.mult, None, [modT], [G2a])

    def bcp(v):
        return v.t[:].unsqueeze(2).broadcast_to([128, 8, 17])
    TS("vector", g1t.t[:], A1.t[:], 0.0, None, ALU.mult, None, [A1], [g1t])
    TT("vector", g1t.t[:], g1t.t[:], bcp(ln1g), ALU.add, [g1t, ln1g], [g1t])
    TT("vector", c1b.t[:], G2a.t[:], bcp(bdown), ALU.mult, [G2a, bdown], [c1b])
    TT("vector", c1b.t[:], c1b.t[:], bcp(ln1b), ALU.add, [c1b, ln1b], [c1b])
    TT("vector", gA2.t[:], A2.t[:], bcp(ln1g), ALU.mult, [A2, ln1g], [gA2])
    TT("vector", bA2.t[:], A2.t[:], bcp(ln1b), ALU.mult, [A2, ln1b], [bA2])
    TT("vector", bA2.t[:], bA2.t[:], modT.t[:, 24:32, :], ALU.add, [bA2, modT], [bA2])
    TS("vector", g2t.t[:], A1.t[:], 0.0, None, ALU.mult, None, [A1], [g2t])
    TT("vector", b2t.t[:], g2t.t[:], bcp(ln2b), ALU.add, [g2t, ln2b], [b2t])
    TT("vector", g2t.t[:], g2t.t[:], bcp(ln2g), ALU.add, [g2t, ln2g], [g2t])

    xst = [R0.alloc("xst%d" % i, [128, NT]) for i in range(2)]
    tmpA = R0.alloc("tmpA", [128, 512])
    for kc in range(KC):
        xs = xst[kc % 2]
        DMA(xs.t[:, 0:2048], xT[kc, :, 0:2048], writes=[xs])
        DMA(xs.t[:, 2048:NT], xT[kc, :, 2048:NT], writes=[xs])
        for t0 in range(0, S0, 1024):
            affine("gpsimd", U(kc, t0, 1024), xs.t[:, t0:t0 + 1024], A1, kc, modT, kc, False, [xs], [uTm, uTh])
        affine("gpsimd", U(kc, S0, 128), xs.t[:, S0:NT], A1, kc, modT, kc, True, [xs], [uTm], tmp=tmpA)
    P.barrier()
    if kstop <= 1:
        return finish()

    R2 = Region(big, o_hi, ARENA_WORDS)
    rings = Rings(R2, 2, 3)
    QT = R2.alloc("QT", [128, NO], BF16)
    KT = R2.alloc("KT", [128, NT], BF16)
    Vb = R2.alloc("Vb", [128, 33, 128], BF16)
    Vs = R2.alloc("Vs", [128, 128], BF16)
    numacc = R2.alloc("numacc", [128, MAIN]); denacc = R2.alloc("denacc", [128, MAIN])
    mk = R2.alloc("mk", [128, 256]); mkh = R2.alloc("mkh", [128, 256])
    smk = R2.alloc("smk", [128, 128]); nmk = R2.alloc("nmk", [128, 128])
    Eb = [R2.alloc("Eb%d" % i, [128, 256], BF16) for i in range(2)]
    Pb = [R2.alloc("Pb%d" % i, [128, 256], BF16) for i in range(2)]
    kst = [R2.alloc("kst%d" % i, [128, 512]) for i in range(2)]
    kst_i = [0]
    sE = [R2.alloc("sE%d" % i, [128, 128]) for i in range(2)]
    sP = [R2.alloc("sP%d" % i, [128, 128], BF16) for i in range(2)]
    sPs = [R2.alloc("sPs%d" % i, [128, 8]) for i in range(2)]
    tkv = kst[0]
    rden = R2.alloc("rden", [128, 512])
    cnt = [0]

    items = []

    def mk_tkv(kv, cbase, g3):
        def loader():
            return rings.load([(0, KC, 512, wsrc(w_in, cbase + g3 * 512, 512))])

        def body(wb):
            pb = PS()
            MMG([(pb.t[:, :], U(kc, S0, 128), wb.t[:, kc * 512:(kc + 1) * 512], kc == 0, kc == KC - 1)
                 for kc in range(KC)], [wb, uTm], [pb])
            COPY("scalar", tkv.t[:], pb.t[:, :], [pb], [tkv])
            w = GROUPS[g3][0]
            for b in range(nsc):
                DMA(kvo[g3][b, w - 8:w, kv * 512:(kv + 1) * 512], tkv.t[b * 8:(b + 1) * 8, :],
                    reads=[tkv], writes=[kvo_bufs[g3]])
        return (loader, body)
    for kv, cbase in ((0, C_K), (1, C_V)):
        for g3 in range(3):
            items.append(mk_tkv(kv, cbase, g3))

    sacc = psacc
    sacc_first = [True]

    def mk_gh(h, g):
        win, dil = GROUPS[g]
        gh = g * 4 + h

        def loader():
            return load_blocks(rings, w_in, [cb + g * 512 + h * 128 for cb in (C_Q, C_K, C_V)])

        def body(wb):
            if g == 0:
                sacc_first[0] = True
            DMA(mk.t[:], masks_d[gh], writes=[mk])
            DMA(smk.t[:], smask_d[gh], writes=[smk])
            DMA(nmk.t[:], nmask_d[gh], writes=[nmk])
            COPY("gpsimd", mkh.t[:, 128:256], mk.t[:, 128:256], [mk], [mkh])
            TS("vector", mkh.t[:, 0:128], mk.t[:, 0:128], flag.t[:, 0:1], None, ALU.mult, None, [mk, flag], [mkh])
            for (t0, wd) in MAIN_TILES + [SAMP_TILE]:
                pb = proj_fm(wb, 384, 0, t0, wd)
                COPY("scalar", QT.t[:, t0 - M0:t0 - M0 + wd], pb.t[:, 0:wd], [pb], [QT])
            ktiles = [(t0, wd) for (t0, wd) in HALO_TILES if t0 + wd > HALO - win] + MAIN_TILES + [SAMP_TILE]
            for (t0, wd) in ktiles:
                pb = proj_fm(wb, 384, 128, t0, wd)
                COPY("scalar", KT.t[:, t0:t0 + wd], pb.t[:, 0:wd], [pb], [KT])
                if M0 <= t0 < S0 and t0 + wd > S0 - win:
                    lo = max(t0, S0 - win)
                    ks = kst[kst_i[0] % 2]; kst_i[0] += 1
                    COPY("vector", ks.t[:, 0:t0 + wd - lo], pb.t[:, lo - t0:wd], [pb], [ks])
                    DMA(kTo[g][h, :, lo - (S0 - win): t0 + wd - (S0 - win)], ks.t[:, 0:t0 + wd - lo], reads=[ks])
            nmb = 1 + MAIN // (128 * dil)
            base = HALO - win
            blocks = [(r, mb) for r in range(dil) for mb in range(nmb)]
            for q0 in range(0, len(blocks), 4):
                grp = blocks[q0:q0 + 4]
                pb = PS()
                mms = []
                for bi, (r, mb) in enumerate(grp):
                    tstart = base + mb * 128 * dil + r
                    for kc in range(KC):
                        mms.append((pb.t[:, bi * 128:(bi + 1) * 128], U(kc, tstart, 128, dil),
                                    wb.t[:, kc * 384 + 256: kc * 384 + 384], kc == 0, kc == KC - 1))
                MMG(mms, [wb, uTm, uTh], [pb])
                COPY("scalar", Vb.t[:, q0:q0 + len(grp), :],
                     pb.t[:, 0:len(grp) * 128].rearrange("p (a b) -> p a b", b=128), [pb], [Vb])
                need = [(bi, r, mb) for bi, (r, mb) in enumerate(grp)
                        if mb >= 1 and (mb * 128 * dil + base) + 127 * dil + r >= S0 - win]
                if need:
                    ks = kst[kst_i[0] % 2]; kst_i[0] += 1
                    COPY("vector", ks.t[:, 0:len(grp) * 128], pb.t[:, 0:len(grp) * 128], [pb], [ks])
                    for (bi, r, mb) in need:
                        ts_ = base + mb * 128 * dil + r - (S0 - win)
                        DMA(vo[g][ts_: ts_ + 127 * dil + 1: dil, h, :], ks.t[:, bi * 128:(bi + 1) * 128], reads=[ks])
            pb = PS()
            MMG([(pb.t[:, 0:128], U(kc, S0, 128), wb.t[:, kc * 384 + 256: kc * 384 + 384], kc == 0, kc == KC - 1)
                 for kc in range(KC)], [wb, uTm], [pb])
            COPY("scalar", Vs.t[:], pb.t[:, 0:128], [pb], [Vs])
            for r in range(dil):
                for mb in range(1, nmb):
                    qstart = (mb - 1) * 128 * dil + r
                    qcols = slice(qstart, qstart + 127 * dil + 1, dil)
                    kprev = base + (mb - 1) * 128 * dil + r
                    kown = base + mb * 128 * dil + r
                    bprev = r * nmb + mb - 1
                    i2 = cnt[0] % 2; cnt[0] += 1
                    pS = PS()
                    MMG([(pS.t[:, 0:128], KT.t[:, kprev: kprev + 127 * dil + 1: dil], QT.t[:, qcols], True, True),
                         (pS.t[:, 128:256], KT.t[:, kown: kown + 127 * dil + 1: dil], QT.t[:, qcols], False, True)],
                        [KT, QT], [pS])
                    ACT(Eb[i2].t[:], pS.t[:, 0:256], AF.Exp, [pS], [Eb[i2]], scale=QSCALE)
                    mm_ = mkh if mb == 1 else mk
                    TT("gpsimd" if i2 else "vector", Pb[i2].t[:], Eb[i2].t[:], mm_.t[:], ALU.mult, [Eb[i2], mm_], [Pb[i2]])
                    pO = PS()
                    MMG([(pO.t[:, 0:128], Vb.t[:, bprev, :], Pb[i2].t[:, 0:128], True, False),
                         (pO.t[:, 0:128], Vb.t[:, bprev + 1, :], Pb[i2].t[:, 128:256], False, True),
                         (pO.t[:, 128:256], ones_bf.t[:], Pb[i2].t[:, 0:128], False, False),
                         (pO.t[:, 128:256], ones_bf.t[:], Pb[i2].t[:, 128:256], False, True)],
                        [Vb, Pb[i2], ones_bf], [pO])
                    if g == 0:
                        COPY("vector", numacc.t[:, qcols], pO.t[:, 0:128], [pO], [numacc])
                        COPY("vector", denacc.t[:, qcols], pO.t[:, 128:256], [pO], [denacc])
                    else:
                        TT("vector", numacc.t[:, qcols], pO.t[:, 0:128], numacc.t[:, qcols], ALU.add, [pO, numacc], [numacc])
                        TT("vector", denacc.t[:, qcols], pO.t[:, 128:256], denacc.t[:, qcols], ALU.add, [pO, denacc], [denacc])
            QS = QT.t[:, MAIN:NO]
            pS = PS()
            MMG([(pS.t[:, 0:128], KT.t[:, S0:NT], QS, True, True)], [KT, QT], [pS])
            i2 = cnt[0] % 2; cnt[0] += 1
            ACT(sE[i2].t[:], pS.t[:, 0:128], AF.Exp, [pS], [sE[i2]], scale=QSCALE)
            TT("gpsimd", sP[i2].t[:], sE[i2].t[:], nmk.t[:], ALU.mult, [sE[i2], nmk], [sP[i2]])
            MMG([(sacc.t[:, 0:128], Vs.t[:], sP[i2].t[:], sacc_first[0], False),
                 (sacc.t[:, 128:256], ones_bf.t[:], sP[i2].t[:], False, False)], [Vs, sP[i2], ones_bf], [sacc])
            sacc_first[0] = False
        return (loader, body)

    def mk_cache(h, g, b):
        win, dil = GROUPS[g]
        nb = win // 128

        def loader():
            return rings.load([
                (0, 1, win, kTc[g][b, h].unsqueeze(1)),
                (2048, nb, 128, kvc[g][b, :, 512 + h * 128: 512 + (h + 1) * 128].rearrange("(a p) d -> p a d", p=128)),
            ])

        def body(cb_):
            QS = QT.t[:, MAIN:NO]
            i2 = cnt[0] % 2; cnt[0] += 1
            pS = PS()
            MMG([(pS.t[:, blk * 8:(blk + 1) * 8], cb_.t[:, blk * 128:(blk + 1) * 128], QS[:, b * 8:(b + 1) * 8],
                  blk == 0, True) for blk in range(nb)], [cb_, QT], [pS])
            ACT(sE[i2].t[:, 0:nb * 8], pS.t[:, 0:nb * 8], AF.Exp, [pS], [sE[i2]], scale=QSCALE)
            TT("vector", sP[i2].t[:, 0:nb * 8], sE[i2].t[:, 0:nb * 8], smk.t[:, 0:nb * 8], ALU.mult,
               [sE[i2], smk], [sP[i2]])
            if nb > 1:
                P.op("vector", (lambda a, bb: (lambda e: e.tensor_reduce(
                    out=a, in_=bb, axis=mybir.AxisListType.X, op=ALU.add)))(
                    sPs[i2].t[:, 0:8], sP[i2].t[:, 0:nb * 8].rearrange("p (a t) -> p t a", t=8)),
                    [sP[i2]], [sPs[i2]])
                srcs, onesx = sPs[i2], ones_f
            else:
                srcs, onesx = sP[i2], ones_bf
            sden = srcs.t[:, 0:8]
            mms = [(sacc.t[:, b * 8:(b + 1) * 8], cb_.t[:, 2048 + blk * 128: 2048 + (blk + 1) * 128],
                    sP[i2].t[:, blk * 8:(blk + 1) * 8], False, False) for blk in range(nb)]
            mms.append((sacc.t[:, 128 + b * 8:128 + (b + 1) * 8], onesx.t[:], sden, False, False))
            MMG(mms, [cb_, sP[i2], srcs, onesx], [sacc])
        return (loader, body)

    def mk_fin(h):
        def body(_):
            for c0 in range(0, MAIN, 512):
                RECIP(rden.t[:], denacc.t[:, c0:c0 + 512], [denacc], [rden])
                TT("vector", attnT.t[:, h, c0:c0 + 512], numacc.t[:, c0:c0 + 512], rden.t[:], ALU.mult,
                   [numacc, rden], [attnT])
            RECIP(rden.t[:, 0:128], sacc.t[:, 128:256], [sacc], [rden])
            TT("vector", attnT.t[:, h, MAIN:NO], sacc.t[:, 0:128], rden.t[:, 0:128], ALU.mult, [sacc, rden], [attnT])
        return ((lambda: None), body)

    for h in range(4):
        for g in range(3):
            items.append(mk_gh(h, g))
            for b in range(nsc):
                items.append(mk_cache(h, g, b))
        items.append(mk_fin(h))
    run_stream(items, 2)
    P.barrier()
    if kstop <= 2:
        return finish()

    R3 = Region(big, o_hi, ARENA_WORDS)
    hgT = R3.alloc("hgT", [128, 8, NO], BF16)
    o_after_hg = R3.top
    wbf_hg = R3.alloc("wbf_hg", [128, 4096], BF16)
    Kt_ = [R3.alloc("Kt%d" % i, [128, 512], BF16) for i in range(2)]
    Kt2_ = [R3.alloc("Kt2%d" % i, [128, 512], BF16) for i in range(2)]
    Qt_ = [R3.alloc("Qt%d" % i, [128, 512], BF16) for i in range(2)]
    sgT_ = [R3.alloc("sgT%d" % i, [128, 512]) for i in range(2)]
    Vh_ = [R3.alloc("Vh%d" % i, [128, 4, 128], BF16) for i in range(2)]
    Sall = R3.alloc("Sall", [128, 8, 128], BF16)
    Sf = [R3.alloc("Sf%d" % i, [128, 128]) for i in range(2)]
    S0f = R3.alloc("S0f", [128, 16, 128])
    S0b = R3.alloc("S0b", [128, 16, 128], BF16)
    Ktm4_ = [R3.alloc("Ktm4%d" % i, [128, 512], BF16) for i in range(2)]
    Ktmm = [R3.alloc("Ktmm%d" % i, [128, 128], BF16) for i in range(2)]
    hb_base = R3.top
    HB = [[R3.alloc("hB%d_%d" % (i, k), [128, 512]) for k in range(6)] for i in range(2)]
    hg_stage = big[:, hb_base:hb_base + 4096]
    hg_stage_bufs = HB[0] + HB[1][:2]

    def load_hg(hh):
        for bi, cb in enumerate((C_QH, C_FH, C_IH, C_OG)):
            DMA(hg_stage.rearrange("p (a b) -> p a b", b=512)[:, :, bi * 128:(bi + 1) * 128],
                wsrc(w_in, cb + hh * 128, 128), writes=hg_stage_bufs)
        COPY("vector" if hh % 2 else "scalar", wbf_hg.t[:], hg_stage, hg_stage_bufs, [wbf_hg])
        return wbf_hg
    dec_ = [R3.alloc("h_dec%d" % i, [128, 16]) for i in range(2)]
    Am4_ = [R3.alloc("Am4%d" % i, [128, 512], BF16) for i in range(2)]
    osq = R3.alloc("osq", [128, 512]); rstd = R3.alloc("rstd", [128, 512]); otmp = R3.alloc("otmp", [128, 512])
    tcount = [0]
    bcount = [0]

    psB = [psb[4], psb[5], psacc]
    psB_i = [0]

    def PSB():
        b = psB[psB_i[0] % 3]
        psB_i[0] += 1
        return b

    def proj_bank(pb, wb, row, woff, t0, wd):
        MMG([(pb.t[:, 0:wd], wb.t[:, kc * row + woff: kc * row + woff + 128], U(kc, t0, wd), kc == 0, kc == KC - 1)
             for kc in range(KC)], [wb, uTm, uTh], [pb])
        return pb

    def hg_head(hh, wb):
        DMA(S0f.t[:], state_d[:, hh].rearrange("b k v -> k b v"), writes=[S0f])
        COPY("gpsimd", S0b.t[:], S0f.t[:], [S0f], [S0b])
        st_ = {"sidx": 0}
        MEMSET("vector", Sf[0].t[:], 0.0, [Sf[0]])
        tiles = HALO_TILES + MAIN_TILES + [SAMP_TILE]

        def ctx(ti):
            t0, wd = tiles[ti]
            c = dict(t0=t0, wd=wd, halo=t0 < M0, samp=t0 >= S0, par=ti % 2)
            c["C"] = 8 if c["samp"] else 64
            c["nchk"] = wd // c["C"]
            c["nblk"] = wd // 128
            c["o0"] = t0 - M0
            return c

        def stageA(ti):
            c = ctx(ti)
            t0, wd, par, samp, halo = c["t0"], c["wd"], c["par"], c["samp"], c["halo"]
            B1, B2, B3, B4, B5, B6 = HB[par]
            Vh, sgT = Vh_[par], sgT_[par]
            pF = proj_bank(psb[0], wb, 512, 128, t0, wd)
            pV = psb[1]
            mms = []
            for bi in range(c["nblk"]):
                for kc in range(KC):
                    mms.append((pV.t[:, bi * 128:(bi + 1) * 128], U(kc, t0 + bi * 128, 128),
                                wb.t[:, kc * 512 + 256: kc * 512 + 384], kc == 0, kc == KC - 1))
            MMG(mms, [wb, uTm, uTh], [pV])
            ACT(B1.t[:, 0:wd], pF.t[:, 0:wd], AF.Exp, [pF], [B1], scale=-1.0)
            ACT(B2.t[:, 0:wd], B1.t[:, 0:wd], AF.Ln, [B1, lbT], [B2], scale=lbT.t[:, hh:hh + 1], bias=1.0)
            ACT(B3.t[:, 0:wd], B1.t[:, 0:wd], AF.Ln, [B1], [B3], bias=1.0)
            COPY("scalar", Vh.t[:, 0:c["nblk"], :], pV.t[:, 0:wd].rearrange("p (a b) -> p a b", b=128), [pV], [Vh])
            TT("gpsimd", B2.t[:, 0:wd], B2.t[:, 0:wd], B3.t[:, 0:wd], ALU.subtract, [B2, B3], [B2])
            P.op("vector", (lambda o_, d0, d1: (lambda e: e.tensor_tensor_scan(
                out=o_, data0=d0, data1=d1, initial=0.0, op0=ALU.mult, op1=ALU.add)))(
                B4.t[:, 0:wd], reset.t[:, 1 if samp else 0, 0:wd], B2.t[:, 0:wd]), [reset, B2], [B4])
            if not halo:
                pQ = proj_bank(psb[2], wb, 512, 0, t0, wd)
                pG = proj_bank(psb[3], wb, 512, 384, t0, wd)
                ACT(B5.t[:, 0:wd], pQ.t[:, 0:wd], AF.Exp, [pQ], [B5], scale=-1.0)
                ACT(B5.t[:, 0:wd], B5.t[:, 0:wd], AF.Ln, [B5], [B5], bias=1.0)
                ACT(B5.t[:, 0:wd], B5.t[:, 0:wd], AF.Exp, [B5], [B5], scale=-1.0)
                TT("vector", B5.t[:, 0:wd], pQ.t[:, 0:wd], B5.t[:, 0:wd], ALU.mult, [pQ, B5], [B5])
                ACT(B6.t[:, 0:wd], pG.t[:, 0:wd], AF.Exp, [pG], [B6], scale=-1.0)
                ACT(B6.t[:, 0:wd], B6.t[:, 0:wd], AF.Ln, [B6], [B6], bias=1.0)
                ACT(B6.t[:, 0:wd], B6.t[:, 0:wd], AF.Exp, [B6], [B6], scale=-1.0)
                STT(sgT.t[:, 0:wd], pG.t[:, 0:wd], hgw.t[:, 0:1], B6.t[:, 0:wd], ALU.mult, ALU.mult,
                    [pG, hgw, B6], [sgT])

        def stageB(ti):
            c = ctx(ti)
            wd, par, C, nchk, halo = c["wd"], c["par"], c["C"], c["nchk"], c["halo"]
            B1, B2, B3, B4, B5, B6 = HB[par]
            Kt, Kt2, Qt, dec = Kt_[par], Kt2_[par], Qt_[par], dec_[par]
            ACT(B1.t[:, 0:wd], B3.t[:, 0:wd], AF.Exp, [B3], [B1], scale=-1.0)
            ACT(B6.t[:, 0:wd], B4.t[:, 0:wd], AF.Exp, [B4], [B6], scale=-1.0)
            ACT(dec.t[:, 0:nchk], B4.t[:, C - 1:wd:C], AF.Exp, [B4], [dec])
            if not halo:
                ACT(B2.t[:, 0:wd], B4.t[:, 0:wd], AF.Exp, [B4], [B2])
            TS("vector", B3.t[:, 0:wd], B1.t[:, 0:wd], nlb1.t[:, hh:hh + 1], lb1.t[:, hh:hh + 1], ALU.mult, ALU.add,
               [B1, nlb1, lb1], [B3])
            TT("gpsimd", Kt.t[:, 0:wd], B3.t[:, 0:wd], B6.t[:, 0:wd], ALU.mult, [B3, B6], [Kt])
            TT("gpsimd", Kt2.t[:, 0:wd].rearrange("p (c t) -> p c t", t=C), Kt.t[:, 0:wd].rearrange("p (c t) -> p c t", t=C),
               dec.t[:, 0:nchk].unsqueeze(2).broadcast_to([128, nchk, C]), ALU.mult, [Kt, dec], [Kt2])
            if not halo:
                TT("vector", Qt.t[:, 0:wd], B5.t[:, 0:wd], B2.t[:, 0:wd], ALU.mult, [B5, B2], [Qt])

        def stageC(ti):
            c = ctx(ti)
            t0, wd, par, C, samp, halo, nblk, o0 = c["t0"], c["wd"], c["par"], c["C"], c["samp"], c["halo"], c["nblk"], c["o0"]
            Kt, Kt2, Qt, sgT, Vh, dec = Kt_[par], Kt2_[par], Qt_[par], sgT_[par], Vh_[par], dec_[par]
            Ktm4, Am4 = Ktm4_[par], Am4_[par]
            sidx = st_["sidx"]
            for bl in range(nblk):
                TRANSPOSE(pst.t[:, bl * 128:(bl + 1) * 128], Kt2.t[:, bl * 128:(bl + 1) * 128], ident_bf.t[:],
                          [Kt2, ident_bf], [pst])
            COPY("scalar", Ktm4.t[:, 0:wd], pst.t[:, 0:wd], [pst], [Ktm4])
            if not samp:
                pDs = []
                for cc in range(2):
                    pD = PSB()
                    MMG([(pD.t[:, bl * 128:(bl + 1) * 128], Ktm4.t[cc * 64:(cc + 1) * 64, bl * 128:(bl + 1) * 128],
                          Vh.t[cc * 64:(cc + 1) * 64, bl, :], bl == 0, True) for bl in range(nblk)], [Ktm4, Vh], [pD])
                    pDs.append(pD)
                for ch in range(2 * nblk):
                    if not halo:
                        COPY("gpsimd", Sall.t[:, ch, :], Sf[sidx].t[:], [Sf[sidx]], [Sall])
                    bl, cc = ch // 2, ch % 2
                    pD = pDs[cc]
                    STT(Sf[1 - sidx].t[:], Sf[sidx].t[:], dec.t[:, ch:ch + 1], pD.t[:, bl * 128:(bl + 1) * 128], ALU.mult, ALU.add,
                        [Sf[sidx], dec, pD], [Sf[1 - sidx]])
                    sidx = 1 - sidx
            else:
                for q4 in range(4):
                    pD = PSB()
                    for q in range(4):
                        cc = q4 * 4 + q
                        i2 = cc % 2
                        TS("vector" if cc % 2 else "gpsimd", Ktmm[i2].t[:], Ktm4.t[:, 0:128], oh.t[:, cc:cc + 1], None, ALU.mult, None,
                           [Ktm4, oh], [Ktmm[i2]])
                        MMG([(pD.t[:, q * 128:(q + 1) * 128], Ktmm[i2].t[:], Vh.t[:, 0, :], q == 0, True)], [Ktmm[i2], Vh], [pD])
                    for q in range(4):
                        cc = q4 * 4 + q
                        STT(S0f.t[:, cc, :], S0f.t[:, cc, :], dec.t[:, cc:cc + 1], pD.t[:, q * 128:(q + 1) * 128], ALU.mult, ALU.add,
                            [S0f, dec, pD], [S0f])
            if t0 + wd == M0:
                TS("vector", Sf[1 - sidx].t[:], Sf[sidx].t[:], flag.t[:, 0:1], None, ALU.mult, None,
                   [Sf[sidx], flag], [Sf[1 - sidx]])
                sidx = 1 - sidx
            if t0 + wd == S0:
                DMA(s_out[hh], Sf[sidx].t[:], reads=[Sf[sidx]])
            if samp:
                DMA(state_o[:, hh].rearrange("b k v -> k b v"), S0f.t[:], reads=[S0f])
            st_["sidx"] = sidx
            if halo:
                return
            pA = PSB()
            MMG([(pA.t[:, bl * 128:(bl + 1) * 128], Kt.t[:, bl * 128:(bl + 1) * 128], Qt.t[:, bl * 128:(bl + 1) * 128],
                  bl == 0, True) for bl in range(nblk)], [Kt, Qt], [pA])
            TT("vector", Am4.t[:, 0:wd].rearrange("p (a b) -> p a b", b=128), pA.t[:, 0:wd].rearrange("p (a b) -> p a b", b=128),
               hmask.t[:, (1 if samp else 0):(2 if samp else 1), :].broadcast_to([128, nblk, 128]), ALU.mult, [pA, hmask], [Am4])
            pO = PSB()
            mms = []
            for bl in range(nblk):
                c0 = bl * 128
                mms.append((pO.t[:, c0:c0 + 128], Vh.t[:, bl, :], Am4.t[:, c0:c0 + 128], bl == 0, False))
                if samp:
                    for cc in range(16):
                        mms.append((pO.t[:, cc * 8:(cc + 1) * 8], S0b.t[:, cc, :], Qt.t[:, cc * 8:(cc + 1) * 8], False, cc == 15))
                else:
                    for cc in range(2):
                        mms.append((pO.t[:, c0 + cc * 64:c0 + (cc + 1) * 64], Sall.t[:, bl * 2 + cc, :],
                                    Qt.t[:, c0 + cc * 64:c0 + (cc + 1) * 64], False, cc == 1))
            MMG(mms, [Vh, Am4, S0b, Sall, Qt], [pO])
            ACT(osq.t[:, 0:wd], pO.t[:, 0:wd], AF.Square, [pO], [osq])
            pM = PSB()
            MMG([(pM.t[:, 0:wd], ones_r.t[:], osq.t[:, 0:wd], True, True)], [ones_r, osq], [pM])
            ACT(rstd.t[:, 0:wd], pM.t[:, 0:wd], AF.Ln, [pM], [rstd], bias=RMS_EPS)
            ACT(rstd.t[:, 0:wd], rstd.t[:, 0:wd], AF.Exp, [rstd], [rstd], scale=-0.5)
            TT("vector", otmp.t[:, 0:wd], pO.t[:, 0:wd], rstd.t[:, 0:wd], ALU.mult, [pO, rstd], [otmp])
            TT("gpsimd", hgT.t[:, hh, o0:o0 + wd], otmp.t[:, 0:wd], sgT.t[:, 0:wd], ALU.mult, [otmp, sgT], [hgT])

        n_t = len(tiles)
        stageA(0)
        for ti in range(n_t):
            stageB(ti)
            if ti + 1 < n_t:
                stageA(ti + 1)
            stageC(ti)

    run_stream([((lambda hh=hh: load_hg(hh)), (lambda wb, hh=hh: hg_head(hh, wb))) for hh in range(8)], 0)
    P.barrier()
    if kstop <= 3:
        return finish()

    mg_m = Region(big, o_uTh, o_hi).alloc("mg_m", [128, KC, MAIN], BF16)

    def MG(c8, o0, wd):
        return mg_m.t[:, c8, o0:o0 + wd] if o0 < MAIN else mg_s.t[:, c8, 0:wd]

    R4 = Region(big, o_after_hg, ARENA_WORDS)
    rings = Rings(R4, 2, 3)
    ea_ = [R4.alloc("ea%d" % i, [128, 512]) for i in range(2)]
    eb_ = [R4.alloc("eb%d" % i, [128, 512]) for i in range(2)]
    m1_ = [R4.alloc("m1%d" % i, [128, 512]) for i in range(2)]
    p3cnt = [0]

    def p3a_load(j):
        s_ = rings.stage()
        d_ = rings.bf[rings.j % len(rings.bf)]
        rings.j += 1
        for bi, (w_, c_) in enumerate(((w_in, C_GA + j * 128), (w_in, C_GB + j * 128), (w_bb, j * 128))):
            DMA(s_.t[:, 0:3072].rearrange("p (a b) -> p a b", b=384)[:, :, bi * 128:(bi + 1) * 128],
                wsrc(w_, c_, 128), writes=[s_])
        DMA(s_.t[:, 3072:3584].rearrange("p (a b) -> p a b", b=128), wsrc(w_ba, j * 128, 128, kc=4), writes=[s_])
        COPY("scalar" if j % 2 else "vector", d_.t[:, 0:3584], s_.t[:, 0:3584], [s_], [d_])
        return d_

    def p3a_body(j, wb):
        for (t0, wd) in MAIN_TILES + [SAMP_TILE]:
            o0 = t0 - M0
            pa = proj_fm(wb, 384, 0, t0, wd)
            pbb = proj_fm(wb, 384, 128, t0, wd)
            pBA = PS()
            MMG([(pBA.t[:, 0:wd], wb.t[:, 3072 + c4 * 128: 3072 + (c4 + 1) * 128], attnT.t[:, c4, o0:o0 + wd],
                  c4 == 0, c4 == 3) for c4 in range(4)], [wb, attnT], [pBA])
            pBB = PS()
            MMG([(pBB.t[:, 0:wd], wb.t[:, c8 * 384 + 256: c8 * 384 + 384], hgT.t[:, c8, o0:o0 + wd],
                  c8 == 0, c8 == 7) for c8 in range(8)], [wb, hgT], [pBB])
            i2 = p3cnt[0] % 2
            p3cnt[0] += 1
            ea, eb, m1 = ea_[i2], eb_[i2], m1_[i2]
            ACT(ea.t[:, 0:wd], pa.t[:, 0:wd], AF.Exp, [pa], [ea], scale=-1.0)
            ACT(ea.t[:, 0:wd], ea.t[:, 0:wd], AF.Ln, [ea], [ea], bias=1.0)
            ACT(ea.t[:, 0:wd], ea.t[:, 0:wd], AF.Exp, [ea], [ea], scale=-1.0)
            TT("vector", m1.t[:, 0:wd], pBA.t[:, 0:wd], ea.t[:, 0:wd], ALU.mult, [pBA, ea], [m1])
            ACT(eb.t[:, 0:wd], pbb.t[:, 0:wd], AF.Exp, [pbb], [eb], scale=-1.0)
            ACT(eb.t[:, 0:wd], eb.t[:, 0:wd], AF.Ln, [eb], [eb], bias=1.0)
            ACT(eb.t[:, 0:wd], eb.t[:, 0:wd], AF.Exp, [eb], [eb], scale=-1.0)
            TT("vector", eb.t[:, 0:wd], pBB.t[:, 0:wd], eb.t[:, 0:wd], ALU.mult, [pBB, eb], [eb])
            TT("gpsimd", MG(j, o0, wd), m1.t[:, 0:wd], eb.t[:, 0:wd], ALU.add, [m1, eb], [mg_m, mg_s])

    run_stream([((lambda j=j: p3a_load(j)), (lambda wb, j=j: p3a_body(j, wb))) for j in range(KC)], 2)
    P.barrier()
    if kstop <= 4:
        return finish()

    RL = Region(big, o_uTm, o_uTh)
    RH = Region(big, o_hi, ARENA_WORDS)
    hid = RL.alloc("hid", [128, 32, 512], BF16)
    h1 = RL.alloc("h1", [128, KC, 512])
    x1 = RH.alloc("x1", [128, KC, 512])
    u2 = RH.alloc("u2", [128, KC, 512], BF16)
    rings = Rings(RH, 2, 3)
    xres = [RH.alloc("xres%d" % i, [128, 512]) for i in range(2)]
    sq_ = [RH.alloc("sq%d" % i, [128, 512]) for i in range(2)]
    m2_ = RH.alloc("m2", [128, 512]); rs_ = RH.alloc("rs", [128, 512])
    tA = RH.alloc("tA", [128, 512]); tB = RH.alloc("tB", [128, 512]); hs_ = RH.alloc("hs", [128, 512])
    yst = [RH.alloc("yst%d" % i, [128, 512]) for i in range(2)]

    def layer_norm(src, wd, emit_j):
        pM = PS()
        MMG([(pM.t[:, 0:wd], ones_m.t[:], src.t[:, j, 0:wd], j == 0, j == KC - 1) for j in range(KC)], [ones_m, src], [pM])
        pV = PS()
        for j in range(KC):
            s2 = sq_[j % 2]
            ACT(s2.t[:, 0:wd], src.t[:, j, 0:wd], AF.Square, [src], [s2])
            MMG([(pV.t[:, 0:wd], ones_m.t[:], s2.t[:, 0:wd], j == 0, j == KC - 1)], [ones_m, s2], [pV])
        ACT(m2_.t[:, 0:wd], pM.t[:, 0:wd], AF.Square, [pM], [m2_])
        TT("vector", m2_.t[:, 0:wd], pV.t[:, 0:wd], m2_.t[:, 0:wd], ALU.subtract, [pV, m2_], [m2_])
        ACT(rs_.t[:, 0:wd], m2_.t[:, 0:wd], AF.Ln, [m2_], [rs_], bias=LN_EPS)
        ACT(rs_.t[:, 0:wd], rs_.t[:, 0:wd], AF.Exp, [rs_], [rs_], scale=-0.5)
        for j in range(KC):
            TT("vector", tA.t[:, 0:wd], src.t[:, j, 0:wd], pM.t[:, 0:wd], ALU.subtract, [src, pM], [tA])
            TT("gpsimd", tB.t[:, 0:wd], tA.t[:, 0:wd], rs_.t[:, 0:wd], ALU.mult, [tA, rs_], [tB])
            emit_j(j)

    items = []

    def p3b_tile(t0, wd):
        samp = t0 >= S0
        o0 = t0 - M0

        def emit1(j):
            affine("gpsimd", x1.t[:, j, 0:wd], tB.t[:, 0:wd], g1t, j, c1b, j, samp, [tB], [x1], tmp=tA)
            affine("gpsimd", u2.t[:, j, 0:wd], tB.t[:, 0:wd], gA2, j, bA2, j, samp, [tB], [u2], tmp=tA)

        def emit2(j):
            ys = yst[j % 2]
            affine("gpsimd", ys.t[:, 0:wd], tB.t[:, 0:wd], g2t, j, b2t, j, False, [tB], [ys])
            DMA(yT[j, :, o0:o0 + wd], ys.t[:, 0:wd], reads=[ys])

        def wo_body(k, wo_b):
            for j in range(4 * k, 4 * k + 4):
                xr = xres[j % 2]
                DMA(xr.t[:, 0:wd], xT[j, :, t0:t0 + wd], writes=[xr])
                pX = PS()
                MMG([(pX.t[:, 0:wd], wo_b.t[:, c8 * 512 + (j % 4) * 128: c8 * 512 + (j % 4 + 1) * 128], MG(c8, o0, wd),
                      c8 == 0, c8 == 7) for c8 in range(8)], [wo_b, mg_m, mg_s], [pX])
                if not samp:
                    STT(h1.t[:, j, 0:wd], pX.t[:, 0:wd], G1a.t[:, j, 0:1], xr.t[:, 0:wd], ALU.mult, ALU.add,
                        [pX, G1a, xr], [h1])
                else:
                    TT("vector", v3(tA.t[:, 0:wd], 8), v3(pX.t[:, 0:wd], 8), bc(G1a, j, 1, 16, 8), ALU.mult, [pX, G1a], [tA])
                    TT("gpsimd", h1.t[:, j, 0:wd], tA.t[:, 0:wd], xr.t[:, 0:wd], ALU.add, [tA, xr], [h1])
            if k == 1:
                layer_norm(h1, wd, emit1)
        for k in range(2):
            items.append(((lambda k=k: rings.load([(0, KC, 512, wsrc(w_out, k * 512, 512))])),
                          (lambda wb, k=k: wo_body(k, wb))))

        def wu_body(gq, wu):
            for fc in range(4 * gq, 4 * gq + 4):
                pH = PS()
                MMG([(pH.t[:, 0:wd], wu.t[:, kc * 512 + (fc % 4) * 128: kc * 512 + (fc % 4 + 1) * 128], u2.t[:, kc, 0:wd],
                      kc == 0, kc == KC - 1) for kc in range(KC)], [wu, u2], [pH])
                ACT(hs_.t[:, 0:wd], pH.t[:, 0:wd], AF.Identity, [pH, bup], [hs_], bias=bup.t[:, fc:fc + 1])
                STT(hid.t[:, fc, 0:wd], hs_.t[:, 0:wd], 0.0, hs_.t[:, 0:wd], ALU.max, ALU.mult, [hs_], [hid])
        for gq in range(8):
            items.append(((lambda gq=gq: rings.load([(0, KC, 512, wsrc(w_up, gq * 512, 512))])),
                          (lambda wb, gq=gq: wu_body(gq, wb))))

        def wd_body(j, wd_b):
            pF = PS()
            MMG([(pF.t[:, 0:wd], wd_b.t[:, c * 128:(c + 1) * 128], hid.t[:, c, 0:wd], c == 0, c == 31)
                 for c in range(32)], [wd_b, hid], [pF])
            if not samp:
                STT(h1.t[:, j, 0:wd], pF.t[:, 0:wd], G2a.t[:, j, 0:1], x1.t[:, j, 0:wd], ALU.mult, ALU.add,
                    [pF, G2a, x1], [h1])
            else:
                TT("vector", v3(tA.t[:, 0:wd], 8), v3(pF.t[:, 0:wd], 8), bc(G2a, j, 1, 16, 8), ALU.mult, [pF, G2a], [tA])
                TT("gpsimd", h1.t[:, j, 0:wd], tA.t[:, 0:wd], x1.t[:, j, 0:wd], ALU.add, [tA, x1], [h1])
            if j == KC - 1:
                layer_norm(h1, wd, emit2)
        for j in range(KC):
            items.append(((lambda j=j: rings.load([(0, 32, 128, w_down[:, j * 128:(j + 1) * 128].rearrange(
                "(k p) n -> p k n", p=128))])), (lambda wb, j=j: wd_body(j, wb))))

    for (t0, wd) in MAIN_TILES + [SAMP_TILE]:
        p3b_tile(t0, wd)
    run_stream(items, 2)

    return finish()


_NC_CACHE = {}
KSTOP = [99]
DEV = {"ncores": NCORES, "nsc": 16}


def _fm(v, n):
    return np.ascontiguousarray(np.asarray(v, np.float32).reshape(n, 128).T)


def kernel(x_prompt, x_sample, c_prompt, c_sample, cache_kv_w128, cache_kv_w512, cache_kv_w2048,
           state_hgrn, w_ada, b_ada, w_in, lb_param, hg_norm_w, w_branch_a, w_branch_b, w_out,
           ln1_g, ln1_b, w_up, b_up, w_down, b_down, ln2_g, ln2_b):
    f = lambda a: np.asarray(a, np.float32)
    x_prompt, x_sample, c_prompt, c_sample = f(x_prompt), f(x_sample), f(c_prompt), f(c_sample)
    caches = [f(cache_kv_w128), f(cache_kv_w512), f(cache_kv_w2048)]
    state_hgrn = f(state_hgrn)
    tabs = const_tables()
    common = dict(
        w_ada=np.ascontiguousarray(f(w_ada)[0]), b_ada=_fm(f(b_ada)[0], 48), w_in=np.ascontiguousarray(f(w_in)[0]),
        lbp=np.ascontiguousarray(f(lb_param).reshape(2, 8, 128).transpose(2, 0, 1)),
        hgw=np.ascontiguousarray(f(hg_norm_w)[0].reshape(128, 1)),
        w_ba=np.ascontiguousarray(f(w_branch_a)[0]), w_bb=np.ascontiguousarray(f(w_branch_b)[0]),
        w_out=np.ascontiguousarray(f(w_out)[0]),
        ln1g=_fm(f(ln1_g)[0], 8), ln1b=_fm(f(ln1_b)[0], 8), ln2g=_fm(f(ln2_g)[0], 8), ln2b=_fm(f(ln2_b)[0], 8),
        w_up=np.ascontiguousarray(f(w_up)[0]), bup=_fm(f(b_up)[0], 32),
        w_down=np.ascontiguousarray(f(w_down)[0]), bdown=_fm(f(b_down)[0], 8),
        masks=tabs["masks"], smask=tabs["smask"], nmask=tabs["nmask"], hmask=tabs["hmask"],
        reset=tabs["reset"], oh=tabs["oh"], ident=np.eye(128, dtype=np.float32),
    )
    in_maps = []
    for c in range(NCORES):
        b, half = c // 2, c % 2
        T0 = half * MAIN
        halo = x_prompt[b, T0 - HALO:T0] if half == 1 else np.zeros((HALO, D), np.float32)
        toks = np.concatenate([halo, x_prompt[b, T0:T0 + MAIN], x_sample[16 * c:16 * c + 16].reshape(SAMP, D)], axis=0)
        cs = np.concatenate([c_prompt[b:b + 1], c_sample[16 * c:16 * c + 16]], axis=0)
        m = dict(common)
        m["xT"] = np.ascontiguousarray(toks.T.reshape(KC, 128, NT))
        m["cT"] = np.ascontiguousarray(cs.T.reshape(KC, 128, 17).transpose(1, 0, 2))
        m["flag"] = np.full((128, 1), float(half), np.float32)
        for (w, _), cache in zip(GROUPS, caches):
            nsc = DEV["nsc"]
            cc = cache[0, 16 * c:16 * c + nsc]
            m["kv%d" % w] = np.ascontiguousarray(cc.reshape(nsc, w, 1024))
            m["kT%d" % w] = np.ascontiguousarray(cc[:, :, 0].transpose(0, 2, 3, 1))
        m["state"] = np.ascontiguousarray(state_hgrn[0, 16 * c:16 * c + 16])
        in_maps.append(m)
    if "nc" not in _NC_CACHE:
        _NC_CACHE["nc"] = build_nc(KSTOP[0], DEV["nsc"])
    ncr = DEV["ncores"]
    if DEV.get("trace"):
        res = run_bass_kernel_spmd(_NC_CACHE["nc"], in_maps[:ncr], core_ids=list(range(ncr)), trace=True)
        print("DEV exec_time_ns", res.exec_time_ns, flush=True)
    else:
        res = run_bass_kernel_spmd(_NC_CACHE["nc"], in_maps[:ncr], core_ids=list(range(ncr)))
    R = res.results
    nsc = DEV["nsc"]
    B, S = x_prompt.shape[0], x_prompt.shape[1]
    y_prompt = np.empty((B, S, D), np.float32)
    y_sample = np.empty((128, 8, D), np.float32)
    pk = [np.empty((1, B, w, 2, 4, 128), np.float32) for (w, _) in GROUPS]
    php = np.empty((1, B, 8, 128, 128), np.float32)
    sk = [np.empty((1, 128, w, 2, 4, 128), np.float32) for (w, _) in GROUPS]
    shs = np.empty((1, 128, 8, 128, 128), np.float32)
    for c in range(ncr):
        b, half = c // 2, c % 2
        T0 = half * MAIN
        yt = np.asarray(R[c]["yT"]).reshape(D, NO)
        y_prompt[b, T0:T0 + MAIN] = yt[:, :MAIN].T
        y_sample[16 * c:16 * c + 16] = yt[:, MAIN:].T.reshape(16, 8, D)
        for gi, (w, _) in enumerate(GROUPS):
            sk[gi][0, 16 * c:16 * c + nsc] = np.asarray(R[c]["kvo%d" % w]).reshape(nsc, w, 2, 4, 128)
            if half == 1:
                pk[gi][0, b, :, 0] = np.asarray(R[c]["kTo%d" % w]).transpose(2, 0, 1)
                pk[gi][0, b, :, 1] = np.asarray(R[c]["vo%d" % w])
        if half == 1:
            php[0, b] = np.asarray(R[c]["s_out"])
        shs[0, 16 * c:16 * c + 16] = np.asarray(R[c]["state_o"])
    return (y_prompt, y_sample, pk[0], pk[1], pk[2], php, sk[0], sk[1], sk[2], shs)
```

```python
import contextlib
import numpy as np
import concourse.bass as bass
import concourse.mybir as mybir
from concourse.bass_utils import run_bass_kernel_spmd

F32 = mybir.dt.float32
BF16 = mybir.dt.bfloat16
ALU = mybir.AluOpType
AF = mybir.ActivationFunctionType

SAME_ENGINE_SYNC = "raw"


def _need_sync(p, o, d):
    if p.is_dma or p.q != o.q:
        return True
    if p.q == "tensor":
        return False
    if SAME_ENGINE_SYNC is True:
        return True
    if SAME_ENGINE_SYNC == "raw":
        return d in o.raw
    return False
COMPUTE = ("tensor", "vector", "scalar", "gpsimd")

NCORES = 8
D = 1024
KC = 8
HALO = 2048
MAIN = 2048
SAMP = 128
NT = HALO + MAIN + SAMP
M0 = HALO
S0 = HALO + MAIN
NO = MAIN + SAMP
GROUPS = ((128, 1), (512, 4), (2048, 16))
ALPHA = 2.0 ** 0.25
LN_EPS = 1e-5 / (ALPHA * ALPHA)
RMS_EPS = 1e-6
QSCALE = 128.0 ** -0.5
C_Q, C_K, C_V = 0, 1536, 3072
C_QH, C_FH, C_IH, C_OG = 4608, 5632, 6656, 7680
C_GA, C_GB = 8704, 9728


def slope(g, h):
    return 2.0 ** (-8.0 * (g * 4 + h + 1) / 12.0)


class Buf:
    __slots__ = ("name", "t", "last_w", "readers", "excl")

    def __init__(self, name, t=None, excl=False):
        self.name = name
        self.t = t
        self.excl = excl
        self.last_w = None
        self.readers = []

    def __getitem__(self, k):
        return self.t[k]


class Op:
    __slots__ = ("q", "fn", "deps", "raw", "is_dma", "signal", "sigval", "sem", "idx", "prev")

    def __init__(self, q, fn, is_dma):
        self.q = q
        self.fn = fn
        self.deps = set()
        self.raw = set()
        self.is_dma = is_dma
        self.signal = False
        self.sigval = None
        self.sem = None
        self.prev = 0


class Prog:
    def __init__(self, nc):
        self.nc = nc
        self.ops = []

    def op(self, q, fn, reads=(), writes=(), dma=False):
        o = Op(q, fn, dma)
        o.idx = len(self.ops)
        for b in reads:
            if b.last_w is not None:
                o.deps.add(b.last_w)
                o.raw.add(b.last_w)
            if b.excl:
                for r in b.readers:
                    if self.ops[r].q != q:
                        o.deps.add(r)
        for b in writes:
            if b.last_w is not None:
                o.deps.add(b.last_w)
            for r in b.readers:
                o.deps.add(r)
        for b in reads:
            b.readers.append(o.idx)
        for b in writes:
            b.last_w = o.idx
            b.readers = []
        o.deps.discard(o.idx)
        self.ops.append(o)
        return o

    def barrier(self):
        self.ops.append(None)

    def emit(self, final_wait_q="sync"):
        nc = self.nc
        ops = self.ops
        lastq = {}
        for o in ops:
            if o is None:
                for q_, lo in lastq.items():
                    lo.signal = True
                continue
            if not o.is_dma:
                lastq[o.q] = o
        for o in ops:
            if o is None:
                continue
            for d in o.deps:
                p = ops[d]
                if _need_sync(p, o, d):
                    p.signal = True
        for o in ops:
            if o is not None and o.is_dma:
                o.signal = True
        stack = contextlib.ExitStack()
        qsem = {q: stack.enter_context(nc.semaphore("s_" + q)) for q in COMPUTE}
        pool_sizes = {"sync": 16, "gpsimd": 8, "scalar": 8}
        dpool = {q: [stack.enter_context(nc.semaphore("d_%s%d" % (q, i))) for i in range(n)]
                 for q, n in pool_sizes.items()}
        dcount = {q: [0] * n for q, n in pool_sizes.items()}
        dnext = {q: 0 for q in pool_sizes}
        qcount = {q: 0 for q in COMPUTE}
        snaps = {}
        for oi, o in enumerate(ops):
            if o is None:
                snaps[oi] = (dict(qcount), {q: list(v) for q, v in dcount.items()})
                continue
            if not o.signal:
                continue
            if o.is_dma:
                i = dnext[o.q]
                dnext[o.q] = (i + 1) % pool_sizes[o.q]
                o.sem = dpool[o.q][i]
                o.prev = dcount[o.q][i]
                dcount[o.q][i] += 16
                o.sigval = dcount[o.q][i]
            else:
                o.sem = qsem[o.q]
                qcount[o.q] += 1
                o.sigval = qcount[o.q]
        with nc.Block() as block:
            def make(qname):
                def body(eng):
                    known = {}
                    for oi, o in enumerate(ops):
                        if o is None:
                            qc, dc = snaps[oi]
                            for q2 in COMPUTE:
                                if qc[q2] > known.get(id(qsem[q2]), 0):
                                    eng.wait_ge(qsem[q2], qc[q2])
                                    known[id(qsem[q2])] = qc[q2]
                            for q2, n in pool_sizes.items():
                                for i in range(n):
                                    if dc[q2][i] > known.get(id(dpool[q2][i]), 0):
                                        eng.wait_ge(dpool[q2][i], dc[q2][i])
                                        known[id(dpool[q2][i])] = dc[q2][i]
                            continue
                        if o.q != qname:
                            continue
                        for d in sorted(o.deps):
                            p = ops[d]
                            if not p.signal:
                                continue
                            if not _need_sync(p, o, d):
                                continue
                            key = id(p.sem)
                            if known.get(key, 0) >= p.sigval:
                                continue
                            eng.wait_ge(p.sem, p.sigval)
                            known[key] = p.sigval
                        if o.is_dma and o.prev > 0 and known.get(id(o.sem), 0) < o.prev:
                            eng.wait_ge(o.sem, o.prev)
                            known[id(o.sem)] = o.prev
                        inst = o.fn(eng)
                        if o.signal:
                            inst.then_inc(o.sem, 16 if o.is_dma else 1)
                    if qname == final_wait_q:
                        for q2, n in pool_sizes.items():
                            for i in range(n):
                                if dcount[q2][i] > 0:
                                    eng.wait_ge(dpool[q2][i], dcount[q2][i])
                        for q in COMPUTE:
                            if qcount[q] > 0:
                                eng.wait_ge(qsem[q], qcount[q])
                return body
            block.sync(make("sync"))
            block.scalar(make("scalar"))
            block.vector(make("vector"))
            block.gpsimd(make("gpsimd"))
            block.tensor(make("tensor"))
        stack.close()


def const_tables():
    j = np.arange(128)[:, None].astype(np.float64)
    i = np.arange(128)[None, :].astype(np.float64)
    masks = np.zeros((12, 128, 256), np.float32)
    smask = np.zeros((12, 128, 128), np.float32)
    nmask = np.zeros((12, 128, 128), np.float32)
    for g, (win, dil) in enumerate(GROUPS):
        for h in range(4):
            sl = slope(g, h)
            prev = np.where(i <= j, np.exp(-sl * dil * (128 + i - j)), 0.0)
            own = np.where(i >= j, np.exp(-sl * dil * (i - j)), 0.0)
            masks[g * 4 + h, :, :128] = prev
            masks[g * 4 + h, :, 128:] = own
            nb = win // 128
            p = np.arange(128)[:, None]
            for blk in range(nb):
                t = np.arange(8)[None, :]
                idx = blk * 128 + p
                diff = win + t - idx
                ok = (diff % dil == 0) & (diff >= 0) & (diff <= win)
                smask[g * 4 + h, :, blk * 8:(blk + 1) * 8] = np.where(ok, np.exp(-sl * diff), 0.0)
            kp = np.arange(128)[:, None]
            qp = np.arange(128)[None, :]
            dj = (qp % 8) - (kp % 8)
            ok = (kp // 8 == qp // 8) & (dj >= 0) & (dj % dil == 0)
            nmask[g * 4 + h] = np.where(ok, np.exp(-sl * dj), 0.0)
    s = np.arange(128)[:, None]
    t = np.arange(128)[None, :]
    hmask = np.zeros((2, 128, 128), np.float32)
    hmask[0] = ((s // 64 == t // 64) & (s <= t)).astype(np.float32)
    hmask[1] = ((s // 8 == t // 8) & (s <= t)).astype(np.float32)
    reset = np.ones((2, 128, 512), np.float32)
    reset[0, :, ::64] = 0.0
    reset[1, :, ::8] = 0.0
    oh = (np.arange(128)[:, None] // 8 == np.arange(16)[None, :]).astype(np.float32)
    return dict(masks=masks, smask=smask, nmask=nmask, hmask=hmask, reset=reset, oh=oh)


ARENA_WORDS = 53100


class Region:
    def __init__(self, big, lo, hi):
        self.big, self.lo, self.hi, self.top = big, lo, hi, lo

    def alloc(self, name, shape, dt=F32):
        assert shape[0] == 128
        n = 1
        for s_ in shape[1:]:
            n *= s_
        words = n if dt == F32 else (n + 1) // 2
        words = (words + 7) // 8 * 8
        assert self.top + words <= self.hi, (name, self.top, words, self.hi)
        ap = self.big[:, self.top:self.top + words]
        if dt != F32:
            ap = ap.bitcast(dt)
        ap = ap[:, 0:n]
        if len(shape) == 3:
            ap = ap.rearrange("p (a b) -> p a b", b=shape[2])
        self.top += words
        return Buf(name, ap)


def build_nc(kstop=99, nsc=16):
    nc = bass.Bass("TRN2", target_bir_lowering=False)
    P = Prog(nc)
    st = contextlib.ExitStack()

    def finish():
        P.emit()
        st.close()
        return nc

    def din(name, shape):
        return nc.dram_tensor(name, list(shape), F32, kind="ExternalInput").ap()

    def dout(name, shape):
        return nc.dram_tensor(name, list(shape), F32, kind="ExternalOutput").ap()

    xT = din("xT", [KC, 128, NT])
    cT = din("cT", [128, KC, 17])
    flag_d = din("flag", [128, 1])
    w_ada = din("w_ada", [D, 6144])
    b_ada = din("b_ada", [128, 48])
    w_in = din("w_in", [D, 10752])
    lbp = din("lbp", [128, 2, 8])
    hgw_d = din("hgw", [128, 1])
    w_ba = din("w_ba", [512, D])
    w_bb = din("w_bb", [D, D])
    w_out = din("w_out", [D, D])
    ln1g_d = din("ln1g", [128, 8]); ln1b_d = din("ln1b", [128, 8])
    ln2g_d = din("ln2g", [128, 8]); ln2b_d = din("ln2b", [128, 8])
    w_up = din("w_up", [D, 4096]); bup_d = din("bup", [128, 32])
    w_down = din("w_down", [4096, D]); bdown_d = din("bdown", [128, 8])
    masks_d = din("masks", [12, 128, 256])
    smask_d = din("smask", [12, 128, 128])
    nmask_d = din("nmask", [12, 128, 128])
    hmask_d = din("hmask", [2, 128, 128])
    reset_d = din("reset", [2, 128, 512])
    oh_d = din("oh", [128, 16])
    ident_d = din("ident", [128, 128])
    kvc = [din("kv%d" % w, [nsc, w, 1024]) for (w, _) in GROUPS]
    kTc = [din("kT%d" % w, [nsc, 4, 128, w]) for (w, _) in GROUPS]
    state_d = din("state", [16, 8, 128, 128])

    yT = dout("yT", [KC, 128, NO])
    kTo = [dout("kTo%d" % w, [4, 128, w]) for (w, _) in GROUPS]
    vo = [dout("vo%d" % w, [w, 4, 128]) for (w, _) in GROUPS]
    s_out = dout("s_out", [8, 128, 128])
    kvo = [dout("kvo%d" % w, [nsc, w, 1024]) for (w, _) in GROUPS]
    state_o = dout("state_o", [16, 8, 128, 128])

    big = st.enter_context(nc.sbuf_tensor("big", [128, ARENA_WORDS], F32))

    def psum(name, shape, dt=F32):
        return Buf(name, st.enter_context(nc.psum_tensor(name, list(shape), dt)), excl=True)

    def DMA(out, in_, reads=(), writes=(), q="sync"):
        P.op(q, lambda e: e.dma_start(out=out, in_=in_), reads, writes, dma=True)

    def ACT(out, in_, func, reads, writes, scale=None, bias=None):
        kw = {}
        if scale is not None:
            kw["scale"] = scale
        if bias is not None:
            kw["bias"] = bias
        P.op("scalar", lambda e: e.activation(out=out, in_=in_, func=func, **kw), reads, writes)

    def TT(q, out, in0, in1, op, reads, writes):
        P.op(q, lambda e: e.tensor_tensor(out=out, in0=in0, in1=in1, op=op), reads, writes)

    def TS(q, out, in0, s1, s2, op0, op1, reads, writes):
        if op1 is None:
            P.op(q, lambda e: e.tensor_scalar(out=out, in0=in0, scalar1=s1, scalar2=None, op0=op0), reads, writes)
        else:
            P.op(q, lambda e: e.tensor_scalar(out=out, in0=in0, scalar1=s1, scalar2=s2, op0=op0, op1=op1), reads, writes)

    def STT(out, in0, scalar, in1, op0, op1, reads, writes):
        P.op("vector", lambda e: e.scalar_tensor_tensor(out=out, in0=in0, scalar=scalar, in1=in1, op0=op0, op1=op1),
             reads, writes)

    def COPY(q, out, in_, reads, writes):
        if q == "scalar":
            P.op(q, lambda e: e.activation(out=out, in_=in_, func=AF.Copy), reads, writes)
        else:
            P.op(q, lambda e: e.tensor_copy(out=out, in_=in_), reads, writes)

    def RECIP(out, in_, reads, writes):
        P.op("vector", lambda e: e.reciprocal(out=out, in_=in_), reads, writes)

    def MEMSET(q, out, val, writes):
        P.op(q, lambda e: e.memset(out, val), (), writes)

    def MMG(mms, reads, writes):
        mms = list(mms)

        def fn(e):
            inst = None
            for (o, l, r, s_, p_) in mms:
                inst = e.matmul(o, lhsT=l, rhs=r, start=s_, stop=p_, skip_group_check=True)
            return inst
        P.op("tensor", fn, reads, writes)

    def TRANSPOSE(out, in_, ident, reads, writes):
        P.op("tensor", lambda e: e.transpose(out, in_, ident), reads, writes)

    psb = [psum("ps%d" % i, [128, 512]) for i in range(6)]
    psacc = psum("psacc", [128, 512])
    pst = psum("pst", [128, 1024], BF16)
    ps_i = [0]
    ps_n = [6]

    def PS():
        b = psb[ps_i[0] % ps_n[0]]
        ps_i[0] += 1
        return b

    RA = Region(big, 0, 4608)
    o_uTm = 4608
    o_attn = o_uTm + 8704
    o_uTh = o_attn + 4352
    o_hi = o_uTh + 8192
    RB = Region(big, o_uTm, o_hi)
    uTm = RB.alloc("uTm", [128, KC, NO], BF16)
    attnT = RB.alloc("attnT", [128, 4, NO], BF16)
    uTh = RB.alloc("uTh", [128, KC, HALO], BF16)
    assert RB.top == o_hi

    def U(kc, t0, n, step=1):
        if t0 < M0:
            assert t0 + (n - 1) * step < M0
            return uTh.t[:, kc, t0: t0 + (n - 1) * step + 1: step]
        a = t0 - M0
        return uTm.t[:, kc, a: a + (n - 1) * step + 1: step]

    def sb(name, shape, dt=F32):
        return RA.alloc(name, shape, dt)

    modT = sb("modT", [128, 48, 17])
    A1 = sb("A1", [128, 8, 17]); A2 = sb("A2", [128, 8, 17])
    G1a = sb("G1a", [128, 8, 17]); G2a = sb("G2a", [128, 8, 17])
    c1b = sb("c1b", [128, 8, 17]); g1t = sb("g1t", [128, 8, 17])
    gA2 = sb("gA2", [128, 8, 17]); bA2 = sb("bA2", [128, 8, 17])
    g2t = sb("g2t", [128, 8, 17]); b2t = sb("b2t", [128, 8, 17])
    flag = sb("flag_s", [128, 1])
    lbT = sb("lbT", [128, 8]); lb1 = sb("lb1", [128, 8]); nlb1 = sb("nlb1", [128, 8])
    hgw = sb("hgw_s", [128, 1])
    ln1g = sb("ln1g_s", [128, 8]); ln1b = sb("ln1b_s", [128, 8])
    ln2g = sb("ln2g_s", [128, 8]); ln2b = sb("ln2b_s", [128, 8])
    bup = sb("bup_s", [128, 32]); bdown = sb("bdown_s", [128, 8])
    ones_bf = sb("ones_bf", [128, 128], BF16)
    ones_m = sb("ones_m", [128, 128])
    ones_r = sb("ones_r", [128, 128])
    ident_bf = sb("ident_bf", [128, 128], BF16)
    hmask = sb("hmask_s", [128, 2, 128])
    reset = sb("reset_s", [128, 2, 512])
    oh = sb("oh_s", [128, 16])
    mg_s = sb("mg_s", [128, KC, 128], BF16)
    ones_f = sb("ones_f", [128, 128])

    class Rings:
        def __init__(self, R, nst, nbf):
            self.st = [R.alloc("wst%d" % i, [128, 4096]) for i in range(nst)]
            self.bf = [R.alloc("wbf%d" % i, [128, 4096], BF16) for i in range(nbf)]
            self.i = 0
            self.j = 0

        def stage(self):
            s_ = self.st[self.i % len(self.st)]
            self.i += 1
            return s_

        def load(self, srcs, castq=None):
            s_ = self.stage()
            d_ = self.bf[self.j % len(self.bf)]
            self.j += 1
            tot = 0
            for src in srcs:
                (off, a, b, ap) = src[:4]
                DMA(s_.t[:, off:off + a * b].rearrange("p (a b) -> p a b", b=b), ap, writes=[s_],
                    q=(src[4] if len(src) > 4 else "sync"))
                tot = max(tot, off + a * b)
            q = castq or ("scalar" if (self.j % 2 == 0) else "vector")
            COPY(q, d_.t[:, 0:tot], s_.t[:, 0:tot], [s_], [d_])
            return d_

    def run_stream(items, lookahead):
        loaded = {}
        n = len(items)
        for i in range(min(lookahead, n)):
            loaded[i] = items[i][0]()
        for i in range(n):
            if i + lookahead < n:
                loaded[i + lookahead] = items[i + lookahead][0]()
            items[i][1](loaded.pop(i))
        assert not loaded

    def wsrc(w, c0, ncols, kc=KC, r0=0):
        return w[r0:r0 + kc * 128, c0:c0 + ncols].rearrange("(k p) n -> p k n", p=128)

    def wblocks(w, cols, kc=KC, width=128):
        nb_ = len(cols)
        return [(0, None, None, None)] and [
            (bi, c) for bi, c in enumerate(cols)], nb_ * width

    def load_blocks(rings, w, cols, kc=KC, width=128, castq=None):
        s_ = rings.stage()
        d_ = rings.bf[rings.j % len(rings.bf)]
        rings.j += 1
        row = len(cols) * width
        tot = kc * row
        for bi, c in enumerate(cols):
            DMA(s_.t[:, 0:tot].rearrange("p (a b) -> p a b", b=row)[:, :, bi * width:(bi + 1) * width],
                wsrc(w, c, width, kc=kc), writes=[s_])
        q = castq or ("scalar" if (rings.j % 2 == 0) else "vector")
        COPY(q, d_.t[:, 0:tot], s_.t[:, 0:tot], [s_], [d_])
        return d_

    def proj_fm(wb, row, woff, t0, wd):
        pb = PS()
        MMG([(pb.t[:, 0:wd], wb.t[:, kc * row + woff: kc * row + woff + 128], U(kc, t0, wd), kc == 0, kc == KC - 1)
             for kc in range(KC)], [wb, uTm, uTh], [pb])
        return pb

    def bc(tab, j, lo, n, rep):
        return tab.t[:, j, lo:lo + n].unsqueeze(2).broadcast_to([128, n, rep])

    def v3(ap, rep):
        return ap.rearrange("p (s t) -> p s t", t=rep)

    def affine(q2, out, in_, stab, sj, btab, bj, sample, reads, writes, tmp=None):
        if not sample:
            ACT(out, in_, AF.Identity, list(reads) + [stab, btab], writes, scale=stab.t[:, sj, 0:1], bias=btab.t[:, bj, 0:1])
        else:
            TT("vector", v3(tmp.t[:, 0:128], 8), v3(in_, 8), bc(stab, sj, 1, 16, 8), ALU.mult, list(reads) + [stab], [tmp])
            TT(q2, v3(out, 8), v3(tmp.t[:, 0:128], 8), bc(btab, bj, 1, 16, 8), ALU.add, [tmp, btab], writes)

    MAIN_TILES = [(M0 + i * 512, 512) for i in range(4)]
    HALO_TILES = [(i * 512, 512) for i in range(4)]
    SAMP_TILE = (S0, 128)

    R0 = Region(big, o_hi, ARENA_WORDS)
    for (dst, src) in ((flag, flag_d), (hgw, hgw_d), (ln1g, ln1g_d), (ln1b, ln1b_d), (ln2g, ln2g_d),
                       (ln2b, ln2b_d), (bup, bup_d), (bdown, bdown_d), (oh, oh_d)):
        DMA(dst.t[:], src, writes=[dst])
    DMA(hmask.t[:], hmask_d.rearrange("a p n -> p a n"), writes=[hmask])
    DMA(reset.t[:], reset_d.rearrange("a p n -> p a n"), writes=[reset])
    MEMSET("vector", ones_bf.t[:], 1.0, [ones_bf])
    MEMSET("vector", ones_m.t[:], 1.0 / 1024.0, [ones_m])
    MEMSET("vector", ones_r.t[:], 1.0 / 128.0, [ones_r])
    MEMSET("vector", ones_f.t[:], 1.0, [ones_f])
    identf = R0.alloc("identf", [128, 128])
    DMA(identf.t[:], ident_d, writes=[identf])
    COPY("vector", ident_bf.t[:], identf.t[:], [identf], [ident_bf])

    kvo_bufs = [Buf("kvo%d" % g) for g in range(3)]
    copy_jobs = []
    for b in range(nsc):
        for g, (w, _) in enumerate(GROUPS):
            copy_jobs.append((g, b, w))

    def issue_copies(n):
        for _ in range(n):
            if copy_jobs:
                g, b, w = copy_jobs.pop(0)
                DMA(kvo[g][b, 0:w - 8, :], kvc[g][b, 8:w, :], q="sync")

    lbs = R0.alloc("lbs", [128, 2, 8])
    DMA(lbs.t[:], lbp, writes=[lbs])
    TT("vector", lbT.t[:], lbs.t[:, 1, :], lbs.t[:, 0, :], ALU.subtract, [lbs], [lbT])
    ACT(lbT.t[:], lbT.t[:], AF.Exp, [lbT], [lbT])
    TS("vector", lbT.t[:], lbT.t[:], 1.0, None, ALU.add, None, [lbT], [lbT])
    RECIP(lbT.t[:], lbT.t[:], [lbT], [lbT])
    TS("vector", lb1.t[:], lbT.t[:], -1.0, 1.0, ALU.mult, ALU.add, [lbT], [lb1])
    TS("vector", nlb1.t[:], lb1.t[:], -1.0, None, ALU.mult, None, [lb1], [nlb1])

    scT = R0.alloc("scT", [128, KC, 17])
    sct = R0.alloc("sct", [128, KC, 17])
    DMA(scT.t[:], cT, writes=[scT])
    ACT(sct.t[:], scT.t[:], AF.Exp, [scT], [sct], scale=-1.0)
    TS("vector", sct.t[:], sct.t[:], 1.0, None, ALU.add, None, [sct], [sct])
    RECIP(sct.t[:], sct.t[:], [sct], [sct])
    TT("vector", scT.t[:], scT.t[:], sct.t[:], ALU.mult, [scT, sct], [scT])
    bada = R0.alloc("bada", [128, 48])
    DMA(bada.t[:], b_ada, writes=[bada])
    wada = [R0.alloc("wada%d" % i, [128, 4096]) for i in range(2)]
    for ng in range(12):
        s_ = wada[ng % 2]
        DMA(s_.t[:, 0:4096].rearrange("p (a b) -> p a b", b=512), wsrc(w_ada, ng * 512, 512), writes=[s_])
        pb = PS()
        mms = []
        for jj in range(4):
            for kc in range(KC):
                mms.append((pb.t[:, jj * 17:(jj + 1) * 17], s_.t[:, kc * 512 + jj * 128: kc * 512 + (jj + 1) * 128],
                            scT.t[:, kc, :], kc == 0 and jj == 0, kc == KC - 1))
        MMG(mms, [s_, scT], [pb])
        for jj in range(4):
            j = ng * 4 + jj
            TS("vector", modT.t[:, j, :], pb.t[:, jj * 17:(jj + 1) * 17], bada.t[:, j:j + 1], None, ALU.add, None,
               [pb, bada], [modT])
    TS("vector", A1.t[:], modT.t[:, 8:16, :], 1.0, None, ALU.add, None, [modT], [A1])
    TS("vector", A2.t[:], modT.t[:, 32:40, :], 1.0, None, ALU.add, None, [modT], [A2])
    TS("vector", G1a.t[:], modT.t[:, 16:24, :], 1.0 / ALPHA, None, ALU.mult, None, [modT], [G1a])
    TS("vector", G2a.t[:], modT.t[:, 40:48, :], 1.0 / ALPHA, None, ALU.mult, None, [modT], [G2a])

    def bcp(v):
        return v.t[:].unsqueeze(2).broadcast_to([128, 8, 17])
    TS("vector", g1t.t[:], A1.t[:], 0.0, None, ALU.mult, None, [A1], [g1t])
    TT("vector", g1t.t[:], g1t.t[:], bcp(ln1g), ALU.add, [g1t, ln1g], [g1t])
    TT("vector", c1b.t[:], G2a.t[:], bcp(bdown), ALU.mult, [G2a, bdown], [c1b])
    TT("vector", c1b.t[:], c1b.t[:], bcp(ln1b), ALU.add, [c1b, ln1b], [c1b])
    TT("vector", gA2.t[:], A2.t[:], bcp(ln1g), ALU.mult, [A2, ln1g], [gA2])
    TT("vector", bA2.t[:], A2.t[:], bcp(ln1b), ALU.mult, [A2, ln1b], [bA2])
    TT("vector", bA2.t[:], bA2.t[:], modT.t[:, 24:32, :], ALU.add, [bA2, modT], [bA2])
    TS("vector", g2t.t[:], A1.t[:], 0.0, None, ALU.mult, None, [A1], [g2t])
    TT("vector", b2t.t[:], g2t.t[:], bcp(ln2b), ALU.add, [g2t, ln2b], [b2t])
    TT("vector", g2t.t[:], g2t.t[:], bcp(ln2g), ALU.add, [g2t, ln2g], [g2t])

    xst = [R0.alloc("xst%d" % i, [128, NT]) for i in range(2)]
    tmpA = R0.alloc("tmpA", [128, 512])
    for kc in range(KC):
        xs = xst[kc % 2]
        DMA(xs.t[:, 0:2048], xT[kc, :, 0:2048], writes=[xs])
        DMA(xs.t[:, 2048:NT], xT[kc, :, 2048:NT], writes=[xs])
        for t0 in range(0, S0, 1024):
            affine("gpsimd", U(kc, t0, 1024), xs.t[:, t0:t0 + 1024], A1, kc, modT, kc, False, [xs], [uTm, uTh])
        affine("gpsimd", U(kc, S0, 128), xs.t[:, S0:NT], A1, kc, modT, kc, True, [xs], [uTm], tmp=tmpA)
    P.barrier()
    if kstop <= 1:
        return finish()

    R2 = Region(big, o_hi, ARENA_WORDS)
    rings = Rings(R2, 2, 1)
    rings_c = Rings(R2, 0, 2)
    rings_c.stage = rings.stage
    QT = R2.alloc("QT", [128, NO], BF16)
    QSs = [R2.alloc("QSs%d" % i, [128, 128], BF16) for i in range(2)]
    KT = R2.alloc("KT", [128, NT], BF16)
    Vb = R2.alloc("Vb", [128, 33, 128], BF16)
    Vs = R2.alloc("Vs", [128, 128], BF16)
    numacc = R2.alloc("numacc", [128, MAIN]); denacc = R2.alloc("denacc", [128, MAIN])
    mk = R2.alloc("mk", [128, 256]); mkh = R2.alloc("mkh", [128, 256])
    smk_ = [R2.alloc("smk%d" % i, [128, 128]) for i in range(2)]; nmk = R2.alloc("nmk", [128, 128])
    Eb = [R2.alloc("Eb%d" % i, [128, 256], BF16) for i in range(2)]
    Pb = [R2.alloc("Pb%d" % i, [128, 256], BF16) for i in range(2)]
    kst = [R2.alloc("kst%d" % i, [128, 512]) for i in range(2)]
    kst_i = [0]
    sE = [R2.alloc("sE%d" % i, [128, 128]) for i in range(2)]
    sP = [R2.alloc("sP%d" % i, [128, 128], BF16) for i in range(2)]
    sPs = [R2.alloc("sPs%d" % i, [128, 8]) for i in range(2)]
    tkv = kst[0]
    rden = kst[1]
    cnt = [0]
    import collections
    pending = collections.deque()
    inflight = []

    def pump(drain=False):
        while True:
            while len(inflight) < 2 and pending:
                ld, bd = pending.popleft()
                inflight.append((bd, ld()))
            if not inflight:
                return
            bd, buf = inflight.pop(0)
            bd(buf)
            if not drain:
                return
    saccs = [psacc, psb[5]]

    items = []
    ps_n[0] = 5

    def mk_tkv(kv, cbase, g3):
        def loader():
            return rings.load([(0, KC, 512, wsrc(w_in, cbase + g3 * 512, 512))])

        def body(wb):
            pb = PS()
            MMG([(pb.t[:, :], U(kc, S0, 128), wb.t[:, kc * 512:(kc + 1) * 512], kc == 0, kc == KC - 1)
                 for kc in range(KC)], [wb, uTm], [pb])
            COPY("scalar", tkv.t[:], pb.t[:, :], [pb], [tkv])
            w = GROUPS[g3][0]
            for b in range(nsc):
                DMA(kvo[g3][b, w - 8:w, kv * 512:(kv + 1) * 512], tkv.t[b * 8:(b + 1) * 8, :],
                    reads=[tkv])
        return (loader, body)
    for kv, cbase in ((0, C_K), (1, C_V)):
        for g3 in range(3):
            items.append(mk_tkv(kv, cbase, g3))

    sacc_first = [True]

    def mk_gh(h, g):
        win, dil = GROUPS[g]
        gh = g * 4 + h
        sacc = saccs[h % 2]
        par = (h * 3 + g) % 2
        smk = smk_[par]
        pcount = [0]

        def pstep():
            pcount[0] += 1
            if pcount[0] % 3 == 0:
                pump()

        def loader():
            return load_blocks(rings, w_in, [cb + g * 512 + h * 128 for cb in (C_Q, C_K, C_V)])

        def body(wb):
            if g == 0:
                sacc_first[0] = True
            DMA(mk.t[:], masks_d[gh], writes=[mk])
            DMA(smk.t[:], smask_d[gh], writes=[smk])
            DMA(nmk.t[:], nmask_d[gh], writes=[nmk])
            COPY("gpsimd", mkh.t[:, 128:256], mk.t[:, 128:256], [mk], [mkh])
            TS("vector", mkh.t[:, 0:128], mk.t[:, 0:128], flag.t[:, 0:1], None, ALU.mult, None, [mk, flag], [mkh])
            for (t0, wd) in MAIN_TILES + [SAMP_TILE]:
                pb = proj_fm(wb, 384, 0, t0, wd)
                COPY("scalar", QT.t[:, t0 - M0:t0 - M0 + wd], pb.t[:, 0:wd], [pb], [QT])
                pstep()
            ktiles = [(t0, wd) for (t0, wd) in HALO_TILES if t0 + wd > HALO - win] + MAIN_TILES + [SAMP_TILE]
            for (t0, wd) in ktiles:
                pb = proj_fm(wb, 384, 128, t0, wd)
                COPY("scalar", KT.t[:, t0:t0 + wd], pb.t[:, 0:wd], [pb], [KT])
                if M0 <= t0 < S0 and t0 + wd > S0 - win:
                    lo = max(t0, S0 - win)
                    ks = kst[kst_i[0] % 2]; kst_i[0] += 1
                    COPY("vector", ks.t[:, 0:t0 + wd - lo], pb.t[:, lo - t0:wd], [pb], [ks])
                    DMA(kTo[g][h, :, lo - (S0 - win): t0 + wd - (S0 - win)], ks.t[:, 0:t0 + wd - lo], reads=[ks])
                pstep()
            nmb = 1 + MAIN // (128 * dil)
            base = HALO - win
            blocks = [(r, mb) for r in range(dil) for mb in range(nmb)]
            for q0 in range(0, len(blocks), 4):
                grp = blocks[q0:q0 + 4]
                pb = PS()
                mms = []
                for bi, (r, mb) in enumerate(grp):
                    tstart = base + mb * 128 * dil + r
                    for kc in range(KC):
                        mms.append((pb.t[:, bi * 128:(bi + 1) * 128], U(kc, tstart, 128, dil),
                                    wb.t[:, kc * 384 + 256: kc * 384 + 384], kc == 0, kc == KC - 1))
                MMG(mms, [wb, uTm, uTh], [pb])
                COPY("scalar", Vb.t[:, q0:q0 + len(grp), :],
                     pb.t[:, 0:len(grp) * 128].rearrange("p (a b) -> p a b", b=128), [pb], [Vb])
                need = [(bi, r, mb) for bi, (r, mb) in enumerate(grp)
                        if mb >= 1 and (mb * 128 * dil + base) + 127 * dil + r >= S0 - win]
                if need:
                    ks = kst[kst_i[0] % 2]; kst_i[0] += 1
                    COPY("vector", ks.t[:, 0:len(grp) * 128], pb.t[:, 0:len(grp) * 128], [pb], [ks])
                    for (bi, r, mb) in need:
                        ts_ = base + mb * 128 * dil + r - (S0 - win)
                        DMA(vo[g][ts_: ts_ + 127 * dil + 1: dil, h, :], ks.t[:, bi * 128:(bi + 1) * 128], reads=[ks])
                pstep()
            pb = PS()
            MMG([(pb.t[:, 0:128], U(kc, S0, 128), wb.t[:, kc * 384 + 256: kc * 384 + 384], kc == 0, kc == KC - 1)
                 for kc in range(KC)], [wb, uTm], [pb])
            COPY("scalar", Vs.t[:], pb.t[:, 0:128], [pb], [Vs])
            for r in range(dil):
                for mb in range(1, nmb):
                    qstart = (mb - 1) * 128 * dil + r
                    qcols = slice(qstart, qstart + 127 * dil + 1, dil)
                    kprev = base + (mb - 1) * 128 * dil + r
                    kown = base + mb * 128 * dil + r
                    bprev = r * nmb + mb - 1
                    i2 = cnt[0] % 2; cnt[0] += 1
                    pS = PS()
                    MMG([(pS.t[:, 0:128], KT.t[:, kprev: kprev + 127 * dil + 1: dil], QT.t[:, qcols], True, True),
                         (pS.t[:, 128:256], KT.t[:, kown: kown + 127 * dil + 1: dil], QT.t[:, qcols], False, True)],
                        [KT, QT], [pS])
                    ACT(Eb[i2].t[:], pS.t[:, 0:256], AF.Exp, [pS], [Eb[i2]], scale=QSCALE)
                    mm_ = mkh if mb == 1 else mk
                    TT("gpsimd" if i2 else "vector", Pb[i2].t[:], Eb[i2].t[:], mm_.t[:], ALU.mult, [Eb[i2], mm_], [Pb[i2]])
                    pO = PS()
                    MMG([(pO.t[:, 0:128], Vb.t[:, bprev, :], Pb[i2].t[:, 0:128], True, False),
                         (pO.t[:, 0:128], Vb.t[:, bprev + 1, :], Pb[i2].t[:, 128:256], False, True),
                         (pO.t[:, 128:256], ones_bf.t[:], Pb[i2].t[:, 0:128], False, False),
                         (pO.t[:, 128:256], ones_bf.t[:], Pb[i2].t[:, 128:256], False, True)],
                        [Vb, Pb[i2], ones_bf], [pO])
                    if g == 0:
                        COPY("vector", numacc.t[:, qcols], pO.t[:, 0:128], [pO], [numacc])
                        COPY("vector", denacc.t[:, qcols], pO.t[:, 128:256], [pO], [denacc])
                    else:
                        TT("vector", numacc.t[:, qcols], pO.t[:, 0:128], numacc.t[:, qcols], ALU.add, [pO, numacc], [numacc])
                        TT("vector", denacc.t[:, qcols], pO.t[:, 128:256], denacc.t[:, qcols], ALU.add, [pO, denacc], [denacc])
            QS = QT.t[:, MAIN:NO]
            pS = PS()
            MMG([(pS.t[:, 0:128], KT.t[:, S0:NT], QS, True, True)], [KT, QT], [pS])
            i2 = cnt[0] % 2; cnt[0] += 1
            ACT(sE[i2].t[:], pS.t[:, 0:128], AF.Exp, [pS], [sE[i2]], scale=QSCALE)
            TT("gpsimd", sP[i2].t[:], sE[i2].t[:], nmk.t[:], ALU.mult, [sE[i2], nmk], [sP[i2]])
            MMG([(sacc.t[:, 0:128], Vs.t[:], sP[i2].t[:], sacc_first[0], False),
                 (sacc.t[:, 128:256], ones_bf.t[:], sP[i2].t[:], False, False)], [Vs, sP[i2], ones_bf], [sacc])
            sacc_first[0] = False
            pump(drain=True)
            if g == 0 and h > 0:
                fin_sample(h - 1)
            COPY("vector", QSs[par].t[:], QT.t[:, MAIN:NO], [QT], [QSs[par]])
            for b in range(nsc):
                pending.append(mk_cache(h, g, b))
            if g == 2:
                fin_prompt(h)
        return (loader, body)

    def mk_cache(h, g, b):
        win, dil = GROUPS[g]
        nb = win // 128
        gh = g * 4 + h
        sacc = saccs[h % 2]
        par = (h * 3 + g) % 2
        smk = smk_[par]

        def loader():
            return rings_c.load([
                (0, 1, win, kTc[g][b, h].unsqueeze(1)),
                (2048, nb, 128, kvc[g][b, :, 512 + h * 128: 512 + (h + 1) * 128].rearrange("(a p) d -> p a d", p=128),
                 "scalar"),
            ])

        def body(cb_):
            QSb = QSs[par]
            QS = QSb.t[:]
            i2 = cnt[0] % 2; cnt[0] += 1
            pS = PS()
            MMG([(pS.t[:, blk * 8:(blk + 1) * 8], cb_.t[:, blk * 128:(blk + 1) * 128], QS[:, b * 8:(b + 1) * 8],
                  blk == 0, True) for blk in range(nb)], [cb_, QSb], [pS])
            ACT(sE[i2].t[:, 0:nb * 8], pS.t[:, 0:nb * 8], AF.Exp, [pS], [sE[i2]], scale=QSCALE)
            TT("vector", sP[i2].t[:, 0:nb * 8], sE[i2].t[:, 0:nb * 8], smk.t[:, 0:nb * 8], ALU.mult,
               [sE[i2], smk], [sP[i2]])
            if nb > 1:
                P.op("vector", (lambda a, bb: (lambda e: e.tensor_reduce(
                    out=a, in_=bb, axis=mybir.AxisListType.X, op=ALU.add)))(
                    sPs[i2].t[:, 0:8], sP[i2].t[:, 0:nb * 8].rearrange("p (a t) -> p t a", t=8)),
                    [sP[i2]], [sPs[i2]])
                srcs, onesx = sPs[i2], ones_f
            else:
                srcs, onesx = sP[i2], ones_bf
            sden = srcs.t[:, 0:8]
            mms = [(sacc.t[:, b * 8:(b + 1) * 8], cb_.t[:, 2048 + blk * 128: 2048 + (blk + 1) * 128],
                    sP[i2].t[:, blk * 8:(blk + 1) * 8], False, False) for blk in range(nb)]
            mms.append((sacc.t[:, 128 + b * 8:128 + (b + 1) * 8], onesx.t[:], sden, False, False))
            MMG(mms, [cb_, sP[i2], srcs, onesx], [sacc])
        return (loader, body)

    def fin_prompt(h):
        for c0 in range(0, MAIN, 512):
            RECIP(rden.t[:], denacc.t[:, c0:c0 + 512], [denacc], [rden])
            TT("vector", attnT.t[:, h, c0:c0 + 512], numacc.t[:, c0:c0 + 512], rden.t[:], ALU.mult,
               [numacc, rden], [attnT])

    def fin_sample(h):
        sacc = saccs[h % 2]
        RECIP(rden.t[:, 0:128], sacc.t[:, 128:256], [sacc], [rden])
        TT("vector", attnT.t[:, h, MAIN:NO], sacc.t[:, 0:128], rden.t[:, 0:128], ALU.mult, [sacc, rden], [attnT])

    for h in range(4):
        for g in range(3):
            items.append(mk_gh(h, g))
    run_stream(items, 0)
    pump(drain=True)
    fin_sample(3)
    ps_n[0] = 6
    P.barrier()
    if kstop <= 2:
        return finish()

    R3 = Region(big, o_hi, ARENA_WORDS)
    hgT = R3.alloc("hgT", [128, 8, NO], BF16)
    o_after_hg = R3.top
    wbf_hg = R3.alloc("wbf_hg", [128, 4096], BF16)
    Kt_ = [R3.alloc("Kt%d" % i, [128, 512], BF16) for i in range(2)]
    Kt2_ = [R3.alloc("Kt2%d" % i, [128, 512], BF16) for i in range(2)]
    Qt_ = [R3.alloc("Qt%d" % i, [128, 512], BF16) for i in range(2)]
    sgT_ = [R3.alloc("sgT%d" % i, [128, 512]) for i in range(2)]
    Vh_ = [R3.alloc("Vh%d" % i, [128, 4, 128], BF16) for i in range(2)]
    Sall = R3.alloc("Sall", [128, 8, 128], BF16)
    Sf = [R3.alloc("Sf%d" % i, [128, 128]) for i in range(2)]
    S0f = R3.alloc("S0f", [128, 16, 128])
    S0b = R3.alloc("S0b", [128, 16, 128], BF16)
    Ktm4_ = [R3.alloc("Ktm4%d" % i, [128, 512], BF16) for i in range(2)]
    Ktmm = [R3.alloc("Ktmm%d" % i, [128, 128], BF16) for i in range(2)]
    hb_base = R3.top
    HB = [[R3.alloc("hB%d_%d" % (i, k), [128, 512]) for k in range(6)] for i in range(2)]
    hg_stage = big[:, hb_base:hb_base + 4096]
    hg_stage_bufs = HB[0] + HB[1][:2]

    def load_hg(hh):
        for bi, cb in enumerate((C_QH, C_FH, C_IH, C_OG)):
            DMA(hg_stage.rearrange("p (a b) -> p a b", b=512)[:, :, bi * 128:(bi + 1) * 128],
                wsrc(w_in, cb + hh * 128, 128), writes=hg_stage_bufs)
        COPY("vector" if hh % 2 else "scalar", wbf_hg.t[:], hg_stage, hg_stage_bufs, [wbf_hg])
        return wbf_hg
    dec_ = [R3.alloc("h_dec%d" % i, [128, 16]) for i in range(2)]
    Am4_ = [R3.alloc("Am4%d" % i, [128, 512], BF16) for i in range(2)]
    osq = R3.alloc("osq", [128, 512]); rstd = R3.alloc("rstd", [128, 512]); otmp = R3.alloc("otmp", [128, 512])
    tcount = [0]
    bcount = [0]

    psB = [psb[4], psb[5], psacc]
    psB_i = [0]

    def PSB():
        b = psB[psB_i[0] % 3]
        psB_i[0] += 1
        return b

    def proj_bank(pb, wb, row, woff, t0, wd):
        MMG([(pb.t[:, 0:wd], wb.t[:, kc * row + woff: kc * row + woff + 128], U(kc, t0, wd), kc == 0, kc == KC - 1)
             for kc in range(KC)], [wb, uTm, uTh], [pb])
        return pb

    def hg_head(hh, wb):
        DMA(S0f.t[:], state_d[:, hh].rearrange("b k v -> k b v"), writes=[S0f])
        issue_copies((3 * nsc + 7) // 8)
        COPY("gpsimd", S0b.t[:], S0f.t[:], [S0f], [S0b])
        st_ = {"sidx": 0}
        MEMSET("vector", Sf[0].t[:], 0.0, [Sf[0]])
        tiles = HALO_TILES + MAIN_TILES + [SAMP_TILE]

        def ctx(ti):
            t0, wd = tiles[ti]
            c = dict(t0=t0, wd=wd, halo=t0 < M0, samp=t0 >= S0, par=ti % 2)
            c["C"] = 8 if c["samp"] else 64
            c["nchk"] = wd // c["C"]
            c["nblk"] = wd // 128
            c["o0"] = t0 - M0
            return c

        def stageA(ti):
            c = ctx(ti)
            t0, wd, par, samp, halo = c["t0"], c["wd"], c["par"], c["samp"], c["halo"]
            B1, B2, B3, B4, B5, B6 = HB[par]
            Vh, sgT = Vh_[par], sgT_[par]
            pF = proj_bank(psb[0], wb, 512, 128, t0, wd)
            pV = psb[1]
            mms = []
            for bi in range(c["nblk"]):
                for kc in range(KC):
                    mms.append((pV.t[:, bi * 128:(bi + 1) * 128], U(kc, t0 + bi * 128, 128),
                                wb.t[:, kc * 512 + 256: kc * 512 + 384], kc == 0, kc == KC - 1))
            MMG(mms, [wb, uTm, uTh], [pV])
            ACT(B1.t[:, 0:wd], pF.t[:, 0:wd], AF.Exp, [pF], [B1], scale=-1.0)
            ACT(B2.t[:, 0:wd], B1.t[:, 0:wd], AF.Ln, [B1, lbT], [B2], scale=lbT.t[:, hh:hh + 1], bias=1.0)
            ACT(B3.t[:, 0:wd], B1.t[:, 0:wd], AF.Ln, [B1], [B3], bias=1.0)
            COPY("scalar", Vh.t[:, 0:c["nblk"], :], pV.t[:, 0:wd].rearrange("p (a b) -> p a b", b=128), [pV], [Vh])
            TT("gpsimd", B2.t[:, 0:wd], B2.t[:, 0:wd], B3.t[:, 0:wd], ALU.subtract, [B2, B3], [B2])
            P.op("vector", (lambda o_, d0, d1: (lambda e: e.tensor_tensor_scan(
                out=o_, data0=d0, data1=d1, initial=0.0, op0=ALU.mult, op1=ALU.add)))(
                B4.t[:, 0:wd], reset.t[:, 1 if samp else 0, 0:wd], B2.t[:, 0:wd]), [reset, B2], [B4])
            if not halo:
                pQ = proj_bank(psb[2], wb, 512, 0, t0, wd)
                pG = proj_bank(psb[3], wb, 512, 384, t0, wd)
                ACT(B5.t[:, 0:wd], pQ.t[:, 0:wd], AF.Exp, [pQ], [B5], scale=-1.0)
                ACT(B5.t[:, 0:wd], B5.t[:, 0:wd], AF.Ln, [B5], [B5], bias=1.0)
                ACT(B5.t[:, 0:wd], B5.t[:, 0:wd], AF.Exp, [B5], [B5], scale=-1.0)
                TT("vector", B5.t[:, 0:wd], pQ.t[:, 0:wd], B5.t[:, 0:wd], ALU.mult, [pQ, B5], [B5])
                ACT(B6.t[:, 0:wd], pG.t[:, 0:wd], AF.Exp, [pG], [B6], scale=-1.0)
                ACT(B6.t[:, 0:wd], B6.t[:, 0:wd], AF.Ln, [B6], [B6], bias=1.0)
                ACT(B6.t[:, 0:wd], B6.t[:, 0:wd], AF.Exp, [B6], [B6], scale=-1.0)
                STT(sgT.t[:, 0:wd], pG.t[:, 0:wd], hgw.t[:, 0:1], B6.t[:, 0:wd], ALU.mult, ALU.mult,
                    [pG, hgw, B6], [sgT])

        def stageB(ti):
            c = ctx(ti)
            wd, par, C, nchk, halo = c["wd"], c["par"], c["C"], c["nchk"], c["halo"]
            B1, B2, B3, B4, B5, B6 = HB[par]
            Kt, Kt2, Qt, dec = Kt_[par], Kt2_[par], Qt_[par], dec_[par]
            ACT(B1.t[:, 0:wd], B3.t[:, 0:wd], AF.Exp, [B3], [B1], scale=-1.0)
            ACT(B6.t[:, 0:wd], B4.t[:, 0:wd], AF.Exp, [B4], [B6], scale=-1.0)
            ACT(dec.t[:, 0:nchk], B4.t[:, C - 1:wd:C], AF.Exp, [B4], [dec])
            if not halo:
                ACT(B2.t[:, 0:wd], B4.t[:, 0:wd], AF.Exp, [B4], [B2])
            TS("vector", B3.t[:, 0:wd], B1.t[:, 0:wd], nlb1.t[:, hh:hh + 1], lb1.t[:, hh:hh + 1], ALU.mult, ALU.add,
               [B1, nlb1, lb1], [B3])
            TT("gpsimd", Kt.t[:, 0:wd], B3.t[:, 0:wd], B6.t[:, 0:wd], ALU.mult, [B3, B6], [Kt])
            TT("gpsimd", Kt2.t[:, 0:wd].rearrange("p (c t) -> p c t", t=C), Kt.t[:, 0:wd].rearrange("p (c t) -> p c t", t=C),
               dec.t[:, 0:nchk].unsqueeze(2).broadcast_to([128, nchk, C]), ALU.mult, [Kt, dec], [Kt2])
            if not halo:
                TT("vector", Qt.t[:, 0:wd], B5.t[:, 0:wd], B2.t[:, 0:wd], ALU.mult, [B5, B2], [Qt])

        def stageC(ti):
            c = ctx(ti)
            t0, wd, par, C, samp, halo, nblk, o0 = c["t0"], c["wd"], c["par"], c["C"], c["samp"], c["halo"], c["nblk"], c["o0"]
            Kt, Kt2, Qt, sgT, Vh, dec = Kt_[par], Kt2_[par], Qt_[par], sgT_[par], Vh_[par], dec_[par]
            Ktm4, Am4 = Ktm4_[par], Am4_[par]
            sidx = st_["sidx"]
            for bl in range(nblk):
                TRANSPOSE(pst.t[:, bl * 128:(bl + 1) * 128], Kt2.t[:, bl * 128:(bl + 1) * 128], ident_bf.t[:],
                          [Kt2, ident_bf], [pst])
            COPY("scalar", Ktm4.t[:, 0:wd], pst.t[:, 0:wd], [pst], [Ktm4])
            if not samp:
                pDs = []
                for cc in range(2):
                    pD = PSB()
                    MMG([(pD.t[:, bl * 128:(bl + 1) * 128], Ktm4.t[cc * 64:(cc + 1) * 64, bl * 128:(bl + 1) * 128],
                          Vh.t[cc * 64:(cc + 1) * 64, bl, :], bl == 0, True) for bl in range(nblk)], [Ktm4, Vh], [pD])
                    pDs.append(pD)
                for ch in range(2 * nblk):
                    if not halo:
                        COPY("gpsimd", Sall.t[:, ch, :], Sf[sidx].t[:], [Sf[sidx]], [Sall])
                    bl, cc = ch // 2, ch % 2
                    pD = pDs[cc]
                    STT(Sf[1 - sidx].t[:], Sf[sidx].t[:], dec.t[:, ch:ch + 1], pD.t[:, bl * 128:(bl + 1) * 128], ALU.mult, ALU.add,
                        [Sf[sidx], dec, pD], [Sf[1 - sidx]])
                    sidx = 1 - sidx
            else:
                for q4 in range(4):
                    pD = PSB()
                    for q in range(4):
                        cc = q4 * 4 + q
                        i2 = cc % 2
                        TS("vector" if cc % 2 else "gpsimd", Ktmm[i2].t[:], Ktm4.t[:, 0:128], oh.t[:, cc:cc + 1], None, ALU.mult, None,
                           [Ktm4, oh], [Ktmm[i2]])
                        MMG([(pD.t[:, q * 128:(q + 1) * 128], Ktmm[i2].t[:], Vh.t[:, 0, :], q == 0, True)], [Ktmm[i2], Vh], [pD])
                    for q in range(4):
                        cc = q4 * 4 + q
                        STT(S0f.t[:, cc, :], S0f.t[:, cc, :], dec.t[:, cc:cc + 1], pD.t[:, q * 128:(q + 1) * 128], ALU.mult, ALU.add,
                            [S0f, dec, pD], [S0f])
            if t0 + wd == M0:
                TS("vector", Sf[1 - sidx].t[:], Sf[sidx].t[:], flag.t[:, 0:1], None, ALU.mult, None,
                   [Sf[sidx], flag], [Sf[1 - sidx]])
                sidx = 1 - sidx
            if t0 + wd == S0:
                DMA(s_out[hh], Sf[sidx].t[:], reads=[Sf[sidx]])
            if samp:
                DMA(state_o[:, hh].rearrange("b k v -> k b v"), S0f.t[:], reads=[S0f])
            st_["sidx"] = sidx

        def stageC2(ti):
            c = ctx(ti)
            t0, wd, par, C, samp, halo, nblk, o0 = c["t0"], c["wd"], c["par"], c["C"], c["samp"], c["halo"], c["nblk"], c["o0"]
            Kt, Kt2, Qt, sgT, Vh, dec = Kt_[par], Kt2_[par], Qt_[par], sgT_[par], Vh_[par], dec_[par]
            Ktm4, Am4 = Ktm4_[par], Am4_[par]
            if halo:
                return
            pA = PSB()
            MMG([(pA.t[:, bl * 128:(bl + 1) * 128], Kt.t[:, bl * 128:(bl + 1) * 128], Qt.t[:, bl * 128:(bl + 1) * 128],
                  bl == 0, True) for bl in range(nblk)], [Kt, Qt], [pA])
            TT("vector", Am4.t[:, 0:wd].rearrange("p (a b) -> p a b", b=128), pA.t[:, 0:wd].rearrange("p (a b) -> p a b", b=128),
               hmask.t[:, (1 if samp else 0):(2 if samp else 1), :].broadcast_to([128, nblk, 128]), ALU.mult, [pA, hmask], [Am4])
            pO = PSB()
            mms = []
            for bl in range(nblk):
                c0 = bl * 128
                mms.append((pO.t[:, c0:c0 + 128], Vh.t[:, bl, :], Am4.t[:, c0:c0 + 128], bl == 0, False))
                if samp:
                    for cc in range(16):
                        mms.append((pO.t[:, cc * 8:(cc + 1) * 8], S0b.t[:, cc, :], Qt.t[:, cc * 8:(cc + 1) * 8], False, cc == 15))
                else:
                    for cc in range(2):
                        mms.append((pO.t[:, c0 + cc * 64:c0 + (cc + 1) * 64], Sall.t[:, bl * 2 + cc, :],
                                    Qt.t[:, c0 + cc * 64:c0 + (cc + 1) * 64], False, cc == 1))
            MMG(mms, [Vh, Am4, S0b, Sall, Qt], [pO])
            ACT(osq.t[:, 0:wd], pO.t[:, 0:wd], AF.Square, [pO], [osq])
            pM = PSB()
            MMG([(pM.t[:, 0:wd], ones_r.t[:], osq.t[:, 0:wd], True, True)], [ones_r, osq], [pM])
            ACT(rstd.t[:, 0:wd], pM.t[:, 0:wd], AF.Ln, [pM], [rstd], bias=RMS_EPS)
            ACT(rstd.t[:, 0:wd], rstd.t[:, 0:wd], AF.Exp, [rstd], [rstd], scale=-0.5)
            TT("vector", otmp.t[:, 0:wd], pO.t[:, 0:wd], rstd.t[:, 0:wd], ALU.mult, [pO, rstd], [otmp])
            TT("gpsimd", hgT.t[:, hh, o0:o0 + wd], otmp.t[:, 0:wd], sgT.t[:, 0:wd], ALU.mult, [otmp, sgT], [hgT])

        n_t = len(tiles)
        stageA(0)
        for ti in range(n_t):
            stageB(ti)
            if ti + 1 < n_t:
                stageA(ti + 1)
            stageC(ti)
            stageC2(ti)

    run_stream([((lambda hh=hh: load_hg(hh)), (lambda wb, hh=hh: hg_head(hh, wb))) for hh in range(8)], 0)
    issue_copies(len(copy_jobs))
    P.barrier()
    if kstop <= 3:
        return finish()

    mg_m = Region(big, o_uTh, o_hi).alloc("mg_m", [128, KC, MAIN], BF16)

    def MG(c8, o0, wd):
        return mg_m.t[:, c8, o0:o0 + wd] if o0 < MAIN else mg_s.t[:, c8, 0:wd]

    R4 = Region(big, o_after_hg, ARENA_WORDS)
    rings = Rings(R4, 2, 3)
    ea_ = [R4.alloc("ea%d" % i, [128, 512]) for i in range(2)]
    eb_ = [R4.alloc("eb%d" % i, [128, 512]) for i in range(2)]
    m1_ = [R4.alloc("m1%d" % i, [128, 512]) for i in range(2)]
    p3cnt = [0]

    def p3a_load(j):
        s_ = rings.stage()
        d_ = rings.bf[rings.j % len(rings.bf)]
        rings.j += 1
        for bi, (w_, c_) in enumerate(((w_in, C_GA + j * 128), (w_in, C_GB + j * 128), (w_bb, j * 128))):
            DMA(s_.t[:, 0:3072].rearrange("p (a b) -> p a b", b=384)[:, :, bi * 128:(bi + 1) * 128],
                wsrc(w_, c_, 128), writes=[s_])
        DMA(s_.t[:, 3072:3584].rearrange("p (a b) -> p a b", b=128), wsrc(w_ba, j * 128, 128, kc=4), writes=[s_])
        COPY("scalar" if j % 2 else "vector", d_.t[:, 0:3584], s_.t[:, 0:3584], [s_], [d_])
        return d_

    def p3a_body(j, wb):
        for (t0, wd) in MAIN_TILES + [SAMP_TILE]:
            o0 = t0 - M0
            pa = proj_fm(wb, 384, 0, t0, wd)
            pbb = proj_fm(wb, 384, 128, t0, wd)
            pBA = PS()
            MMG([(pBA.t[:, 0:wd], wb.t[:, 3072 + c4 * 128: 3072 + (c4 + 1) * 128], attnT.t[:, c4, o0:o0 + wd],
                  c4 == 0, c4 == 3) for c4 in range(4)], [wb, attnT], [pBA])
            pBB = PS()
            MMG([(pBB.t[:, 0:wd], wb.t[:, c8 * 384 + 256: c8 * 384 + 384], hgT.t[:, c8, o0:o0 + wd],
                  c8 == 0, c8 == 7) for c8 in range(8)], [wb, hgT], [pBB])
            i2 = p3cnt[0] % 2
            p3cnt[0] += 1
            ea, eb, m1 = ea_[i2], eb_[i2], m1_[i2]
            ACT(ea.t[:, 0:wd], pa.t[:, 0:wd], AF.Exp, [pa], [ea], scale=-1.0)
            ACT(ea.t[:, 0:wd], ea.t[:, 0:wd], AF.Ln, [ea], [ea], bias=1.0)
            ACT(ea.t[:, 0:wd], ea.t[:, 0:wd], AF.Exp, [ea], [ea], scale=-1.0)
            TT("vector", m1.t[:, 0:wd], pBA.t[:, 0:wd], ea.t[:, 0:wd], ALU.mult, [pBA, ea], [m1])
            ACT(eb.t[:, 0:wd], pbb.t[:, 0:wd], AF.Exp, [pbb], [eb], scale=-1.0)
            ACT(eb.t[:, 0:wd], eb.t[:, 0:wd], AF.Ln, [eb], [eb], bias=1.0)
            ACT(eb.t[:, 0:wd], eb.t[:, 0:wd], AF.Exp, [eb], [eb], scale=-1.0)
            TT("vector", eb.t[:, 0:wd], pBB.t[:, 0:wd], eb.t[:, 0:wd], ALU.mult, [pBB, eb], [eb])
            TT("gpsimd", MG(j, o0, wd), m1.t[:, 0:wd], eb.t[:, 0:wd], ALU.add, [m1, eb], [mg_m, mg_s])

    run_stream([((lambda j=j: p3a_load(j)), (lambda wb, j=j: p3a_body(j, wb))) for j in range(KC)], 2)
    P.barrier()
    if kstop <= 4:
        return finish()

    RL = Region(big, o_uTm, o_uTh)
    RH = Region(big, o_hi, ARENA_WORDS)
    hid = RL.alloc("hid", [128, 32, 512], BF16)
    h1 = RL.alloc("h1", [128, KC, 512])
    x1 = RH.alloc("x1", [128, KC, 512])
    u2 = RH.alloc("u2", [128, KC, 512], BF16)
    rings = Rings(RH, 2, 3)
    xres = [RH.alloc("xres%d" % i, [128, 512]) for i in range(2)]
    sq_ = [RH.alloc("sq%d" % i, [128, 512]) for i in range(2)]
    m2_ = RH.alloc("m2", [128, 512]); rs_ = RH.alloc("rs", [128, 512])
    tA = RH.alloc("tA", [128, 512]); tB = RH.alloc("tB", [128, 512]); hs_ = RH.alloc("hs", [128, 512])
    yst = [RH.alloc("yst%d" % i, [128, 512]) for i in range(2)]

    def layer_norm(src, wd, emit_j):
        pM = PS()
        MMG([(pM.t[:, 0:wd], ones_m.t[:], src.t[:, j, 0:wd], j == 0, j == KC - 1) for j in range(KC)], [ones_m, src], [pM])
        pV = PS()
        for j in range(KC):
            s2 = sq_[j % 2]
            ACT(s2.t[:, 0:wd], src.t[:, j, 0:wd], AF.Square, [src], [s2])
            MMG([(pV.t[:, 0:wd], ones_m.t[:], s2.t[:, 0:wd], j == 0, j == KC - 1)], [ones_m, s2], [pV])
        ACT(m2_.t[:, 0:wd], pM.t[:, 0:wd], AF.Square, [pM], [m2_])
        TT("vector", m2_.t[:, 0:wd], pV.t[:, 0:wd], m2_.t[:, 0:wd], ALU.subtract, [pV, m2_], [m2_])
        ACT(rs_.t[:, 0:wd], m2_.t[:, 0:wd], AF.Ln, [m2_], [rs_], bias=LN_EPS)
        ACT(rs_.t[:, 0:wd], rs_.t[:, 0:wd], AF.Exp, [rs_], [rs_], scale=-0.5)
        for j in range(KC):
            TT("vector", tA.t[:, 0:wd], src.t[:, j, 0:wd], pM.t[:, 0:wd], ALU.subtract, [src, pM], [tA])
            TT("gpsimd", tB.t[:, 0:wd], tA.t[:, 0:wd], rs_.t[:, 0:wd], ALU.mult, [tA, rs_], [tB])
            emit_j(j)

    items = []

    def p3b_tile(t0, wd):
        samp = t0 >= S0
        o0 = t0 - M0

        def emit1(j):
            affine("gpsimd", x1.t[:, j, 0:wd], tB.t[:, 0:wd], g1t, j, c1b, j, samp, [tB], [x1], tmp=tA)
            affine("gpsimd", u2.t[:, j, 0:wd], tB.t[:, 0:wd], gA2, j, bA2, j, samp, [tB], [u2], tmp=tA)

        def emit2(j):
            ys = yst[j % 2]
            affine("gpsimd", ys.t[:, 0:wd], tB.t[:, 0:wd], g2t, j, b2t, j, False, [tB], [ys])
            DMA(yT[j, :, o0:o0 + wd], ys.t[:, 0:wd], reads=[ys])

        def wo_body(k, wo_b):
            for j in range(4 * k, 4 * k + 4):
                xr = xres[j % 2]
                DMA(xr.t[:, 0:wd], xT[j, :, t0:t0 + wd], writes=[xr])
                pX = PS()
                MMG([(pX.t[:, 0:wd], wo_b.t[:, c8 * 512 + (j % 4) * 128: c8 * 512 + (j % 4 + 1) * 128], MG(c8, o0, wd),
                      c8 == 0, c8 == 7) for c8 in range(8)], [wo_b, mg_m, mg_s], [pX])
                if not samp:
                    STT(h1.t[:, j, 0:wd], pX.t[:, 0:wd], G1a.t[:, j, 0:1], xr.t[:, 0:wd], ALU.mult, ALU.add,
                        [pX, G1a, xr], [h1])
                else:
                    TT("vector", v3(tA.t[:, 0:wd], 8), v3(pX.t[:, 0:wd], 8), bc(G1a, j, 1, 16, 8), ALU.mult, [pX, G1a], [tA])
                    TT("gpsimd", h1.t[:, j, 0:wd], tA.t[:, 0:wd], xr.t[:, 0:wd], ALU.add, [tA, xr], [h1])
            if k == 1:
                layer_norm(h1, wd, emit1)
        for k in range(2):
            items.append(((lambda k=k: rings.load([(0, KC, 512, wsrc(w_out, k * 512, 512))])),
                          (lambda wb, k=k: wo_body(k, wb))))

        def wu_body(gq, wu):
            for fc in range(4 * gq, 4 * gq + 4):
                pH = PS()
                MMG([(pH.t[:, 0:wd], wu.t[:, kc * 512 + (fc % 4) * 128: kc * 512 + (fc % 4 + 1) * 128], u2.t[:, kc, 0:wd],
                      kc == 0, kc == KC - 1) for kc in range(KC)], [wu, u2], [pH])
                ACT(hs_.t[:, 0:wd], pH.t[:, 0:wd], AF.Identity, [pH, bup], [hs_], bias=bup.t[:, fc:fc + 1])
                STT(hid.t[:, fc, 0:wd], hs_.t[:, 0:wd], 0.0, hs_.t[:, 0:wd], ALU.max, ALU.mult, [hs_], [hid])
        for gq in range(8):
            items.append(((lambda gq=gq: rings.load([(0, KC, 512, wsrc(w_up, gq * 512, 512))])),
                          (lambda wb, gq=gq: wu_body(gq, wb))))

        def wd_body(j, wd_b):
            pF = PS()
            MMG([(pF.t[:, 0:wd], wd_b.t[:, c * 128:(c + 1) * 128], hid.t[:, c, 0:wd], c == 0, c == 31)
                 for c in range(32)], [wd_b, hid], [pF])
            if not samp:
                STT(h1.t[:, j, 0:wd], pF.t[:, 0:wd], G2a.t[:, j, 0:1], x1.t[:, j, 0:wd], ALU.mult, ALU.add,
                    [pF, G2a, x1], [h1])
            else:
                TT("vector", v3(tA.t[:, 0:wd], 8), v3(pF.t[:, 0:wd], 8), bc(G2a, j, 1, 16, 8), ALU.mult, [pF, G2a], [tA])
                TT("gpsimd", h1.t[:, j, 0:wd], tA.t[:, 0:wd], x1.t[:, j, 0:wd], ALU.add, [tA, x1], [h1])
            if j == KC - 1:
                layer_norm(h1, wd, emit2)
        for j in range(KC):
            items.append(((lambda j=j: rings.load([(0, 32, 128, w_down[:, j * 128:(j + 1) * 128].rearrange(
                "(k p) n -> p k n", p=128))])), (lambda wb, j=j: wd_body(j, wb))))

    for (t0, wd) in MAIN_TILES + [SAMP_TILE]:
        p3b_tile(t0, wd)
    run_stream(items, 2)

    return finish()


_NC_CACHE = {}
KSTOP = [99]
DEV = {"ncores": NCORES, "nsc": 16}


def _fm(v, n):
    return np.ascontiguousarray(np.asarray(v, np.float32).reshape(n, 128).T)


def kernel(x_prompt, x_sample, c_prompt, c_sample, cache_kv_w128, cache_kv_w512, cache_kv_w2048,
           state_hgrn, w_ada, b_ada, w_in, lb_param, hg_norm_w, w_branch_a, w_branch_b, w_out,
           ln1_g, ln1_b, w_up, b_up, w_down, b_down, ln2_g, ln2_b):
    f = lambda a: np.asarray(a, np.float32)
    x_prompt, x_sample, c_prompt, c_sample = f(x_prompt), f(x_sample), f(c_prompt), f(c_sample)
    caches = [f(cache_kv_w128), f(cache_kv_w512), f(cache_kv_w2048)]
    state_hgrn = f(state_hgrn)
    tabs = const_tables()
    common = dict(
        w_ada=np.ascontiguousarray(f(w_ada)[0]), b_ada=_fm(f(b_ada)[0], 48), w_in=np.ascontiguousarray(f(w_in)[0]),
        lbp=np.ascontiguousarray(f(lb_param).reshape(2, 8, 128).transpose(2, 0, 1)),
        hgw=np.ascontiguousarray(f(hg_norm_w)[0].reshape(128, 1)),
        w_ba=np.ascontiguousarray(f(w_branch_a)[0]), w_bb=np.ascontiguousarray(f(w_branch_b)[0]),
        w_out=np.ascontiguousarray(f(w_out)[0]),
        ln1g=_fm(f(ln1_g)[0], 8), ln1b=_fm(f(ln1_b)[0], 8), ln2g=_fm(f(ln2_g)[0], 8), ln2b=_fm(f(ln2_b)[0], 8),
        w_up=np.ascontiguousarray(f(w_up)[0]), bup=_fm(f(b_up)[0], 32),
        w_down=np.ascontiguousarray(f(w_down)[0]), bdown=_fm(f(b_down)[0], 8),
        masks=tabs["masks"], smask=tabs["smask"], nmask=tabs["nmask"], hmask=tabs["hmask"],
        reset=tabs["reset"], oh=tabs["oh"], ident=np.eye(128, dtype=np.float32),
    )
    in_maps = []
    for c in range(NCORES):
        b, half = c // 2, c % 2
        T0 = half * MAIN
        halo = x_prompt[b, T0 - HALO:T0] if half == 1 else np.zeros((HALO, D), np.float32)
        toks = np.concatenate([halo, x_prompt[b, T0:T0 + MAIN], x_sample[16 * c:16 * c + 16].reshape(SAMP, D)], axis=0)
        cs = np.concatenate([c_prompt[b:b + 1], c_sample[16 * c:16 * c + 16]], axis=0)
        m = dict(common)
        m["xT"] = np.ascontiguousarray(toks.T.reshape(KC, 128, NT))
        m["cT"] = np.ascontiguousarray(cs.T.reshape(KC, 128, 17).transpose(1, 0, 2))
        m["flag"] = np.full((128, 1), float(half), np.float32)
        for (w, _), cache in zip(GROUPS, caches):
            nsc = DEV["nsc"]
            cc = cache[0, 16 * c:16 * c + nsc]
            m["kv%d" % w] = np.ascontiguousarray(cc.reshape(nsc, w, 1024))
            m["kT%d" % w] = np.ascontiguousarray(cc[:, :, 0].transpose(0, 2, 3, 1))
        m["state"] = np.ascontiguousarray(state_hgrn[0, 16 * c:16 * c + 16])
        in_maps.append(m)
    if "nc" not in _NC_CACHE:
        _NC_CACHE["nc"] = build_nc(KSTOP[0], DEV["nsc"])
    ncr = DEV["ncores"]
    if DEV.get("trace"):
        res = run_bass_kernel_spmd(_NC_CACHE["nc"], in_maps[:ncr], core_ids=list(range(ncr)), trace=True)
        print("DEV exec_time_ns", res.exec_time_ns, flush=True)
    else:
        res = run_bass_kernel_spmd(_NC_CACHE["nc"], in_maps[:ncr], core_ids=list(range(ncr)))
    R = res.results
    nsc = DEV["nsc"]
    B, S = x_prompt.shape[0], x_prompt.shape[1]
    y_prompt = np.empty((B, S, D), np.float32)
    y_sample = np.empty((128, 8, D), np.float32)
    pk = [np.empty((1, B, w, 2, 4, 128), np.float32) for (w, _) in GROUPS]
    php = np.empty((1, B, 8, 128, 128), np.float32)
    sk = [np.empty((1, 128, w, 2, 4, 128), np.float32) for (w, _) in GROUPS]
    shs = np.empty((1, 128, 8, 128, 128), np.float32)
    for c in range(ncr):
        b, half = c // 2, c % 2
        T0 = half * MAIN
        yt = np.asarray(R[c]["yT"]).reshape(D, NO)
        y_prompt[b, T0:T0 + MAIN] = yt[:, :MAIN].T
        y_sample[16 * c:16 * c + 16] = yt[:, MAIN:].T.reshape(16, 8, D)
        for gi, (w, _) in enumerate(GROUPS):
            sk[gi][0, 16 * c:16 * c + nsc] = np.asarray(R[c]["kvo%d" % w]).reshape(nsc, w, 2, 4, 128)
            if half == 1:
                pk[gi][0, b, :, 0] = np.asarray(R[c]["kTo%d" % w]).transpose(2, 0, 1)
                pk[gi][0, b, :, 1] = np.asarray(R[c]["vo%d" % w])
        if half == 1:
            php[0, b] = np.asarray(R[c]["s_out"])
        shs[0, 16 * c:16 * c + 16] = np.asarray(R[c]["state_o"])
    return (y_prompt, y_sample, pk[0], pk[1], pk[2], php, sk[0], sk[1], sk[2], shs)
```

```python
import contextlib
import numpy as np
import concourse.bass as bass
import concourse.mybir as mybir
from concourse.bass_utils import run_bass_kernel_spmd

F32 = mybir.dt.float32
BF16 = mybir.dt.bfloat16
ALU = mybir.AluOpType
AF = mybir.ActivationFunctionType

SAME_ENGINE_SYNC = "raw"


def _need_sync(p, o, d):
    if p.is_dma or p.q != o.q:
        return True
    if p.q == "tensor":
        return False
    if SAME_ENGINE_SYNC is True:
        return True
    if SAME_ENGINE_SYNC == "raw":
        return d in o.raw
    return False
COMPUTE = ("tensor", "vector", "scalar", "gpsimd")

NCORES = 8
D = 1024
KC = 8
HALO = 2048
MAIN = 2048
SAMP = 128
NT = HALO + MAIN + SAMP
M0 = HALO
S0 = HALO + MAIN
NO = MAIN + SAMP
GROUPS = ((128, 1), (512, 4), (2048, 16))
ALPHA = 2.0 ** 0.25
LN_EPS = 1e-5 / (ALPHA * ALPHA)
RMS_EPS = 1e-6
QSCALE = 128.0 ** -0.5
C_Q, C_K, C_V = 0, 1536, 3072
C_QH, C_FH, C_IH, C_OG = 4608, 5632, 6656, 7680
C_GA, C_GB = 8704, 9728


def slope(g, h):
    return 2.0 ** (-8.0 * (g * 4 + h + 1) / 12.0)


class Buf:
    __slots__ = ("name", "t", "last_w", "readers", "excl")

    def __init__(self, name, t=None, excl=False):
        self.name = name
        self.t = t
        self.excl = excl
        self.last_w = None
        self.readers = []

    def __getitem__(self, k):
        return self.t[k]


class Op:
    __slots__ = ("q", "fn", "deps", "raw", "is_dma", "signal", "sigval", "sem", "idx", "prev")

    def __init__(self, q, fn, is_dma):
        self.q = q
        self.fn = fn
        self.deps = set()
        self.raw = set()
        self.is_dma = is_dma
        self.signal = False
        self.sigval = None
        self.sem = None
        self.prev = 0


class Prog:
    def __init__(self, nc):
        self.nc = nc
        self.ops = []

    def op(self, q, fn, reads=(), writes=(), dma=False):
        o = Op(q, fn, dma)
        o.idx = len(self.ops)
        for b in reads:
            if b.last_w is not None:
                o.deps.add(b.last_w)
                o.raw.add(b.last_w)
            if b.excl:
                for r in b.readers:
                    if self.ops[r].q != q:
                        o.deps.add(r)
        for b in writes:
            if b.last_w is not None:
                o.deps.add(b.last_w)
            for r in b.readers:
                o.deps.add(r)
        for b in reads:
            b.readers.append(o.idx)
        for b in writes:
            b.last_w = o.idx
            b.readers = []
        o.deps.discard(o.idx)
        self.ops.append(o)
        return o

    def barrier(self):
        self.ops.append(None)

    def emit(self, final_wait_q="sync"):
        nc = self.nc
        ops = self.ops
        lastq = {}
        for o in ops:
            if o is None:
                for q_, lo in lastq.items():
                    lo.signal = True
                continue
            if not o.is_dma:
                lastq[o.q] = o
        for o in ops:
            if o is None:
                continue
            for d in o.deps:
                p = ops[d]
                if _need_sync(p, o, d):
                    p.signal = True
        for o in ops:
            if o is not None and o.is_dma:
                o.signal = True
        stack = contextlib.ExitStack()
        qsem = {q: stack.enter_context(nc.semaphore("s_" + q)) for q in COMPUTE}
        pool_sizes = {"sync": 16, "gpsimd": 8, "scalar": 8}
        dpool = {q: [stack.enter_context(nc.semaphore("d_%s%d" % (q, i))) for i in range(n)]
                 for q, n in pool_sizes.items()}
        dcount = {q: [0] * n for q, n in pool_sizes.items()}
        dnext = {q: 0 for q in pool_sizes}
        qcount = {q: 0 for q in COMPUTE}
        snaps = {}
        for oi, o in enumerate(ops):
            if o is None:
                snaps[oi] = (dict(qcount), {q: list(v) for q, v in dcount.items()})
                continue
            if not o.signal:
                continue
            if o.is_dma:
                i = dnext[o.q]
                dnext[o.q] = (i + 1) % pool_sizes[o.q]
                o.sem = dpool[o.q][i]
                o.prev = dcount[o.q][i]
                dcount[o.q][i] += 16
                o.sigval = dcount[o.q][i]
            else:
                o.sem = qsem[o.q]
                qcount[o.q] += 1
                o.sigval = qcount[o.q]
        with nc.Block() as block:
            def make(qname):
                def body(eng):
                    known = {}
                    for oi, o in enumerate(ops):
                        if o is None:
                            qc, dc = snaps[oi]
                            for q2 in COMPUTE:
                                if qc[q2] > known.get(id(qsem[q2]), 0):
                                    eng.wait_ge(qsem[q2], qc[q2])
                                    known[id(qsem[q2])] = qc[q2]
                            for q2, n in pool_sizes.items():
                                for i in range(n):
                                    if dc[q2][i] > known.get(id(dpool[q2][i]), 0):
                                        eng.wait_ge(dpool[q2][i], dc[q2][i])
                                        known[id(dpool[q2][i])] = dc[q2][i]
                            continue
                        if o.q != qname:
                            continue
                        for d in sorted(o.deps):
                            p = ops[d]
                            if not p.signal:
                                continue
                            if not _need_sync(p, o, d):
                                continue
                            key = id(p.sem)
                            if known.get(key, 0) >= p.sigval:
                                continue
                            eng.wait_ge(p.sem, p.sigval)
                            known[key] = p.sigval
                        if o.is_dma and o.prev > 0 and known.get(id(o.sem), 0) < o.prev:
                            eng.wait_ge(o.sem, o.prev)
                            known[id(o.sem)] = o.prev
                        inst = o.fn(eng)
                        if o.signal:
                            inst.then_inc(o.sem, 16 if o.is_dma else 1)
                    if qname == final_wait_q:
                        for q2, n in pool_sizes.items():
                            for i in range(n):
                                if dcount[q2][i] > 0:
                                    eng.wait_ge(dpool[q2][i], dcount[q2][i])
                        for q in COMPUTE:
                            if qcount[q] > 0:
                                eng.wait_ge(qsem[q], qcount[q])
                return body
            block.sync(make("sync"))
            block.scalar(make("scalar"))
            block.vector(make("vector"))
            block.gpsimd(make("gpsimd"))
            block.tensor(make("tensor"))
        stack.close()


def const_tables():
    j = np.arange(128)[:, None].astype(np.float64)
    i = np.arange(128)[None, :].astype(np.float64)
    masks = np.zeros((12, 128, 256), np.float32)
    smask = np.zeros((12, 128, 128), np.float32)
    nmask = np.zeros((12, 128, 128), np.float32)
    for g, (win, dil) in enumerate(GROUPS):
        for h in range(4):
            sl = slope(g, h)
            prev = np.where(i <= j, np.exp(-sl * dil * (128 + i - j)), 0.0)
            own = np.where(i >= j, np.exp(-sl * dil * (i - j)), 0.0)
            masks[g * 4 + h, :, :128] = prev
            masks[g * 4 + h, :, 128:] = own
            nb = win // 128
            p = np.arange(128)[:, None]
            for blk in range(nb):
                t = np.arange(8)[None, :]
                idx = blk * 128 + p
                diff = win + t - idx
                ok = (diff % dil == 0) & (diff >= 0) & (diff <= win)
                smask[g * 4 + h, :, blk * 8:(blk + 1) * 8] = np.where(ok, np.exp(-sl * diff), 0.0)
            kp = np.arange(128)[:, None]
            qp = np.arange(128)[None, :]
            dj = (qp % 8) - (kp % 8)
            ok = (kp // 8 == qp // 8) & (dj >= 0) & (dj % dil == 0)
            nmask[g * 4 + h] = np.where(ok, np.exp(-sl * dj), 0.0)
    s = np.arange(128)[:, None]
    t = np.arange(128)[None, :]
    hmask = np.zeros((2, 128, 128), np.float32)
    hmask[0] = ((s // 64 == t // 64) & (s <= t)).astype(np.float32)
    hmask[1] = ((s // 8 == t // 8) & (s <= t)).astype(np.float32)
    reset = np.ones((2, 128, 512), np.float32)
    reset[0, :, ::64] = 0.0
    reset[1, :, ::8] = 0.0
    oh = (np.arange(128)[:, None] // 8 == np.arange(16)[None, :]).astype(np.float32)
    return dict(masks=masks, smask=smask, nmask=nmask, hmask=hmask, reset=reset, oh=oh)


ARENA_WORDS = 53100


class Region:
    def __init__(self, big, lo, hi):
        self.big, self.lo, self.hi, self.top = big, lo, hi, lo

    def alloc(self, name, shape, dt=F32):
        assert shape[0] == 128
        n = 1
        for s_ in shape[1:]:
            n *= s_
        words = n if dt == F32 else (n + 1) // 2
        words = (words + 7) // 8 * 8
        assert self.top + words <= self.hi, (name, self.top, words, self.hi)
        ap = self.big[:, self.top:self.top + words]
        if dt != F32:
            ap = ap.bitcast(dt)
        ap = ap[:, 0:n]
        if len(shape) == 3:
            ap = ap.rearrange("p (a b) -> p a b", b=shape[2])
        self.top += words
        return Buf(name, ap)


def build_nc(kstop=99, nsc=16):
    nc = bass.Bass("TRN2", target_bir_lowering=False)
    P = Prog(nc)
    st = contextlib.ExitStack()

    def finish():
        P.emit()
        st.close()
        return nc

    def din(name, shape):
        return nc.dram_tensor(name, list(shape), F32, kind="ExternalInput").ap()

    def dout(name, shape):
        return nc.dram_tensor(name, list(shape), F32, kind="ExternalOutput").ap()

    xT = din("xT", [KC, 128, NT])
    cT = din("cT", [128, KC, 17])
    flag_d = din("flag", [128, 1])
    w_ada = din("w_ada", [D, 6144])
    b_ada = din("b_ada", [128, 48])
    w_in = din("w_in", [D, 10752])
    lbp = din("lbp", [128, 2, 8])
    hgw_d = din("hgw", [128, 1])
    w_ba = din("w_ba", [512, D])
    w_bb = din("w_bb", [D, D])
    w_out = din("w_out", [D, D])
    ln1g_d = din("ln1g", [128, 8]); ln1b_d = din("ln1b", [128, 8])
    ln2g_d = din("ln2g", [128, 8]); ln2b_d = din("ln2b", [128, 8])
    w_up = din("w_up", [D, 4096]); bup_d = din("bup", [128, 32])
    w_down = din("w_down", [4096, D]); bdown_d = din("bdown", [128, 8])
    masks_d = din("masks", [12, 128, 256])
    smask_d = din("smask", [12, 128, 128])
    nmask_d = din("nmask", [12, 128, 128])
    hmask_d = din("hmask", [2, 128, 128])
    reset_d = din("reset", [2, 128, 512])
    oh_d = din("oh", [128, 16])
    ident_d = din("ident", [128, 128])
    kvc = [din("kv%d" % w, [nsc, w, 1024]) for (w, _) in GROUPS]
    kTc = [din("kT%d" % w, [nsc, 4, 128, w]) for (w, _) in GROUPS]
    state_d = din("state", [16, 8, 128, 128])

    yT = dout("yT", [KC, 128, NO])
    kTo = [dout("kTo%d" % w, [4, 128, w]) for (w, _) in GROUPS]
    vo = [dout("vo%d" % w, [w, 4, 128]) for (w, _) in GROUPS]
    s_out = dout("s_out", [8, 128, 128])
    kvo = [dout("kvo%d" % w, [nsc, w, 1024]) for (w, _) in GROUPS]
    state_o = dout("state_o", [16, 8, 128, 128])

    big = st.enter_context(nc.sbuf_tensor("big", [128, ARENA_WORDS], F32))

    def psum(name, shape, dt=F32):
        return Buf(name, st.enter_context(nc.psum_tensor(name, list(shape), dt)), excl=True)

    def DMA(out, in_, reads=(), writes=(), q="sync"):
        P.op(q, lambda e: e.dma_start(out=out, in_=in_), reads, writes, dma=True)

    def ACT(out, in_, func, reads, writes, scale=None, bias=None):
        kw = {}
        if scale is not None:
            kw["scale"] = scale
        if bias is not None:
            kw["bias"] = bias
        P.op("scalar", lambda e: e.activation(out=out, in_=in_, func=func, **kw), reads, writes)

    def TT(q, out, in0, in1, op, reads, writes):
        P.op(q, lambda e: e.tensor_tensor(out=out, in0=in0, in1=in1, op=op), reads, writes)

    def TS(q, out, in0, s1, s2, op0, op1, reads, writes):
        if op1 is None:
            P.op(q, lambda e: e.tensor_scalar(out=out, in0=in0, scalar1=s1, scalar2=None, op0=op0), reads, writes)
        else:
            P.op(q, lambda e: e.tensor_scalar(out=out, in0=in0, scalar1=s1, scalar2=s2, op0=op0, op1=op1), reads, writes)

    def STT(out, in0, scalar, in1, op0, op1, reads, writes):
        P.op("vector", lambda e: e.scalar_tensor_tensor(out=out, in0=in0, scalar=scalar, in1=in1, op0=op0, op1=op1),
             reads, writes)

    def COPY(q, out, in_, reads, writes):
        if q == "scalar":
            P.op(q, lambda e: e.activation(out=out, in_=in_, func=AF.Copy), reads, writes)
        else:
            P.op(q, lambda e: e.tensor_copy(out=out, in_=in_), reads, writes)

    def RECIP(out, in_, reads, writes):
        P.op("vector", lambda e: e.reciprocal(out=out, in_=in_), reads, writes)

    def MEMSET(q, out, val, writes):
        P.op(q, lambda e: e.memset(out, val), (), writes)

    def MMG(mms, reads, writes):
        mms = list(mms)

        def fn(e):
            inst = None
            for (o, l, r, s_, p_) in mms:
                inst = e.matmul(o, lhsT=l, rhs=r, start=s_, stop=p_, skip_group_check=True)
            return inst
        P.op("tensor", fn, reads, writes)

    def TRANSPOSE(out, in_, ident, reads, writes):
        P.op("tensor", lambda e: e.transpose(out, in_, ident), reads, writes)

    psb = [psum("ps%d" % i, [128, 512]) for i in range(6)]
    psacc = psum("psacc", [128, 512])
    pst = psum("pst", [128, 1024], BF16)
    ps_i = [0]
    ps_n = [6]

    def PS():
        b = psb[ps_i[0] % ps_n[0]]
        ps_i[0] += 1
        return b

    RA = Region(big, 0, 4608)
    o_uTm = 4608
    o_attn = o_uTm + 8704
    o_uTh = o_attn + 4352
    o_hi = o_uTh + 8192
    RB = Region(big, o_uTm, o_hi)
    uTm = RB.alloc("uTm", [128, KC, NO], BF16)
    attnT = RB.alloc("attnT", [128, 4, NO], BF16)
    uTh = RB.alloc("uTh", [128, KC, HALO], BF16)
    assert RB.top == o_hi

    def U(kc, t0, n, step=1):
        if t0 < M0:
            assert t0 + (n - 1) * step < M0
            return uTh.t[:, kc, t0: t0 + (n - 1) * step + 1: step]
        a = t0 - M0
        return uTm.t[:, kc, a: a + (n - 1) * step + 1: step]

    def sb(name, shape, dt=F32):
        return RA.alloc(name, shape, dt)

    modT = sb("modT", [128, 48, 17])
    A1 = sb("A1", [128, 8, 17]); A2 = sb("A2", [128, 8, 17])
    G1a = sb("G1a", [128, 8, 17]); G2a = sb("G2a", [128, 8, 17])
    c1b = sb("c1b", [128, 8, 17]); g1t = sb("g1t", [128, 8, 17])
    gA2 = sb("gA2", [128, 8, 17]); bA2 = sb("bA2", [128, 8, 17])
    g2t = sb("g2t", [128, 8, 17]); b2t = sb("b2t", [128, 8, 17])
    flag = sb("flag_s", [128, 1])
    lbT = sb("lbT", [128, 8]); lb1 = sb("lb1", [128, 8]); nlb1 = sb("nlb1", [128, 8])
    hgw = sb("hgw_s", [128, 1])
    ln1g = sb("ln1g_s", [128, 8]); ln1b = sb("ln1b_s", [128, 8])
    ln2g = sb("ln2g_s", [128, 8]); ln2b = sb("ln2b_s", [128, 8])
    bup = sb("bup_s", [128, 32]); bdown = sb("bdown_s", [128, 8])
    ones_bf = sb("ones_bf", [128, 128], BF16)
    ones_m = sb("ones_m", [128, 128])
    ones_r = sb("ones_r", [128, 128])
    ident_bf = sb("ident_bf", [128, 128], BF16)
    hmask = sb("hmask_s", [128, 2, 128])
    reset = sb("reset_s", [128, 2, 512])
    oh = sb("oh_s", [128, 16])
    mg_s = sb("mg_s", [128, KC, 128], BF16)
    ones_f = sb("ones_f", [128, 128])

    class Rings:
        def __init__(self, R, nst, nbf):
            self.st = [R.alloc("wst%d" % i, [128, 4096]) for i in range(nst)]
            self.bf = [R.alloc("wbf%d" % i, [128, 4096], BF16) for i in range(nbf)]
            self.i = 0
            self.j = 0
            self.subs = {}

        def W(self, s_):
            return [s_] + list(self.subs.get(id(s_), ()))

        def stage(self):
            s_ = self.st[self.i % len(self.st)]
            self.i += 1
            return s_

        def load(self, srcs, castq=None):
            s_ = self.stage()
            d_ = self.bf[self.j % len(self.bf)]
            self.j += 1
            tot = 0
            for (off, a, b, ap) in srcs:
                DMA(s_.t[:, off:off + a * b].rearrange("p (a b) -> p a b", b=b), ap, writes=self.W(s_))
                tot = max(tot, off + a * b)
            q = castq or ("scalar" if (self.j % 2 == 0) else "vector")
            COPY(q, d_.t[:, 0:tot], s_.t[:, 0:tot], self.W(s_), [d_])
            return d_

    def run_stream(items, lookahead):
        loaded = {}
        n = len(items)
        for i in range(min(lookahead, n)):
            loaded[i] = items[i][0]()
        for i in range(n):
            if i + lookahead < n:
                loaded[i + lookahead] = items[i + lookahead][0]()
            items[i][1](loaded.pop(i))
        assert not loaded

    def wsrc(w, c0, ncols, kc=KC, r0=0):
        return w[r0:r0 + kc * 128, c0:c0 + ncols].rearrange("(k p) n -> p k n", p=128)

    def wblocks(w, cols, kc=KC, width=128):
        nb_ = len(cols)
        return [(0, None, None, None)] and [
            (bi, c) for bi, c in enumerate(cols)], nb_ * width

    def load_blocks(rings, w, cols, kc=KC, width=128, castq=None):
        s_ = rings.stage()
        d_ = rings.bf[rings.j % len(rings.bf)]
        rings.j += 1
        row = len(cols) * width
        tot = kc * row
        for bi, c in enumerate(cols):
            DMA(s_.t[:, 0:tot].rearrange("p (a b) -> p a b", b=row)[:, :, bi * width:(bi + 1) * width],
                wsrc(w, c, width, kc=kc), writes=rings.W(s_))
        q = castq or ("scalar" if (rings.j % 2 == 0) else "vector")
        COPY(q, d_.t[:, 0:tot], s_.t[:, 0:tot], rings.W(s_), [d_])
        return d_

    def proj_fm(wb, row, woff, t0, wd):
        pb = PS()
        MMG([(pb.t[:, 0:wd], wb.t[:, kc * row + woff: kc * row + woff + 128], U(kc, t0, wd), kc == 0, kc == KC - 1)
             for kc in range(KC)], [wb, uTm, uTh], [pb])
        return pb

    def bc(tab, j, lo, n, rep):
        return tab.t[:, j, lo:lo + n].unsqueeze(2).broadcast_to([128, n, rep])

    def v3(ap, rep):
        return ap.rearrange("p (s t) -> p s t", t=rep)

    def affine(q2, out, in_, stab, sj, btab, bj, sample, reads, writes, tmp=None):
        if not sample:
            ACT(out, in_, AF.Identity, list(reads) + [stab, btab], writes, scale=stab.t[:, sj, 0:1], bias=btab.t[:, bj, 0:1])
        else:
            TT("vector", v3(tmp.t[:, 0:128], 8), v3(in_, 8), bc(stab, sj, 1, 16, 8), ALU.mult, list(reads) + [stab], [tmp])
            TT(q2, v3(out, 8), v3(tmp.t[:, 0:128], 8), bc(btab, bj, 1, 16, 8), ALU.add, [tmp, btab], writes)

    MAIN_TILES = [(M0 + i * 512, 512) for i in range(4)]
    HALO_TILES = [(i * 512, 512) for i in range(4)]
    SAMP_TILE = (S0, 128)

    R0 = Region(big, o_hi, ARENA_WORDS)
    for (dst, src) in ((flag, flag_d), (hgw, hgw_d), (ln1g, ln1g_d), (ln1b, ln1b_d), (ln2g, ln2g_d),
                       (ln2b, ln2b_d), (bup, bup_d), (bdown, bdown_d), (oh, oh_d)):
        DMA(dst.t[:], src, writes=[dst])
    DMA(hmask.t[:], hmask_d.rearrange("a p n -> p a n"), writes=[hmask])
    DMA(reset.t[:], reset_d.rearrange("a p n -> p a n"), writes=[reset])
    MEMSET("vector", ones_bf.t[:], 1.0, [ones_bf])
    MEMSET("vector", ones_m.t[:], 1.0 / 1024.0, [ones_m])
    MEMSET("vector", ones_r.t[:], 1.0 / 128.0, [ones_r])
    MEMSET("vector", ones_f.t[:], 1.0, [ones_f])
    identf = R0.alloc("identf", [128, 128])
    DMA(identf.t[:], ident_d, writes=[identf])
    COPY("vector", ident_bf.t[:], identf.t[:], [identf], [ident_bf])

    kvo_bufs = [Buf("kvo%d" % g) for g in range(3)]
    copy_jobs = []
    for b in range(nsc):
        for g, (w, _) in enumerate(GROUPS):
            copy_jobs.append((g, b, w))

    def issue_copies(n):
        for _ in range(n):
            if copy_jobs:
                g, b, w = copy_jobs.pop(0)
                DMA(kvo[g][b, 0:w - 8, :], kvc[g][b, 8:w, :], q="sync")

    lbs = R0.alloc("lbs", [128, 2, 8])
    DMA(lbs.t[:], lbp, writes=[lbs])
    TT("vector", lbT.t[:], lbs.t[:, 1, :], lbs.t[:, 0, :], ALU.subtract, [lbs], [lbT])
    ACT(lbT.t[:], lbT.t[:], AF.Exp, [lbT], [lbT])
    TS("vector", lbT.t[:], lbT.t[:], 1.0, None, ALU.add, None, [lbT], [lbT])
    RECIP(lbT.t[:], lbT.t[:], [lbT], [lbT])
    TS("vector", lb1.t[:], lbT.t[:], -1.0, 1.0, ALU.mult, ALU.add, [lbT], [lb1])
    TS("vector", nlb1.t[:], lb1.t[:], -1.0, None, ALU.mult, None, [lb1], [nlb1])

    scT = R0.alloc("scT", [128, KC, 17])
    sct = R0.alloc("sct", [128, KC, 17])
    DMA(scT.t[:], cT, writes=[scT])
    ACT(sct.t[:], scT.t[:], AF.Exp, [scT], [sct], scale=-1.0)
    TS("vector", sct.t[:], sct.t[:], 1.0, None, ALU.add, None, [sct], [sct])
    RECIP(sct.t[:], sct.t[:], [sct], [sct])
    TT("vector", scT.t[:], scT.t[:], sct.t[:], ALU.mult, [scT, sct], [scT])
    bada = R0.alloc("bada", [128, 48])
    DMA(bada.t[:], b_ada, writes=[bada])
    wada = [R0.alloc("wada%d" % i, [128, 4096]) for i in range(2)]
    for ng in range(12):
        s_ = wada[ng % 2]
        DMA(s_.t[:, 0:4096].rearrange("p (a b) -> p a b", b=512), wsrc(w_ada, ng * 512, 512), writes=[s_])
        pb = PS()
        mms = []
        for jj in range(4):
            for kc in range(KC):
                mms.append((pb.t[:, jj * 17:(jj + 1) * 17], s_.t[:, kc * 512 + jj * 128: kc * 512 + (jj + 1) * 128],
                            scT.t[:, kc, :], kc == 0 and jj == 0, kc == KC - 1))
        MMG(mms, [s_, scT], [pb])
        for jj in range(4):
            j = ng * 4 + jj
            TS("vector", modT.t[:, j, :], pb.t[:, jj * 17:(jj + 1) * 17], bada.t[:, j:j + 1], None, ALU.add, None,
               [pb, bada], [modT])
    TS("vector", A1.t[:], modT.t[:, 8:16, :], 1.0, None, ALU.add, None, [modT], [A1])
    TS("vector", A2.t[:], modT.t[:, 32:40, :], 1.0, None, ALU.add, None, [modT], [A2])
    TS("vector", G1a.t[:], modT.t[:, 16:24, :], 1.0 / ALPHA, None, ALU.mult, None, [modT], [G1a])
    TS("vector", G2a.t[:], modT.t[:, 40:48, :], 1.0 / ALPHA, None, ALU.mult, None, [modT], [G2a])

    def bcp(v):
        return v.t[:].unsqueeze(2).broadcast_to([128, 8, 17])
    TS("vector", g1t.t[:], A1.t[:], 0.0, None, ALU.mult, None, [A1], [g1t])
    TT("vector", g1t.t[:], g1t.t[:], bcp(ln1g), ALU.add, [g1t, ln1g], [g1t])
    TT("vector", c1b.t[:], G2a.t[:], bcp(bdown), ALU.mult, [G2a, bdown], [c1b])
    TT("vector", c1b.t[:], c1b.t[:], bcp(ln1b), ALU.add, [c1b, ln1b], [c1b])
    TT("vector", gA2.t[:], A2.t[:], bcp(ln1g), ALU.mult, [A2, ln1g], [gA2])
    TT("vector", bA2.t[:], A2.t[:], bcp(ln1b), ALU.mult, [A2, ln1b], [bA2])
    TT("vector", bA2.t[:], bA2.t[:], modT.t[:, 24:32, :], ALU.add, [bA2, modT], [bA2])
    TS("vector", g2t.t[:], A1.t[:], 0.0, None, ALU.mult, None, [A1], [g2t])
    TT("vector", b2t.t[:], g2t.t[:], bcp(ln2b), ALU.add, [g2t, ln2b], [b2t])
    TT("vector", g2t.t[:], g2t.t[:], bcp(ln2g), ALU.add, [g2t, ln2g], [g2t])

    xst = [R0.alloc("xst%d" % i, [128, NT]) for i in range(2)]
    tmpA = R0.alloc("tmpA", [128, 512])
    for kc in range(KC):
        xs = xst[kc % 2]
        DMA(xs.t[:, 0:2048], xT[kc, :, 0:2048], writes=[xs])
        DMA(xs.t[:, 2048:NT], xT[kc, :, 2048:NT], writes=[xs])
        for t0 in range(0, S0, 1024):
            affine("gpsimd", U(kc, t0, 1024), xs.t[:, t0:t0 + 1024], A1, kc, modT, kc, False, [xs], [uTm, uTh])
        affine("gpsimd", U(kc, S0, 128), xs.t[:, S0:NT], A1, kc, modT, kc, True, [xs], [uTm], tmp=tmpA)
    P.barrier()
    if kstop <= 1:
        return finish()

    R2 = Region(big, o_hi, ARENA_WORDS)
    rings = Rings(R2, 2, 1)
    rings_c = Rings(R2, 0, 2)
    rings_c.stage = rings.stage
    for s_ in rings.st:
        rings.subs[id(s_)] = (Buf(s_.name + "K", s_.t), Buf(s_.name + "V", s_.t))
    cslots = [(Buf(d_.name + "K", d_.t), Buf(d_.name + "V", d_.t)) for d_ in rings_c.bf]
    QT = R2.alloc("QT", [128, NO], BF16)
    QSs = [R2.alloc("QSs%d" % i, [128, 128], BF16) for i in range(2)]
    KT = R2.alloc("KT", [128, NT], BF16)
    Vb = R2.alloc("Vb", [128, 33, 128], BF16)
    Vs = R2.alloc("Vs", [128, 128], BF16)
    numacc = R2.alloc("numacc", [128, MAIN]); denacc = R2.alloc("denacc", [128, MAIN])
    mk = R2.alloc("mk", [128, 256]); mkh = R2.alloc("mkh", [128, 256])
    smk_ = [R2.alloc("smk%d" % i, [128, 128]) for i in range(2)]; nmk = R2.alloc("nmk", [128, 128])
    Eb = [R2.alloc("Eb%d" % i, [128, 256], BF16) for i in range(2)]
    Pb = [R2.alloc("Pb%d" % i, [128, 256], BF16) for i in range(2)]
    kst = [R2.alloc("kst%d" % i, [128, 512]) for i in range(2)]
    kst_i = [0]
    sE = [R2.alloc("sE%d" % i, [128, 128]) for i in range(2)]
    sP = [R2.alloc("sP%d" % i, [128, 128], BF16) for i in range(2)]
    sPs = [R2.alloc("sPs%d" % i, [128, 8]) for i in range(2)]
    tkv = kst[0]
    rden = kst[1]
    cnt = [0]
    import collections
    nsE = R2.alloc("nsE", [128, 128])
    nsP = R2.alloc("nsP", [128, 128], BF16)
    plist = []
    pidx = {"dma": 0, "cast": 0, "p1": 0, "p2": 0}
    pcnt = [0]

    def pieces_begin(new):
        assert pidx["p1"] == len(plist) and pidx["p2"] == len(plist)
        plist[:] = new
        for k_ in pidx:
            pidx[k_] = 0

    def prologue():
        n = len(plist)
        while pidx["dma"] < min(2, n):
            plist[pidx["dma"]]["dma"](); pidx["dma"] += 1

    def pump(drain=False):
        while True:
            n = len(plist)
            j = pidx["p1"]
            if j >= n:
                if pidx["p2"] < n:
                    plist[pidx["p2"]]["p2"](); pidx["p2"] += 1
                return
            while pidx["dma"] < min(j + 2, n):
                plist[pidx["dma"]]["dma"](); pidx["dma"] += 1
            if pidx["cast"] <= j:
                plist[j]["cast"](); pidx["cast"] = j + 1
            plist[j]["p1"](); pidx["p1"] = j + 1
            if j >= 1:
                plist[j - 1]["p2"](); pidx["p2"] = j
            if j + 1 < n:
                plist[j + 1]["cast"](); pidx["cast"] = j + 2
            if j + 2 < n:
                plist[j + 2]["dma"](); pidx["dma"] = j + 3
            if not drain:
                return
    saccs = [psacc, psb[5]]

    items = []
    ps_n[0] = 5

    def mk_tkv(kv, cbase, g3):
        def loader():
            return rings.load([(0, KC, 512, wsrc(w_in, cbase + g3 * 512, 512))])

        def body(wb):
            pb = PS()
            MMG([(pb.t[:, :], U(kc, S0, 128), wb.t[:, kc * 512:(kc + 1) * 512], kc == 0, kc == KC - 1)
                 for kc in range(KC)], [wb, uTm], [pb])
            COPY("scalar", tkv.t[:], pb.t[:, :], [pb], [tkv])
            w = GROUPS[g3][0]
            for b in range(nsc):
                DMA(kvo[g3][b, w - 8:w, kv * 512:(kv + 1) * 512], tkv.t[b * 8:(b + 1) * 8, :],
                    reads=[tkv])
        return (loader, body)
    for kv, cbase in ((0, C_K), (1, C_V)):
        for g3 in range(3):
            items.append(mk_tkv(kv, cbase, g3))

    sacc_first = [True]

    def mk_gh(h, g):
        win, dil = GROUPS[g]
        gh = g * 4 + h
        sacc = saccs[h % 2]
        par = (h * 3 + g) % 2
        smk = smk_[par]
        pcount = [0]

        def pstep():
            pump()

        def loader():
            return load_blocks(rings, w_in, [cb + g * 512 + h * 128 for cb in (C_Q, C_K, C_V)])

        def body(wb):
            if g == 0:
                sacc_first[0] = True
            prologue()
            DMA(mk.t[:], masks_d[gh], writes=[mk])
            DMA(smk.t[:], smask_d[gh], writes=[smk])
            DMA(nmk.t[:], nmask_d[gh], writes=[nmk])
            COPY("gpsimd", mkh.t[:, 128:256], mk.t[:, 128:256], [mk], [mkh])
            TS("vector", mkh.t[:, 0:128], mk.t[:, 0:128], flag.t[:, 0:1], None, ALU.mult, None, [mk, flag], [mkh])
            for (t0, wd) in MAIN_TILES + [SAMP_TILE]:
                pb = proj_fm(wb, 384, 0, t0, wd)
                COPY("scalar", QT.t[:, t0 - M0:t0 - M0 + wd], pb.t[:, 0:wd], [pb], [QT])
                pstep()
            ktiles = [(t0, wd) for (t0, wd) in HALO_TILES if t0 + wd > HALO - win] + MAIN_TILES + [SAMP_TILE]
            for (t0, wd) in ktiles:
                pb = proj_fm(wb, 384, 128, t0, wd)
                COPY("scalar", KT.t[:, t0:t0 + wd], pb.t[:, 0:wd], [pb], [KT])
                if M0 <= t0 < S0 and t0 + wd > S0 - win:
                    lo = max(t0, S0 - win)
                    ks = kst[kst_i[0] % 2]; kst_i[0] += 1
                    COPY("vector", ks.t[:, 0:t0 + wd - lo], pb.t[:, lo - t0:wd], [pb], [ks])
                    DMA(kTo[g][h, :, lo - (S0 - win): t0 + wd - (S0 - win)], ks.t[:, 0:t0 + wd - lo], reads=[ks])
                pstep()
            nmb = 1 + MAIN // (128 * dil)
            base = HALO - win
            blocks = [(r, mb) for r in range(dil) for mb in range(nmb)]
            for q0 in range(0, len(blocks), 4):
                grp = blocks[q0:q0 + 4]
                pb = PS()
                mms = []
                for bi, (r, mb) in enumerate(grp):
                    tstart = base + mb * 128 * dil + r
                    for kc in range(KC):
                        mms.append((pb.t[:, bi * 128:(bi + 1) * 128], U(kc, tstart, 128, dil),
                                    wb.t[:, kc * 384 + 256: kc * 384 + 384], kc == 0, kc == KC - 1))
                MMG(mms, [wb, uTm, uTh], [pb])
                COPY("scalar", Vb.t[:, q0:q0 + len(grp), :],
                     pb.t[:, 0:len(grp) * 128].rearrange("p (a b) -> p a b", b=128), [pb], [Vb])
                need = [(bi, r, mb) for bi, (r, mb) in enumerate(grp)
                        if mb >= 1 and (mb * 128 * dil + base) + 127 * dil + r >= S0 - win]
                if need:
                    ks = kst[kst_i[0] % 2]; kst_i[0] += 1
                    COPY("vector", ks.t[:, 0:len(grp) * 128], pb.t[:, 0:len(grp) * 128], [pb], [ks])
                    for (bi, r, mb) in need:
                        ts_ = base + mb * 128 * dil + r - (S0 - win)
                        DMA(vo[g][ts_: ts_ + 127 * dil + 1: dil, h, :], ks.t[:, bi * 128:(bi + 1) * 128], reads=[ks])
                pstep()
            pb = PS()
            MMG([(pb.t[:, 0:128], U(kc, S0, 128), wb.t[:, kc * 384 + 256: kc * 384 + 384], kc == 0, kc == KC - 1)
                 for kc in range(KC)], [wb, uTm], [pb])
            COPY("scalar", Vs.t[:], pb.t[:, 0:128], [pb], [Vs])
            for r in range(dil):
                for mb in range(1, nmb):
                    qstart = (mb - 1) * 128 * dil + r
                    qcols = slice(qstart, qstart + 127 * dil + 1, dil)
                    kprev = base + (mb - 1) * 128 * dil + r
                    kown = base + mb * 128 * dil + r
                    bprev = r * nmb + mb - 1
                    i2 = cnt[0] % 2; cnt[0] += 1
                    pS = PS()
                    MMG([(pS.t[:, 0:128], KT.t[:, kprev: kprev + 127 * dil + 1: dil], QT.t[:, qcols], True, True),
                         (pS.t[:, 128:256], KT.t[:, kown: kown + 127 * dil + 1: dil], QT.t[:, qcols], False, True)],
                        [KT, QT], [pS])
                    ACT(Eb[i2].t[:], pS.t[:, 0:256], AF.Exp, [pS], [Eb[i2]], scale=QSCALE)
                    mm_ = mkh if mb == 1 else mk
                    TT("gpsimd" if i2 else "vector", Pb[i2].t[:], Eb[i2].t[:], mm_.t[:], ALU.mult, [Eb[i2], mm_], [Pb[i2]])
                    pO = PS()
                    MMG([(pO.t[:, 0:128], Vb.t[:, bprev, :], Pb[i2].t[:, 0:128], True, False),
                         (pO.t[:, 0:128], Vb.t[:, bprev + 1, :], Pb[i2].t[:, 128:256], False, True),
                         (pO.t[:, 128:256], ones_bf.t[:], Pb[i2].t[:, 0:128], False, False),
                         (pO.t[:, 128:256], ones_bf.t[:], Pb[i2].t[:, 128:256], False, True)],
                        [Vb, Pb[i2], ones_bf], [pO])
                    if g == 0:
                        COPY("vector", numacc.t[:, qcols], pO.t[:, 0:128], [pO], [numacc])
                        COPY("vector", denacc.t[:, qcols], pO.t[:, 128:256], [pO], [denacc])
                    else:
                        TT("vector", numacc.t[:, qcols], pO.t[:, 0:128], numacc.t[:, qcols], ALU.add, [pO, numacc], [numacc])
                        TT("vector", denacc.t[:, qcols], pO.t[:, 128:256], denacc.t[:, qcols], ALU.add, [pO, denacc], [denacc])
            QS = QT.t[:, MAIN:NO]
            pS = PS()
            MMG([(pS.t[:, 0:128], KT.t[:, S0:NT], QS, True, True)], [KT, QT], [pS])
            ACT(nsE.t[:], pS.t[:, 0:128], AF.Exp, [pS], [nsE], scale=QSCALE)
            TT("gpsimd", nsP.t[:], nsE.t[:], nmk.t[:], ALU.mult, [nsE, nmk], [nsP])
            MMG([(sacc.t[:, 0:128], Vs.t[:], nsP.t[:], sacc_first[0], False),
                 (sacc.t[:, 128:256], ones_bf.t[:], nsP.t[:], False, False)], [Vs, nsP, ones_bf], [sacc])
            sacc_first[0] = False
            pump(drain=True)
            if g == 0 and h > 0:
                fin_sample(h - 1)
            COPY("vector", QSs[par].t[:], QT.t[:, MAIN:NO], [QT], [QSs[par]])
            pieces_begin([mk_cache(h, g, b) for b in range(nsc)])
            if g == 2:
                fin_prompt(h)
        return (loader, body)

    def mk_cache(h, g, b):
        win, dil = GROUPS[g]
        nb = win // 128
        gh = g * 4 + h
        sacc = saccs[h % 2]
        par = (h * 3 + g) % 2
        smk = smk_[par]

        st_ = {}

        def dma():
            s_ = rings.stage()
            sK, sV = rings.subs[id(s_)]
            st_["s"] = (sK, sV)
            DMA(sK.t[:, 0:win].rearrange("p (a b) -> p a b", b=win), kTc[g][b, h].unsqueeze(1), writes=[sK])
            DMA(sV.t[:, 2048:2048 + nb * 128].rearrange("p (a b) -> p a b", b=128),
                kvc[g][b, :, 512 + h * 128: 512 + (h + 1) * 128].rearrange("(a p) d -> p a d", p=128), writes=[sV])

        def cast():
            sK, sV = st_["s"]
            dK, dV = cslots[rings_c.j % len(cslots)]
            rings_c.j += 1
            st_["d"] = (dK, dV)
            qk, qv = ("vector", "scalar") if (rings_c.j % 2 == 0) else ("scalar", "vector")
            COPY(qk, dK.t[:, 0:win], sK.t[:, 0:win], [sK], [dK])
            COPY(qv, dV.t[:, 2048:2048 + nb * 128], sV.t[:, 2048:2048 + nb * 128], [sV], [dV])

        def part1():
            cb_ = st_["d"][0]
            QSb = QSs[par]
            QS = QSb.t[:]
            i2 = pcnt[0] % 2; pcnt[0] += 1
            st_["i2"] = i2
            pS = PS()
            MMG([(pS.t[:, blk * 8:(blk + 1) * 8], cb_.t[:, blk * 128:(blk + 1) * 128], QS[:, b * 8:(b + 1) * 8],
                  blk == 0, True) for blk in range(nb)], [cb_, QSb], [pS])
            ACT(sE[i2].t[:, 0:nb * 8], pS.t[:, 0:nb * 8], AF.Exp, [pS], [sE[i2]], scale=QSCALE)
            TT("vector", sP[i2].t[:, 0:nb * 8], sE[i2].t[:, 0:nb * 8], smk.t[:, 0:nb * 8], ALU.mult,
               [sE[i2], smk], [sP[i2]])
            if nb > 1:
                P.op("vector", (lambda a, bb: (lambda e: e.tensor_reduce(
                    out=a, in_=bb, axis=mybir.AxisListType.X, op=ALU.add)))(
                    sPs[i2].t[:, 0:8], sP[i2].t[:, 0:nb * 8].rearrange("p (a t) -> p t a", t=8)),
                    [sP[i2]], [sPs[i2]])

        def part2():
            cb_ = st_["d"][1]
            i2 = st_["i2"]
            if nb > 1:
                srcs, onesx = sPs[i2], ones_f
            else:
                srcs, onesx = sP[i2], ones_bf
            sden = srcs.t[:, 0:8]
            mms = [(sacc.t[:, b * 8:(b + 1) * 8], cb_.t[:, 2048 + blk * 128: 2048 + (blk + 1) * 128],
                    sP[i2].t[:, blk * 8:(blk + 1) * 8], False, False) for blk in range(nb)]
            mms.append((sacc.t[:, 128 + b * 8:128 + (b + 1) * 8], onesx.t[:], sden, False, False))
            MMG(mms, [cb_, sP[i2], srcs, onesx], [sacc])
        return {"dma": dma, "cast": cast, "p1": part1, "p2": part2}

    def fin_prompt(h):
        for c0 in range(0, MAIN, 512):
            RECIP(rden.t[:], denacc.t[:, c0:c0 + 512], [denacc], [rden])
            TT("vector", attnT.t[:, h, c0:c0 + 512], numacc.t[:, c0:c0 + 512], rden.t[:], ALU.mult,
               [numacc, rden], [attnT])

    def fin_sample(h):
        sacc = saccs[h % 2]
        RECIP(rden.t[:, 0:128], sacc.t[:, 128:256], [sacc], [rden])
        TT("vector", attnT.t[:, h, MAIN:NO], sacc.t[:, 0:128], rden.t[:, 0:128], ALU.mult, [sacc, rden], [attnT])

    for h in range(4):
        for g in range(3):
            items.append(mk_gh(h, g))
    run_stream(items, 0)
    pump(drain=True)
    fin_sample(3)
    ps_n[0] = 6
    P.barrier()
    if kstop <= 2:
        return finish()

    R3 = Region(big, o_hi, ARENA_WORDS)
    hgT = R3.alloc("hgT", [128, 8, NO], BF16)
    o_after_hg = R3.top
    wbf_hg = R3.alloc("wbf_hg", [128, 4096], BF16)
    Kt_ = [R3.alloc("Kt%d" % i, [128, 512], BF16) for i in range(2)]
    Kt2_ = [R3.alloc("Kt2%d" % i, [128, 512], BF16) for i in range(2)]
    Qt_ = [R3.alloc("Qt%d" % i, [128, 512], BF16) for i in range(2)]
    sgT_ = [R3.alloc("sgT%d" % i, [128, 512]) for i in range(2)]
    Vh_ = [R3.alloc("Vh%d" % i, [128, 4, 128], BF16) for i in range(2)]
    Sall = R3.alloc("Sall", [128, 8, 128], BF16)
    Sf = [R3.alloc("Sf%d" % i, [128, 128]) for i in range(2)]
    S0f = R3.alloc("S0f", [128, 16, 128])
    S0b = R3.alloc("S0b", [128, 16, 128], BF16)
    Ktm4_ = [R3.alloc("Ktm4%d" % i, [128, 512], BF16) for i in range(2)]
    Ktmm = [R3.alloc("Ktmm%d" % i, [128, 128], BF16) for i in range(2)]
    hb_base = R3.top
    HB = [[R3.alloc("hB%d_%d" % (i, k), [128, 512]) for k in range(6)] for i in range(2)]
    hg_stage = big[:, hb_base:hb_base + 4096]
    hg_stage_bufs = HB[0] + HB[1][:2]

    def load_hg(hh):
        for bi, cb in enumerate((C_QH, C_FH, C_IH, C_OG)):
            DMA(hg_stage.rearrange("p (a b) -> p a b", b=512)[:, :, bi * 128:(bi + 1) * 128],
                wsrc(w_in, cb + hh * 128, 128), writes=hg_stage_bufs)
        COPY("vector" if hh % 2 else "scalar", wbf_hg.t[:], hg_stage, hg_stage_bufs, [wbf_hg])
        return wbf_hg
    dec_ = [R3.alloc("h_dec%d" % i, [128, 16]) for i in range(2)]
    Am4_ = [R3.alloc("Am4%d" % i, [128, 512], BF16) for i in range(2)]
    osq = R3.alloc("osq", [128, 512]); rstd = R3.alloc("rstd", [128, 512]); otmp = R3.alloc("otmp", [128, 512])
    tcount = [0]
    bcount = [0]

    psB = [psb[4], psb[5], psacc]
    psB_i = [0]

    def PSB():
        b = psB[psB_i[0] % 3]
        psB_i[0] += 1
        return b

    def proj_bank(pb, wb, row, woff, t0, wd):
        MMG([(pb.t[:, 0:wd], wb.t[:, kc * row + woff: kc * row + woff + 128], U(kc, t0, wd), kc == 0, kc == KC - 1)
             for kc in range(KC)], [wb, uTm, uTh], [pb])
        return pb

    def hg_head(hh, wb):
        DMA(S0f.t[:], state_d[:, hh].rearrange("b k v -> k b v"), writes=[S0f])
        issue_copies((3 * nsc + 7) // 8)
        COPY("gpsimd", S0b.t[:], S0f.t[:], [S0f], [S0b])
        st_ = {"sidx": 0}
        MEMSET("vector", Sf[0].t[:], 0.0, [Sf[0]])
        tiles = HALO_TILES + MAIN_TILES + [SAMP_TILE]

        def ctx(ti):
            t0, wd = tiles[ti]
            c = dict(t0=t0, wd=wd, halo=t0 < M0, samp=t0 >= S0, par=ti % 2)
            c["C"] = 8 if c["samp"] else 64
            c["nchk"] = wd // c["C"]
            c["nblk"] = wd // 128
            c["o0"] = t0 - M0
            return c

        def stageA(ti):
            c = ctx(ti)
            t0, wd, par, samp, halo = c["t0"], c["wd"], c["par"], c["samp"], c["halo"]
            B1, B2, B3, B4, B5, B6 = HB[par]
            Vh, sgT = Vh_[par], sgT_[par]
            pF = proj_bank(psb[0], wb, 512, 128, t0, wd)
            pV = psb[1]
            mms = []
            for bi in range(c["nblk"]):
                for kc in range(KC):
                    mms.append((pV.t[:, bi * 128:(bi + 1) * 128], U(kc, t0 + bi * 128, 128),
                                wb.t[:, kc * 512 + 256: kc * 512 + 384], kc == 0, kc == KC - 1))
            MMG(mms, [wb, uTm, uTh], [pV])
            ACT(B1.t[:, 0:wd], pF.t[:, 0:wd], AF.Exp, [pF], [B1], scale=-1.0)
            ACT(B2.t[:, 0:wd], B1.t[:, 0:wd], AF.Ln, [B1, lbT], [B2], scale=lbT.t[:, hh:hh + 1], bias=1.0)
            ACT(B3.t[:, 0:wd], B1.t[:, 0:wd], AF.Ln, [B1], [B3], bias=1.0)
            COPY("scalar", Vh.t[:, 0:c["nblk"], :], pV.t[:, 0:wd].rearrange("p (a b) -> p a b", b=128), [pV], [Vh])
            TT("gpsimd", B2.t[:, 0:wd], B2.t[:, 0:wd], B3.t[:, 0:wd], ALU.subtract, [B2, B3], [B2])
            P.op("vector", (lambda o_, d0, d1: (lambda e: e.tensor_tensor_scan(
                out=o_, data0=d0, data1=d1, initial=0.0, op0=ALU.mult, op1=ALU.add)))(
                B4.t[:, 0:wd], reset.t[:, 1 if samp else 0, 0:wd], B2.t[:, 0:wd]), [reset, B2], [B4])
            if not halo:
                pQ = proj_bank(psb[2], wb, 512, 0, t0, wd)
                pG = proj_bank(psb[3], wb, 512, 384, t0, wd)
                ACT(B5.t[:, 0:wd], pQ.t[:, 0:wd], AF.Exp, [pQ], [B5], scale=-1.0)
                ACT(B5.t[:, 0:wd], B5.t[:, 0:wd], AF.Ln, [B5], [B5], bias=1.0)
                ACT(B5.t[:, 0:wd], B5.t[:, 0:wd], AF.Exp, [B5], [B5], scale=-1.0)
                TT("vector", B5.t[:, 0:wd], pQ.t[:, 0:wd], B5.t[:, 0:wd], ALU.mult, [pQ, B5], [B5])
                ACT(B6.t[:, 0:wd], pG.t[:, 0:wd], AF.Exp, [pG], [B6], scale=-1.0)
                ACT(B6.t[:, 0:wd], B6.t[:, 0:wd], AF.Ln, [B6], [B6], bias=1.0)
                ACT(B6.t[:, 0:wd], B6.t[:, 0:wd], AF.Exp, [B6], [B6], scale=-1.0)
                STT(sgT.t[:, 0:wd], pG.t[:, 0:wd], hgw.t[:, 0:1], B6.t[:, 0:wd], ALU.mult, ALU.mult,
                    [pG, hgw, B6], [sgT])

        def stageB(ti):
            c = ctx(ti)
            wd, par, C, nchk, halo = c["wd"], c["par"], c["C"], c["nchk"], c["halo"]
            B1, B2, B3, B4, B5, B6 = HB[par]
            Kt, Kt2, Qt, dec = Kt_[par], Kt2_[par], Qt_[par], dec_[par]
            ACT(B1.t[:, 0:wd], B3.t[:, 0:wd], AF.Exp, [B3], [B1], scale=-1.0)
            ACT(B6.t[:, 0:wd], B4.t[:, 0:wd], AF.Exp, [B4], [B6], scale=-1.0)
            ACT(dec.t[:, 0:nchk], B4.t[:, C - 1:wd:C], AF.Exp, [B4], [dec])
            if not halo:
                ACT(B2.t[:, 0:wd], B4.t[:, 0:wd], AF.Exp, [B4], [B2])
            TS("vector", B3.t[:, 0:wd], B1.t[:, 0:wd], nlb1.t[:, hh:hh + 1], lb1.t[:, hh:hh + 1], ALU.mult, ALU.add,
               [B1, nlb1, lb1], [B3])
            TT("gpsimd", Kt.t[:, 0:wd], B3.t[:, 0:wd], B6.t[:, 0:wd], ALU.mult, [B3, B6], [Kt])
            TT("gpsimd", Kt2.t[:, 0:wd].rearrange("p (c t) -> p c t", t=C), Kt.t[:, 0:wd].rearrange("p (c t) -> p c t", t=C),
               dec.t[:, 0:nchk].unsqueeze(2).broadcast_to([128, nchk, C]), ALU.mult, [Kt, dec], [Kt2])
            if not halo:
                TT("vector", Qt.t[:, 0:wd], B5.t[:, 0:wd], B2.t[:, 0:wd], ALU.mult, [B5, B2], [Qt])

        def stageC(ti):
            c = ctx(ti)
            t0, wd, par, C, samp, halo, nblk, o0 = c["t0"], c["wd"], c["par"], c["C"], c["samp"], c["halo"], c["nblk"], c["o0"]
            Kt, Kt2, Qt, sgT, Vh, dec = Kt_[par], Kt2_[par], Qt_[par], sgT_[par], Vh_[par], dec_[par]
            Ktm4, Am4 = Ktm4_[par], Am4_[par]
            sidx = st_["sidx"]
            for bl in range(nblk):
                TRANSPOSE(pst.t[:, bl * 128:(bl + 1) * 128], Kt2.t[:, bl * 128:(bl + 1) * 128], ident_bf.t[:],
                          [Kt2, ident_bf], [pst])
            COPY("scalar", Ktm4.t[:, 0:wd], pst.t[:, 0:wd], [pst], [Ktm4])
            if not samp:
                pDs = []
                for cc in range(2):
                    pD = PSB()
                    MMG([(pD.t[:, bl * 128:(bl + 1) * 128], Ktm4.t[cc * 64:(cc + 1) * 64, bl * 128:(bl + 1) * 128],
                          Vh.t[cc * 64:(cc + 1) * 64, bl, :], bl == 0, True) for bl in range(nblk)], [Ktm4, Vh], [pD])
                    pDs.append(pD)
                for ch in range(2 * nblk):
                    if not halo:
                        COPY("gpsimd", Sall.t[:, ch, :], Sf[sidx].t[:], [Sf[sidx]], [Sall])
                    bl, cc = ch // 2, ch % 2
                    pD = pDs[cc]
                    STT(Sf[1 - sidx].t[:], Sf[sidx].t[:], dec.t[:, ch:ch + 1], pD.t[:, bl * 128:(bl + 1) * 128], ALU.mult, ALU.add,
                        [Sf[sidx], dec, pD], [Sf[1 - sidx]])
                    sidx = 1 - sidx
            else:
                for q4 in range(4):
                    pD = PSB()
                    for q in range(4):
                        cc = q4 * 4 + q
                        i2 = cc % 2
                        TS("vector" if cc % 2 else "gpsimd", Ktmm[i2].t[:], Ktm4.t[:, 0:128], oh.t[:, cc:cc + 1], None, ALU.mult, None,
                           [Ktm4, oh], [Ktmm[i2]])
                        MMG([(pD.t[:, q * 128:(q + 1) * 128], Ktmm[i2].t[:], Vh.t[:, 0, :], q == 0, True)], [Ktmm[i2], Vh], [pD])
                    for q in range(4):
                        cc = q4 * 4 + q
                        STT(S0f.t[:, cc, :], S0f.t[:, cc, :], dec.t[:, cc:cc + 1], pD.t[:, q * 128:(q + 1) * 128], ALU.mult, ALU.add,
                            [S0f, dec, pD], [S0f])
            if t0 + wd == M0:
                TS("vector", Sf[1 - sidx].t[:], Sf[sidx].t[:], flag.t[:, 0:1], None, ALU.mult, None,
                   [Sf[sidx], flag], [Sf[1 - sidx]])
                sidx = 1 - sidx
            if t0 + wd == S0:
                DMA(s_out[hh], Sf[sidx].t[:], reads=[Sf[sidx]])
            if samp:
                DMA(state_o[:, hh].rearrange("b k v -> k b v"), S0f.t[:], reads=[S0f])
            st_["sidx"] = sidx
            if halo:
                return
            pA = PSB()
            MMG([(pA.t[:, bl * 128:(bl + 1) * 128], Kt.t[:, bl * 128:(bl + 1) * 128], Qt.t[:, bl * 128:(bl + 1) * 128],
                  bl == 0, True) for bl in range(nblk)], [Kt, Qt], [pA])
            TT("vector", Am4.t[:, 0:wd].rearrange("p (a b) -> p a b", b=128), pA.t[:, 0:wd].rearrange("p (a b) -> p a b", b=128),
               hmask.t[:, (1 if samp else 0):(2 if samp else 1), :].broadcast_to([128, nblk, 128]), ALU.mult, [pA, hmask], [Am4])
            pO = PSB()
            mms = []
            for bl in range(nblk):
                c0 = bl * 128
                mms.append((pO.t[:, c0:c0 + 128], Vh.t[:, bl, :], Am4.t[:, c0:c0 + 128], bl == 0, False))
                if samp:
                    for cc in range(16):
                        mms.append((pO.t[:, cc * 8:(cc + 1) * 8], S0b.t[:, cc, :], Qt.t[:, cc * 8:(cc + 1) * 8], False, cc == 15))
                else:
                    for cc in range(2):
                        mms.append((pO.t[:, c0 + cc * 64:c0 + (cc + 1) * 64], Sall.t[:, bl * 2 + cc, :],
                                    Qt.t[:, c0 + cc * 64:c0 + (cc + 1) * 64], False, cc == 1))
            MMG(mms, [Vh, Am4, S0b, Sall, Qt], [pO])
            ACT(osq.t[:, 0:wd], pO.t[:, 0:wd], AF.Square, [pO], [osq])
            pM = PSB()
            MMG([(pM.t[:, 0:wd], ones_r.t[:], osq.t[:, 0:wd], True, True)], [ones_r, osq], [pM])
            ACT(rstd.t[:, 0:wd], pM.t[:, 0:wd], AF.Ln, [pM], [rstd], bias=RMS_EPS)
            ACT(rstd.t[:, 0:wd], rstd.t[:, 0:wd], AF.Exp, [rstd], [rstd], scale=-0.5)
            TT("vector", otmp.t[:, 0:wd], pO.t[:, 0:wd], rstd.t[:, 0:wd], ALU.mult, [pO, rstd], [otmp])
            TT("gpsimd", hgT.t[:, hh, o0:o0 + wd], otmp.t[:, 0:wd], sgT.t[:, 0:wd], ALU.mult, [otmp, sgT], [hgT])

        n_t = len(tiles)
        stageA(0)
        for ti in range(n_t):
            stageB(ti)
            if ti + 1 < n_t:
                stageA(ti + 1)
            stageC(ti)

    run_stream([((lambda hh=hh: load_hg(hh)), (lambda wb, hh=hh: hg_head(hh, wb))) for hh in range(8)], 0)
    issue_copies(len(copy_jobs))
    P.barrier()
    if kstop <= 3:
        return finish()

    mg_m = Region(big, o_uTh, o_hi).alloc("mg_m", [128, KC, MAIN], BF16)

    def MG(c8, o0, wd):
        return mg_m.t[:, c8, o0:o0 + wd] if o0 < MAIN else mg_s.t[:, c8, 0:wd]

    R4 = Region(big, o_after_hg, ARENA_WORDS)
    rings = Rings(R4, 2, 3)
    ea_ = [R4.alloc("ea%d" % i, [128, 512]) for i in range(2)]
    eb_ = [R4.alloc("eb%d" % i, [128, 512]) for i in range(2)]
    m1_ = [R4.alloc("m1%d" % i, [128, 512]) for i in range(2)]
    p3cnt = [0]

    def p3a_load(j):
        s_ = rings.stage()
        d_ = rings.bf[rings.j % len(rings.bf)]
        rings.j += 1
        for bi, (w_, c_) in enumerate(((w_in, C_GA + j * 128), (w_in, C_GB + j * 128), (w_bb, j * 128))):
            DMA(s_.t[:, 0:3072].rearrange("p (a b) -> p a b", b=384)[:, :, bi * 128:(bi + 1) * 128],
                wsrc(w_, c_, 128), writes=[s_])
        DMA(s_.t[:, 3072:3584].rearrange("p (a b) -> p a b", b=128), wsrc(w_ba, j * 128, 128, kc=4), writes=[s_])
        COPY("scalar" if j % 2 else "vector", d_.t[:, 0:3584], s_.t[:, 0:3584], [s_], [d_])
        return d_

    def p3a_body(j, wb):
        for (t0, wd) in MAIN_TILES + [SAMP_TILE]:
            o0 = t0 - M0
            pa = proj_fm(wb, 384, 0, t0, wd)
            pbb = proj_fm(wb, 384, 128, t0, wd)
            pBA = PS()
            MMG([(pBA.t[:, 0:wd], wb.t[:, 3072 + c4 * 128: 3072 + (c4 + 1) * 128], attnT.t[:, c4, o0:o0 + wd],
                  c4 == 0, c4 == 3) for c4 in range(4)], [wb, attnT], [pBA])
            pBB = PS()
            MMG([(pBB.t[:, 0:wd], wb.t[:, c8 * 384 + 256: c8 * 384 + 384], hgT.t[:, c8, o0:o0 + wd],
                  c8 == 0, c8 == 7) for c8 in range(8)], [wb, hgT], [pBB])
            i2 = p3cnt[0] % 2
            p3cnt[0] += 1
            ea, eb, m1 = ea_[i2], eb_[i2], m1_[i2]
            ACT(ea.t[:, 0:wd], pa.t[:, 0:wd], AF.Exp, [pa], [ea], scale=-1.0)
            ACT(ea.t[:, 0:wd], ea.t[:, 0:wd], AF.Ln, [ea], [ea], bias=1.0)
            ACT(ea.t[:, 0:wd], ea.t[:, 0:wd], AF.Exp, [ea], [ea], scale=-1.0)
            TT("vector", m1.t[:, 0:wd], pBA.t[:, 0:wd], ea.t[:, 0:wd], ALU.mult, [pBA, ea], [m1])
            ACT(eb.t[:, 0:wd], pbb.t[:, 0:wd], AF.Exp, [pbb], [eb], scale=-1.0)
            ACT(eb.t[:, 0:wd], eb.t[:, 0:wd], AF.Ln, [eb], [eb], bias=1.0)
            ACT(eb.t[:, 0:wd], eb.t[:, 0:wd], AF.Exp, [eb], [eb], scale=-1.0)
            TT("vector", eb.t[:, 0:wd], pBB.t[:, 0:wd], eb.t[:, 0:wd], ALU.mult, [pBB, eb], [eb])
            TT("gpsimd", MG(j, o0, wd), m1.t[:, 0:wd], eb.t[:, 0:wd], ALU.add, [m1, eb], [mg_m, mg_s])

    run_stream([((lambda j=j: p3a_load(j)), (lambda wb, j=j: p3a_body(j, wb))) for j in range(KC)], 2)
    P.barrier()
    if kstop <= 4:
        return finish()

    RL = Region(big, o_uTm, o_uTh)
    RH = Region(big, o_hi, ARENA_WORDS)
    hid = RL.alloc("hid", [128, 32, 512], BF16)
    h1 = RL.alloc("h1", [128, KC, 512])
    x1 = RH.alloc("x1", [128, KC, 512])
    u2 = RH.alloc("u2", [128, KC, 512], BF16)
    rings = Rings(RH, 2, 3)
    xres = [RH.alloc("xres%d" % i, [128, 512]) for i in range(2)]
    sq_ = [RH.alloc("sq%d" % i, [128, 512]) for i in range(2)]
    m2_ = RH.alloc("m2", [128, 512]); rs_ = RH.alloc("rs", [128, 512])
    tA = RH.alloc("tA", [128, 512]); tB = RH.alloc("tB", [128, 512]); hs_ = RH.alloc("hs", [128, 512])
    yst = [RH.alloc("yst%d" % i, [128, 512]) for i in range(2)]

    def layer_norm(src, wd, emit_j):
        pM = PS()
        MMG([(pM.t[:, 0:wd], ones_m.t[:], src.t[:, j, 0:wd], j == 0, j == KC - 1) for j in range(KC)], [ones_m, src], [pM])
        pV = PS()
        for j in range(KC):
            s2 = sq_[j % 2]
            ACT(s2.t[:, 0:wd], src.t[:, j, 0:wd], AF.Square, [src], [s2])
            MMG([(pV.t[:, 0:wd], ones_m.t[:], s2.t[:, 0:wd], j == 0, j == KC - 1)], [ones_m, s2], [pV])
        ACT(m2_.t[:, 0:wd], pM.t[:, 0:wd], AF.Square, [pM], [m2_])
        TT("vector", m2_.t[:, 0:wd], pV.t[:, 0:wd], m2_.t[:, 0:wd], ALU.subtract, [pV, m2_], [m2_])
        ACT(rs_.t[:, 0:wd], m2_.t[:, 0:wd], AF.Ln, [m2_], [rs_], bias=LN_EPS)
        ACT(rs_.t[:, 0:wd], rs_.t[:, 0:wd], AF.Exp, [rs_], [rs_], scale=-0.5)
        for j in range(KC):
            TT("vector", tA.t[:, 0:wd], src.t[:, j, 0:wd], pM.t[:, 0:wd], ALU.subtract, [src, pM], [tA])
            TT("gpsimd", tB.t[:, 0:wd], tA.t[:, 0:wd], rs_.t[:, 0:wd], ALU.mult, [tA, rs_], [tB])
            emit_j(j)

    items = []

    def p3b_tile(t0, wd):
        samp = t0 >= S0
        o0 = t0 - M0

        def emit1(j):
            affine("gpsimd", x1.t[:, j, 0:wd], tB.t[:, 0:wd], g1t, j, c1b, j, samp, [tB], [x1], tmp=tA)
            affine("gpsimd", u2.t[:, j, 0:wd], tB.t[:, 0:wd], gA2, j, bA2, j, samp, [tB], [u2], tmp=tA)

        def emit2(j):
            ys = yst[j % 2]
            affine("gpsimd", ys.t[:, 0:wd], tB.t[:, 0:wd], g2t, j, b2t, j, False, [tB], [ys])
            DMA(yT[j, :, o0:o0 + wd], ys.t[:, 0:wd], reads=[ys])

        def wo_body(k, wo_b):
            for j in range(4 * k, 4 * k + 4):
                xr = xres[j % 2]
                DMA(xr.t[:, 0:wd], xT[j, :, t0:t0 + wd], writes=[xr])
                pX = PS()
                MMG([(pX.t[:, 0:wd], wo_b.t[:, c8 * 512 + (j % 4) * 128: c8 * 512 + (j % 4 + 1) * 128], MG(c8, o0, wd),
                      c8 == 0, c8 == 7) for c8 in range(8)], [wo_b, mg_m, mg_s], [pX])
                if not samp:
                    STT(h1.t[:, j, 0:wd], pX.t[:, 0:wd], G1a.t[:, j, 0:1], xr.t[:, 0:wd], ALU.mult, ALU.add,
                        [pX, G1a, xr], [h1])
                else:
                    TT("vector", v3(tA.t[:, 0:wd], 8), v3(pX.t[:, 0:wd], 8), bc(G1a, j, 1, 16, 8), ALU.mult, [pX, G1a], [tA])
                    TT("gpsimd", h1.t[:, j, 0:wd], tA.t[:, 0:wd], xr.t[:, 0:wd], ALU.add, [tA, xr], [h1])
            if k == 1:
                layer_norm(h1, wd, emit1)
        for k in range(2):
            items.append(((lambda k=k: rings.load([(0, KC, 512, wsrc(w_out, k * 512, 512))])),
                          (lambda wb, k=k: wo_body(k, wb))))

        def wu_body(gq, wu):
            for fc in range(4 * gq, 4 * gq + 4):
                pH = PS()
                MMG([(pH.t[:, 0:wd], wu.t[:, kc * 512 + (fc % 4) * 128: kc * 512 + (fc % 4 + 1) * 128], u2.t[:, kc, 0:wd],
                      kc == 0, kc == KC - 1) for kc in range(KC)], [wu, u2], [pH])
                ACT(hs_.t[:, 0:wd], pH.t[:, 0:wd], AF.Identity, [pH, bup], [hs_], bias=bup.t[:, fc:fc + 1])
                STT(hid.t[:, fc, 0:wd], hs_.t[:, 0:wd], 0.0, hs_.t[:, 0:wd], ALU.max, ALU.mult, [hs_], [hid])
        for gq in range(8):
            items.append(((lambda gq=gq: rings.load([(0, KC, 512, wsrc(w_up, gq * 512, 512))])),
                          (lambda wb, gq=gq: wu_body(gq, wb))))

        def wd_body(j, wd_b):
            pF = PS()
            MMG([(pF.t[:, 0:wd], wd_b.t[:, c * 128:(c + 1) * 128], hid.t[:, c, 0:wd], c == 0, c == 31)
                 for c in range(32)], [wd_b, hid], [pF])
            if not samp:
                STT(h1.t[:, j, 0:wd], pF.t[:, 0:wd], G2a.t[:, j, 0:1], x1.t[:, j, 0:wd], ALU.mult, ALU.add,
                    [pF, G2a, x1], [h1])
            else:
                TT("vector", v3(tA.t[:, 0:wd], 8), v3(pF.t[:, 0:wd], 8), bc(G2a, j, 1, 16, 8), ALU.mult, [pF, G2a], [tA])
                TT("gpsimd", h1.t[:, j, 0:wd], tA.t[:, 0:wd], x1.t[:, j, 0:wd], ALU.add, [tA, x1], [h1])
            if j == KC - 1:
                layer_norm(h1, wd, emit2)
        for j in range(KC):
            items.append(((lambda j=j: rings.load([(0, 32, 128, w_down[:, j * 128:(j + 1) * 128].rearrange(
                "(k p) n -> p k n", p=128))])), (lambda wb, j=j: wd_body(j, wb))))

    for (t0, wd) in MAIN_TILES + [SAMP_TILE]:
        p3b_tile(t0, wd)
    run_stream(items, 2)

    return finish()


_NC_CACHE = {}
KSTOP = [99]
DEV = {"ncores": NCORES, "nsc": 16}


def _fm(v, n):
    return np.ascontiguousarray(np.asarray(v, np.float32).reshape(n, 128).T)


def kernel(x_prompt, x_sample, c_prompt, c_sample, cache_kv_w128, cache_kv_w512, cache_kv_w2048,
           state_hgrn, w_ada, b_ada, w_in, lb_param, hg_norm_w, w_branch_a, w_branch_b, w_out,
           ln1_g, ln1_b, w_up, b_up, w_down, b_down, ln2_g, ln2_b):
    f = lambda a: np.asarray(a, np.float32)
    x_prompt, x_sample, c_prompt, c_sample = f(x_prompt), f(x_sample), f(c_prompt), f(c_sample)
    caches = [f(cache_kv_w128), f(cache_kv_w512), f(cache_kv_w2048)]
    state_hgrn = f(state_hgrn)
    tabs = const_tables()
    common = dict(
        w_ada=np.ascontiguousarray(f(w_ada)[0]), b_ada=_fm(f(b_ada)[0], 48), w_in=np.ascontiguousarray(f(w_in)[0]),
        lbp=np.ascontiguousarray(f(lb_param).reshape(2, 8, 128).transpose(2, 0, 1)),
        hgw=np.ascontiguousarray(f(hg_norm_w)[0].reshape(128, 1)),
        w_ba=np.ascontiguousarray(f(w_branch_a)[0]), w_bb=np.ascontiguousarray(f(w_branch_b)[0]),
        w_out=np.ascontiguousarray(f(w_out)[0]),
        ln1g=_fm(f(ln1_g)[0], 8), ln1b=_fm(f(ln1_b)[0], 8), ln2g=_fm(f(ln2_g)[0], 8), ln2b=_fm(f(ln2_b)[0], 8),
        w_up=np.ascontiguousarray(f(w_up)[0]), bup=_fm(f(b_up)[0], 32),
        w_down=np.ascontiguousarray(f(w_down)[0]), bdown=_fm(f(b_down)[0], 8),
        masks=tabs["masks"], smask=tabs["smask"], nmask=tabs["nmask"], hmask=tabs["hmask"],
        reset=tabs["reset"], oh=tabs["oh"], ident=np.eye(128, dtype=np.float32),
    )
    in_maps = []
    for c in range(NCORES):
        b, half = c // 2, c % 2
        T0 = half * MAIN
        halo = x_prompt[b, T0 - HALO:T0] if half == 1 else np.zeros((HALO, D), np.float32)
        toks = np.concatenate([halo, x_prompt[b, T0:T0 + MAIN], x_sample[16 * c:16 * c + 16].reshape(SAMP, D)], axis=0)
        cs = np.concatenate([c_prompt[b:b + 1], c_sample[16 * c:16 * c + 16]], axis=0)
        m = dict(common)
        m["xT"] = np.ascontiguousarray(toks.T.reshape(KC, 128, NT))
        m["cT"] = np.ascontiguousarray(cs.T.reshape(KC, 128, 17).transpose(1, 0, 2))
        m["flag"] = np.full((128, 1), float(half), np.float32)
        for (w, _), cache in zip(GROUPS, caches):
            nsc = DEV["nsc"]
            cc = cache[0, 16 * c:16 * c + nsc]
            m["kv%d" % w] = np.ascontiguousarray(cc.reshape(nsc, w, 1024))
            m["kT%d" % w] = np.ascontiguousarray(cc[:, :, 0].transpose(0, 2, 3, 1))
        m["state"] = np.ascontiguousarray(state_hgrn[0, 16 * c:16 * c + 16])
        in_maps.append(m)
    if "nc" not in _NC_CACHE:
        _NC_CACHE["nc"] = build_nc(KSTOP[0], DEV["nsc"])
    ncr = DEV["ncores"]
    if DEV.get("trace"):
        res = run_bass_kernel_spmd(_NC_CACHE["nc"], in_maps[:ncr], core_ids=list(range(ncr)), trace=True)
        print("DEV exec_time_ns", res.exec_time_ns, flush=True)
    else:
        res = run_bass_kernel_spmd(_NC_CACHE["nc"], in_maps[:ncr], core_ids=list(range(ncr)))
    R = res.results
    nsc = DEV["nsc"]
    B, S = x_prompt.shape[0], x_prompt.shape[1]
    y_prompt = np.empty((B, S, D), np.float32)
    y_sample = np.empty((128, 8, D), np.float32)
    pk = [np.empty((1, B, w, 2, 4, 128), np.float32) for (w, _) in GROUPS]
    php = np.empty((1, B, 8, 128, 128), np.float32)
    sk = [np.empty((1, 128, w, 2, 4, 128), np.float32) for (w, _) in GROUPS]
    shs = np.empty((1, 128, 8, 128, 128), np.float32)
    for c in range(ncr):
        b, half = c // 2, c % 2
        T0 = half * MAIN
        yt = np.asarray(R[c]["yT"]).reshape(D, NO)
        y_prompt[b, T0:T0 + MAIN] = yt[:, :MAIN].T
        y_sample[16 * c:16 * c + 16] = yt[:, MAIN:].T.reshape(16, 8, D)
        for gi, (w, _) in enumerate(GROUPS):
            sk[gi][0, 16 * c:16 * c + nsc] = np.asarray(R[c]["kvo%d" % w]).reshape(nsc, w, 2, 4, 128)
            if half == 1:
                pk[gi][0, b, :, 0] = np.asarray(R[c]["kTo%d" % w]).transpose(2, 0, 1)
                pk[gi][0, b, :, 1] = np.asarray(R[c]["vo%d" % w])
        if half == 1:
            php[0, b] = np.asarray(R[c]["s_out"])
        shs[0, 16 * c:16 * c + 16] = np.asarray(R[c]["state_o"])
    return (y_prompt, y_sample, pk[0], pk[1], pk[2], php, sk[0], sk[1], sk[2], shs)
```

```python
import contextlib
import numpy as np
import concourse.bass as bass
import concourse.mybir as mybir
from concourse.bass_utils import run_bass_kernel_spmd

F32 = mybir.dt.float32
BF16 = mybir.dt.bfloat16
ALU = mybir.AluOpType
AF = mybir.ActivationFunctionType

SAME_ENGINE_SYNC = "raw"


def _need_sync(p, o, d):
    if p.is_dma or p.q != o.q:
        return True
    if p.q == "tensor":
        return False
    if SAME_ENGINE_SYNC is True:
        return True
    if SAME_ENGINE_SYNC == "raw":
        return d in o.raw
    return False
COMPUTE = ("tensor", "vector", "scalar", "gpsimd")

NCORES = 8
D = 1024
KC = 8
HALO = 2048
MAIN = 2048
SAMP = 128
NT = HALO + MAIN + SAMP
M0 = HALO
S0 = HALO + MAIN
NO = MAIN + SAMP
GROUPS = ((128, 1), (512, 4), (2048, 16))
ALPHA = 2.0 ** 0.25
LN_EPS = 1e-5 / (ALPHA * ALPHA)
RMS_EPS = 1e-6
QSCALE = 128.0 ** -0.5
C_Q, C_K, C_V = 0, 1536, 3072
C_QH, C_FH, C_IH, C_OG = 4608, 5632, 6656, 7680
C_GA, C_GB = 8704, 9728


def slope(g, h):
    return 2.0 ** (-8.0 * (g * 4 + h + 1) / 12.0)


class Buf:
    __slots__ = ("name", "t", "last_w", "readers", "excl", "co_w", "w_deps")

    def __init__(self, name, t=None, excl=False):
        self.name = name
        self.t = t
        self.excl = excl
        self.last_w = None
        self.readers = []
        self.co_w = []
        self.w_deps = None

    def __getitem__(self, k):
        return self.t[k]


class Op:
    __slots__ = ("q", "fn", "deps", "raw", "is_dma", "signal", "sigval", "sem", "idx", "prev")

    def __init__(self, q, fn, is_dma):
        self.q = q
        self.fn = fn
        self.deps = set()
        self.raw = set()
        self.is_dma = is_dma
        self.signal = False
        self.sigval = None
        self.sem = None
        self.prev = 0


class Prog:
    def __init__(self, nc):
        self.nc = nc
        self.ops = []

    def op(self, q, fn, reads=(), writes=(), dma=False):
        o = Op(q, fn, dma)
        o.idx = len(self.ops)
        for b in reads:
            if b.last_w is not None:
                o.deps.add(b.last_w)
                o.raw.add(b.last_w)
                for c in b.co_w:
                    o.deps.add(c)
                    o.raw.add(c)
            if b.excl:
                for r in b.readers:
                    if self.ops[r].q != q:
                        o.deps.add(r)
        cow = {}
        for b in writes:
            if (dma and b.last_w is not None and not b.readers and not b.excl and b.w_deps is not None
                    and self.ops[b.last_w].is_dma and b not in reads):
                o.deps |= b.w_deps
                cow[id(b)] = True
            else:
                wd = set(b.readers)
                if b.last_w is not None:
                    wd.add(b.last_w)
                    wd.update(b.co_w)
                o.deps |= wd
                cow[id(b)] = wd
        for b in reads:
            b.readers.append(o.idx)
        for b in writes:
            if cow[id(b)] is True:
                b.co_w.append(b.last_w)
            else:
                b.co_w = []
                b.w_deps = cow[id(b)]
            b.last_w = o.idx
            b.readers = []
        o.deps.discard(o.idx)
        self.ops.append(o)
        return o

    def barrier(self):
        self.ops.append(None)

    def emit(self, final_wait_q="sync"):
        nc = self.nc
        ops = self.ops
        lastq = {}
        for o in ops:
            if o is None:
                for q_, lo in lastq.items():
                    lo.signal = True
                continue
            if not o.is_dma:
                lastq[o.q] = o
        for o in ops:
            if o is None:
                continue
            for d in o.deps:
                p = ops[d]
                if _need_sync(p, o, d):
                    p.signal = True
        for o in ops:
            if o is not None and o.is_dma:
                o.signal = True
        stack = contextlib.ExitStack()
        qsem = {q: stack.enter_context(nc.semaphore("s_" + q)) for q in COMPUTE}
        pool_sizes = {"sync": 16, "gpsimd": 8, "scalar": 8}
        dpool = {q: [stack.enter_context(nc.semaphore("d_%s%d" % (q, i))) for i in range(n)]
                 for q, n in pool_sizes.items()}
        dcount = {q: [0] * n for q, n in pool_sizes.items()}
        dnext = {q: 0 for q in pool_sizes}
        qcount = {q: 0 for q in COMPUTE}
        snaps = {}
        for oi, o in enumerate(ops):
            if o is None:
                snaps[oi] = (dict(qcount), {q: list(v) for q, v in dcount.items()})
                continue
            if not o.signal:
                continue
            if o.is_dma:
                i = dnext[o.q]
                dnext[o.q] = (i + 1) % pool_sizes[o.q]
                o.sem = dpool[o.q][i]
                o.prev = dcount[o.q][i]
                dcount[o.q][i] += 16
                o.sigval = dcount[o.q][i]
            else:
                o.sem = qsem[o.q]
                qcount[o.q] += 1
                o.sigval = qcount[o.q]
        with nc.Block() as block:
            def make(qname):
                def body(eng):
                    known = {}
                    for oi, o in enumerate(ops):
                        if o is None:
                            qc, dc = snaps[oi]
                            for q2 in COMPUTE:
                                if qc[q2] > known.get(id(qsem[q2]), 0):
                                    eng.wait_ge(qsem[q2], qc[q2])
                                    known[id(qsem[q2])] = qc[q2]
                            for q2, n in pool_sizes.items():
                                for i in range(n):
                                    if dc[q2][i] > known.get(id(dpool[q2][i]), 0):
                                        eng.wait_ge(dpool[q2][i], dc[q2][i])
                                        known[id(dpool[q2][i])] = dc[q2][i]
                            continue
                        if o.q != qname:
                            continue
                        for d in sorted(o.deps):
                            p = ops[d]
                            if not p.signal:
                                continue
                            if not _need_sync(p, o, d):
                                continue
                            key = id(p.sem)
                            if known.get(key, 0) >= p.sigval:
                                continue
                            eng.wait_ge(p.sem, p.sigval)
                            known[key] = p.sigval
                        if o.is_dma and o.prev > 0 and known.get(id(o.sem), 0) < o.prev:
                            eng.wait_ge(o.sem, o.prev)
                            known[id(o.sem)] = o.prev
                        inst = o.fn(eng)
                        if o.signal:
                            inst.then_inc(o.sem, 16 if o.is_dma else 1)
                    if qname == final_wait_q:
                        for q2, n in pool_sizes.items():
                            for i in range(n):
                                if dcount[q2][i] > 0:
                                    eng.wait_ge(dpool[q2][i], dcount[q2][i])
                        for q in COMPUTE:
                            if qcount[q] > 0:
                                eng.wait_ge(qsem[q], qcount[q])
                return body
            block.sync(make("sync"))
            block.scalar(make("scalar"))
            block.vector(make("vector"))
            block.gpsimd(make("gpsimd"))
            block.tensor(make("tensor"))
        stack.close()


def const_tables():
    j = np.arange(128)[:, None].astype(np.float64)
    i = np.arange(128)[None, :].astype(np.float64)
    masks = np.zeros((12, 128, 256), np.float32)
    smask = np.zeros((12, 128, 128), np.float32)
    nmask = np.zeros((12, 128, 128), np.float32)
    for g, (win, dil) in enumerate(GROUPS):
        for h in range(4):
            sl = slope(g, h)
            prev = np.where(i <= j, np.exp(-sl * dil * (128 + i - j)), 0.0)
            own = np.where(i >= j, np.exp(-sl * dil * (i - j)), 0.0)
            masks[g * 4 + h, :, :128] = prev
            masks[g * 4 + h, :, 128:] = own
            nb = win // 128
            p = np.arange(128)[:, None]
            for blk in range(nb):
                t = np.arange(8)[None, :]
                idx = blk * 128 + p
                diff = win + t - idx
                ok = (diff % dil == 0) & (diff >= 0) & (diff <= win)
                smask[g * 4 + h, :, blk * 8:(blk + 1) * 8] = np.where(ok, np.exp(-sl * diff), 0.0)
            kp = np.arange(128)[:, None]
            qp = np.arange(128)[None, :]
            dj = (qp % 8) - (kp % 8)
            ok = (kp // 8 == qp // 8) & (dj >= 0) & (dj % dil == 0)
            nmask[g * 4 + h] = np.where(ok, np.exp(-sl * dj), 0.0)
    s = np.arange(128)[:, None]
    t = np.arange(128)[None, :]
    hmask = np.zeros((2, 128, 128), np.float32)
    hmask[0] = ((s // 64 == t // 64) & (s <= t)).astype(np.float32)
    hmask[1] = ((s // 8 == t // 8) & (s <= t)).astype(np.float32)
    reset = np.ones((2, 128, 512), np.float32)
    reset[0, :, ::64] = 0.0
    reset[1, :, ::8] = 0.0
    oh = (np.arange(128)[:, None] // 8 == np.arange(16)[None, :]).astype(np.float32)
    return dict(masks=masks, smask=smask, nmask=nmask, hmask=hmask, reset=reset, oh=oh)


ARENA_WORDS = 53100


class Region:
    def __init__(self, big, lo, hi):
        self.big, self.lo, self.hi, self.top = big, lo, hi, lo

    def alloc(self, name, shape, dt=F32):
        assert shape[0] == 128
        n = 1
        for s_ in shape[1:]:
            n *= s_
        words = n if dt == F32 else (n + 1) // 2
        words = (words + 7) // 8 * 8
        assert self.top + words <= self.hi, (name, self.top, words, self.hi)
        ap = self.big[:, self.top:self.top + words]
        if dt != F32:
            ap = ap.bitcast(dt)
        ap = ap[:, 0:n]
        if len(shape) == 3:
            ap = ap.rearrange("p (a b) -> p a b", b=shape[2])
        self.top += words
        return Buf(name, ap)


def build_nc(kstop=99, nsc=16):
    nc = bass.Bass("TRN2", target_bir_lowering=False)
    P = Prog(nc)
    st = contextlib.ExitStack()

    def finish():
        P.emit()
        st.close()
        return nc

    def din(name, shape):
        return nc.dram_tensor(name, list(shape), F32, kind="ExternalInput").ap()

    def dout(name, shape):
        return nc.dram_tensor(name, list(shape), F32, kind="ExternalOutput").ap()

    xT = din("xT", [KC, 128, NT])
    cT = din("cT", [128, KC, 17])
    flag_d = din("flag", [128, 1])
    w_ada = din("w_ada", [D, 6144])
    b_ada = din("b_ada", [128, 48])
    w_in = din("w_in", [D, 10752])
    lbp = din("lbp", [128, 2, 8])
    hgw_d = din("hgw", [128, 1])
    w_ba = din("w_ba", [512, D])
    w_bb = din("w_bb", [D, D])
    w_out = din("w_out", [D, D])
    ln1g_d = din("ln1g", [128, 8]); ln1b_d = din("ln1b", [128, 8])
    ln2g_d = din("ln2g", [128, 8]); ln2b_d = din("ln2b", [128, 8])
    w_up = din("w_up", [D, 4096]); bup_d = din("bup", [128, 32])
    w_down = din("w_down", [4096, D]); bdown_d = din("bdown", [128, 8])
    masks_d = din("masks", [12, 128, 256])
    smask_d = din("smask", [12, 128, 128])
    nmask_d = din("nmask", [12, 128, 128])
    hmask_d = din("hmask", [2, 128, 128])
    reset_d = din("reset", [2, 128, 512])
    oh_d = din("oh", [128, 16])
    ident_d = din("ident", [128, 128])
    kvc = [din("kv%d" % w, [nsc, w, 1024]) for (w, _) in GROUPS]
    kTc = [din("kT%d" % w, [nsc, 4, 128, w]) for (w, _) in GROUPS]
    state_d = din("state", [16, 8, 128, 128])

    yT = dout("yT", [KC, 128, NO])
    kTo = [dout("kTo%d" % w, [4, 128, w]) for (w, _) in GROUPS]
    vo = [dout("vo%d" % w, [w, 4, 128]) for (w, _) in GROUPS]
    s_out = dout("s_out", [8, 128, 128])
    kvo = [dout("kvo%d" % w, [nsc, w, 1024]) for (w, _) in GROUPS]
    state_o = dout("state_o", [16, 8, 128, 128])

    big = st.enter_context(nc.sbuf_tensor("big", [128, ARENA_WORDS], F32))

    def psum(name, shape, dt=F32):
        return Buf(name, st.enter_context(nc.psum_tensor(name, list(shape), dt)), excl=True)

    def DMA(out, in_, reads=(), writes=(), q="sync"):
        P.op(q, lambda e: e.dma_start(out=out, in_=in_), reads, writes, dma=True)

    def ACT(out, in_, func, reads, writes, scale=None, bias=None):
        kw = {}
        if scale is not None:
            kw["scale"] = scale
        if bias is not None:
            kw["bias"] = bias
        P.op("scalar", lambda e: e.activation(out=out, in_=in_, func=func, **kw), reads, writes)

    def TT(q, out, in0, in1, op, reads, writes):
        P.op(q, lambda e: e.tensor_tensor(out=out, in0=in0, in1=in1, op=op), reads, writes)

    def TS(q, out, in0, s1, s2, op0, op1, reads, writes):
        if op1 is None:
            P.op(q, lambda e: e.tensor_scalar(out=out, in0=in0, scalar1=s1, scalar2=None, op0=op0), reads, writes)
        else:
            P.op(q, lambda e: e.tensor_scalar(out=out, in0=in0, scalar1=s1, scalar2=s2, op0=op0, op1=op1), reads, writes)

    def STT(out, in0, scalar, in1, op0, op1, reads, writes):
        P.op("vector", lambda e: e.scalar_tensor_tensor(out=out, in0=in0, scalar=scalar, in1=in1, op0=op0, op1=op1),
             reads, writes)

    def COPY(q, out, in_, reads, writes):
        if q == "scalar":
            P.op(q, lambda e: e.activation(out=out, in_=in_, func=AF.Copy), reads, writes)
        else:
            P.op(q, lambda e: e.tensor_copy(out=out, in_=in_), reads, writes)

    def RECIP(out, in_, reads, writes):
        P.op("vector", lambda e: e.reciprocal(out=out, in_=in_), reads, writes)

    def MEMSET(q, out, val, writes):
        P.op(q, lambda e: e.memset(out, val), (), writes)

    def MMG(mms, reads, writes):
        mms = list(mms)

        def fn(e):
            inst = None
            for (o, l, r, s_, p_) in mms:
                inst = e.matmul(o, lhsT=l, rhs=r, start=s_, stop=p_, skip_group_check=True)
            return inst
        P.op("tensor", fn, reads, writes)

    def TRANSPOSE(out, in_, ident, reads, writes):
        P.op("tensor", lambda e: e.transpose(out, in_, ident), reads, writes)

    psb = [psum("ps%d" % i, [128, 512]) for i in range(6)]
    psacc = psum("psacc", [128, 512])
    pst = psum("pst", [128, 1024], BF16)
    ps_i = [0]
    ps_n = [6]

    def PS():
        b = psb[ps_i[0] % ps_n[0]]
        ps_i[0] += 1
        return b

    RA = Region(big, 0, 4608)
    o_uTm = 4608
    o_attn = o_uTm + 8704
    o_uTh = o_attn + 4352
    o_hi = o_uTh + 8192
    RB = Region(big, o_uTm, o_hi)
    uTm = RB.alloc("uTm", [128, KC, NO], BF16)
    attnT = RB.alloc("attnT", [128, 4, NO], BF16)
    uTh = RB.alloc("uTh", [128, KC, HALO], BF16)
    assert RB.top == o_hi

    def U(kc, t0, n, step=1):
        if t0 < M0:
            assert t0 + (n - 1) * step < M0
            return uTh.t[:, kc, t0: t0 + (n - 1) * step + 1: step]
        a = t0 - M0
        return uTm.t[:, kc, a: a + (n - 1) * step + 1: step]

    def sb(name, shape, dt=F32):
        return RA.alloc(name, shape, dt)

    modT = sb("modT", [128, 48, 17])
    A1 = sb("A1", [128, 8, 17]); A2 = sb("A2", [128, 8, 17])
    G1a = sb("G1a", [128, 8, 17]); G2a = sb("G2a", [128, 8, 17])
    c1b = sb("c1b", [128, 8, 17]); g1t = sb("g1t", [128, 8, 17])
    gA2 = sb("gA2", [128, 8, 17]); bA2 = sb("bA2", [128, 8, 17])
    g2t = sb("g2t", [128, 8, 17]); b2t = sb("b2t", [128, 8, 17])
    flag = sb("flag_s", [128, 1])
    lbT = sb("lbT", [128, 8]); lb1 = sb("lb1", [128, 8]); nlb1 = sb("nlb1", [128, 8])
    hgw = sb("hgw_s", [128, 1])
    ln1g = sb("ln1g_s", [128, 8]); ln1b = sb("ln1b_s", [128, 8])
    ln2g = sb("ln2g_s", [128, 8]); ln2b = sb("ln2b_s", [128, 8])
    bup = sb("bup_s", [128, 32]); bdown = sb("bdown_s", [128, 8])
    ones_bf = sb("ones_bf", [128, 128], BF16)
    ones_m = sb("ones_m", [128, 128])
    ones_r = sb("ones_r", [128, 128])
    ident_bf = sb("ident_bf", [128, 128], BF16)
    hmask = sb("hmask_s", [128, 2, 128])
    reset = sb("reset_s", [128, 2, 512])
    oh = sb("oh_s", [128, 16])
    mg_s = sb("mg_s", [128, KC, 128], BF16)
    ones_f = sb("ones_f", [128, 128])

    class Rings:
        def __init__(self, R, nst, nbf):
            self.st = [R.alloc("wst%d" % i, [128, 4096]) for i in range(nst)]
            self.bf = [R.alloc("wbf%d" % i, [128, 4096], BF16) for i in range(nbf)]
            self.i = 0
            self.j = 0
            self.subs = {}

        def W(self, s_):
            return [s_] + list(self.subs.get(id(s_), ()))

        def stage(self):
            s_ = self.st[self.i % len(self.st)]
            self.i += 1
            return s_

        def load(self, srcs, castq=None):
            s_ = self.stage()
            d_ = self.bf[self.j % len(self.bf)]
            self.j += 1
            tot = 0
            for (off, a, b, ap) in srcs:
                DMA(s_.t[:, off:off + a * b].rearrange("p (a b) -> p a b", b=b), ap, writes=self.W(s_))
                tot = max(tot, off + a * b)
            q = castq or ("scalar" if (self.j % 2 == 0) else "vector")
            COPY(q, d_.t[:, 0:tot], s_.t[:, 0:tot], self.W(s_), [d_])
            return d_

    def run_stream(items, lookahead):
        loaded = {}
        n = len(items)
        for i in range(min(lookahead, n)):
            loaded[i] = items[i][0]()
        for i in range(n):
            if i + lookahead < n:
                loaded[i + lookahead] = items[i + lookahead][0]()
            items[i][1](loaded.pop(i))
        assert not loaded

    def wsrc(w, c0, ncols, kc=KC, r0=0):
        return w[r0:r0 + kc * 128, c0:c0 + ncols].rearrange("(k p) n -> p k n", p=128)

    def wblocks(w, cols, kc=KC, width=128):
        nb_ = len(cols)
        return [(0, None, None, None)] and [
            (bi, c) for bi, c in enumerate(cols)], nb_ * width

    def load_blocks(rings, w, cols, kc=KC, width=128, castq=None):
        s_ = rings.stage()
        d_ = rings.bf[rings.j % len(rings.bf)]
        rings.j += 1
        row = len(cols) * width
        tot = kc * row
        for bi, c in enumerate(cols):
            DMA(s_.t[:, 0:tot].rearrange("p (a b) -> p a b", b=row)[:, :, bi * width:(bi + 1) * width],
                wsrc(w, c, width, kc=kc), writes=rings.W(s_))
        q = castq or ("scalar" if (rings.j % 2 == 0) else "vector")
        COPY(q, d_.t[:, 0:tot], s_.t[:, 0:tot], rings.W(s_), [d_])
        return d_

    def proj_fm(wb, row, woff, t0, wd):
        pb = PS()
        MMG([(pb.t[:, 0:wd], wb.t[:, kc * row + woff: kc * row + woff + 128], U(kc, t0, wd), kc == 0, kc == KC - 1)
             for kc in range(KC)], [wb, uTm, uTh], [pb])
        return pb

    def bc(tab, j, lo, n, rep):
        return tab.t[:, j, lo:lo + n].unsqueeze(2).broadcast_to([128, n, rep])

    def v3(ap, rep):
        return ap.rearrange("p (s t) -> p s t", t=rep)

    def affine(q2, out, in_, stab, sj, btab, bj, sample, reads, writes, tmp=None):
        if not sample:
            ACT(out, in_, AF.Identity, list(reads) + [stab, btab], writes, scale=stab.t[:, sj, 0:1], bias=btab.t[:, bj, 0:1])
        else:
            TT("vector", v3(tmp.t[:, 0:128], 8), v3(in_, 8), bc(stab, sj, 1, 16, 8), ALU.mult, list(reads) + [stab], [tmp])
            TT(q2, v3(out, 8), v3(tmp.t[:, 0:128], 8), bc(btab, bj, 1, 16, 8), ALU.add, [tmp, btab], writes)

    MAIN_TILES = [(M0 + i * 512, 512) for i in range(4)]
    HALO_TILES = [(i * 512, 512) for i in range(4)]
    SAMP_TILE = (S0, 128)

    R0 = Region(big, o_hi, ARENA_WORDS)
    for (dst, src) in ((flag, flag_d), (hgw, hgw_d), (ln1g, ln1g_d), (ln1b, ln1b_d), (ln2g, ln2g_d),
                       (ln2b, ln2b_d), (bup, bup_d), (bdown, bdown_d), (oh, oh_d)):
        DMA(dst.t[:], src, writes=[dst])
    DMA(hmask.t[:], hmask_d.rearrange("a p n -> p a n"), writes=[hmask])
    DMA(reset.t[:], reset_d.rearrange("a p n -> p a n"), writes=[reset])
    MEMSET("vector", ones_bf.t[:], 1.0, [ones_bf])
    MEMSET("vector", ones_m.t[:], 1.0 / 1024.0, [ones_m])
    MEMSET("vector", ones_r.t[:], 1.0 / 128.0, [ones_r])
    MEMSET("vector", ones_f.t[:], 1.0, [ones_f])
    identf = R0.alloc("identf", [128, 128])
    DMA(identf.t[:], ident_d, writes=[identf])
    COPY("vector", ident_bf.t[:], identf.t[:], [identf], [ident_bf])

    kvo_bufs = [Buf("kvo%d" % g) for g in range(3)]
    copy_jobs = []
    for b in range(nsc):
        for g, (w, _) in enumerate(GROUPS):
            copy_jobs.append((g, b, w))

    def issue_copies(n):
        for _ in range(n):
            if copy_jobs:
                g, b, w = copy_jobs.pop(0)
                DMA(kvo[g][b, 0:w - 8, :], kvc[g][b, 8:w, :], q="sync")

    lbs = R0.alloc("lbs", [128, 2, 8])
    DMA(lbs.t[:], lbp, writes=[lbs])
    TT("vector", lbT.t[:], lbs.t[:, 1, :], lbs.t[:, 0, :], ALU.subtract, [lbs], [lbT])
    ACT(lbT.t[:], lbT.t[:], AF.Exp, [lbT], [lbT])
    TS("vector", lbT.t[:], lbT.t[:], 1.0, None, ALU.add, None, [lbT], [lbT])
    RECIP(lbT.t[:], lbT.t[:], [lbT], [lbT])
    TS("vector", lb1.t[:], lbT.t[:], -1.0, 1.0, ALU.mult, ALU.add, [lbT], [lb1])
    TS("vector", nlb1.t[:], lb1.t[:], -1.0, None, ALU.mult, None, [lb1], [nlb1])

    scT = R0.alloc("scT", [128, KC, 17])
    sct = R0.alloc("sct", [128, KC, 17])
    DMA(scT.t[:], cT, writes=[scT])
    ACT(sct.t[:], scT.t[:], AF.Exp, [scT], [sct], scale=-1.0)
    TS("vector", sct.t[:], sct.t[:], 1.0, None, ALU.add, None, [sct], [sct])
    RECIP(sct.t[:], sct.t[:], [sct], [sct])
    TT("vector", scT.t[:], scT.t[:], sct.t[:], ALU.mult, [scT, sct], [scT])
    bada = R0.alloc("bada", [128, 48])
    DMA(bada.t[:], b_ada, writes=[bada])
    wada = [R0.alloc("wada%d" % i, [128, 4096]) for i in range(2)]
    for ng in range(12):
        s_ = wada[ng % 2]
        DMA(s_.t[:, 0:4096].rearrange("p (a b) -> p a b", b=512), wsrc(w_ada, ng * 512, 512), writes=[s_])
        pb = PS()
        mms = []
        for jj in range(4):
            for kc in range(KC):
                mms.append((pb.t[:, jj * 17:(jj + 1) * 17], s_.t[:, kc * 512 + jj * 128: kc * 512 + (jj + 1) * 128],
                            scT.t[:, kc, :], kc == 0 and jj == 0, kc == KC - 1))
        MMG(mms, [s_, scT], [pb])
        for jj in range(4):
            j = ng * 4 + jj
            TS("vector", modT.t[:, j, :], pb.t[:, jj * 17:(jj + 1) * 17], bada.t[:, j:j + 1], None, ALU.add, None,
               [pb, bada], [modT])
    TS("vector", A1.t[:], modT.t[:, 8:16, :], 1.0, None, ALU.add, None, [modT], [A1])
    TS("vector", A2.t[:], modT.t[:, 32:40, :], 1.0, None, ALU.add, None, [modT], [A2])
    TS("vector", G1a.t[:], modT.t[:, 16:24, :], 1.0 / ALPHA, None, ALU.mult, None, [modT], [G1a])
    TS("vector", G2a.t[:], modT.t[:, 40:48, :], 1.0 / ALPHA, None, ALU.mult, None, [modT], [G2a])

    def bcp(v):
        return v.t[:].unsqueeze(2).broadcast_to([128, 8, 17])
    TS("vector", g1t.t[:], A1.t[:], 0.0, None, ALU.mult, None, [A1], [g1t])
    TT("vector", g1t.t[:], g1t.t[:], bcp(ln1g), ALU.add, [g1t, ln1g], [g1t])
    TT("vector", c1b.t[:], G2a.t[:], bcp(bdown), ALU.mult, [G2a, bdown], [c1b])
    TT("vector", c1b.t[:], c1b.t[:], bcp(ln1b), ALU.add, [c1b, ln1b], [c1b])
    TT("vector", gA2.t[:], A2.t[:], bcp(ln1g), ALU.mult, [A2, ln1g], [gA2])
    TT("vector", bA2.t[:], A2.t[:], bcp(ln1b), ALU.mult, [A2, ln1b], [bA2])
    TT("vector", bA2.t[:], bA2.t[:], modT.t[:, 24:32, :], ALU.add, [bA2, modT], [bA2])
    TS("vector", g2t.t[:], A1.t[:], 0.0, None, ALU.mult, None, [A1], [g2t])
    TT("vector", b2t.t[:], g2t.t[:], bcp(ln2b), ALU.add, [g2t, ln2b], [b2t])
    TT("vector", g2t.t[:], g2t.t[:], bcp(ln2g), ALU.add, [g2t, ln2g], [g2t])

    xst = [R0.alloc("xst%d" % i, [128, NT]) for i in range(2)]
    tmpA = R0.alloc("tmpA", [128, 512])
    for kc in range(KC):
        xs = xst[kc % 2]
        DMA(xs.t[:, 0:2048], xT[kc, :, 0:2048], writes=[xs])
        DMA(xs.t[:, 2048:NT], xT[kc, :, 2048:NT], writes=[xs])
        for t0 in range(0, S0, 1024):
            affine("gpsimd", U(kc, t0, 1024), xs.t[:, t0:t0 + 1024], A1, kc, modT, kc, False, [xs], [uTm, uTh])
        affine("gpsimd", U(kc, S0, 128), xs.t[:, S0:NT], A1, kc, modT, kc, True, [xs], [uTm], tmp=tmpA)
    P.barrier()
    if kstop <= 1:
        return finish()

    R2 = Region(big, o_hi, ARENA_WORDS)
    rings = Rings(R2, 2, 1)
    rings_c = Rings(R2, 0, 2)
    rings_c.stage = rings.stage
    for s_ in rings.st:
        rings.subs[id(s_)] = (Buf(s_.name + "K", s_.t), Buf(s_.name + "V", s_.t))
    cslots = [(Buf(d_.name + "K", d_.t), Buf(d_.name + "V", d_.t)) for d_ in rings_c.bf]
    QT = R2.alloc("QT", [128, NO], BF16)
    QSs = [R2.alloc("QSs%d" % i, [128, 128], BF16) for i in range(2)]
    KT = R2.alloc("KT", [128, NT], BF16)
    Vb = R2.alloc("Vb", [128, 33, 128], BF16)
    Vs = R2.alloc("Vs", [128, 128], BF16)
    numacc = R2.alloc("numacc", [128, MAIN]); denacc = R2.alloc("denacc", [128, MAIN])
    mk = R2.alloc("mk", [128, 256]); mkh = R2.alloc("mkh", [128, 256])
    smk_ = [R2.alloc("smk%d" % i, [128, 128]) for i in range(2)]; nmk = R2.alloc("nmk", [128, 128])
    Eb = [R2.alloc("Eb%d" % i, [128, 256], BF16) for i in range(2)]
    Pb = [R2.alloc("Pb%d" % i, [128, 256], BF16) for i in range(2)]
    kst = [R2.alloc("kst%d" % i, [128, 512]) for i in range(2)]
    kst_i = [0]
    sE = [R2.alloc("sE%d" % i, [128, 128]) for i in range(2)]
    sP = [R2.alloc("sP%d" % i, [128, 128], BF16) for i in range(2)]
    sPs = [R2.alloc("sPs%d" % i, [128, 8]) for i in range(2)]
    tkv = kst[0]
    rden = kst[1]
    cnt = [0]
    import collections
    nsE = R2.alloc("nsE", [128, 128])
    nsP = R2.alloc("nsP", [128, 128], BF16)
    plist = []
    pidx = {"dma": 0, "cast": 0, "p1": 0, "p2": 0}
    pcnt = [0]

    def pieces_begin(new):
        assert pidx["p1"] == len(plist) and pidx["p2"] == len(plist)
        plist[:] = new
        for k_ in pidx:
            pidx[k_] = 0

    def prologue():
        n = len(plist)
        while pidx["dma"] < min(2, n):
            plist[pidx["dma"]]["dma"](); pidx["dma"] += 1

    def pump(drain=False):
        while True:
            n = len(plist)
            j = pidx["p1"]
            if j >= n:
                if pidx["p2"] < n:
                    plist[pidx["p2"]]["p2"](); pidx["p2"] += 1
                return
            while pidx["dma"] < min(j + 2, n):
                plist[pidx["dma"]]["dma"](); pidx["dma"] += 1
            if pidx["cast"] <= j:
                plist[j]["cast"](); pidx["cast"] = j + 1
            plist[j]["p1"](); pidx["p1"] = j + 1
            if j >= 1:
                plist[j - 1]["p2"](); pidx["p2"] = j
            if j + 1 < n:
                plist[j + 1]["cast"](); pidx["cast"] = j + 2
            if j + 2 < n:
                plist[j + 2]["dma"](); pidx["dma"] = j + 3
            if not drain:
                return
    saccs = [psacc, psb[5]]

    items = []
    ps_n[0] = 5

    def mk_tkv(kv, cbase, g3):
        def loader():
            return rings.load([(0, KC, 512, wsrc(w_in, cbase + g3 * 512, 512))])

        def body(wb):
            pb = PS()
            MMG([(pb.t[:, :], U(kc, S0, 128), wb.t[:, kc * 512:(kc + 1) * 512], kc == 0, kc == KC - 1)
                 for kc in range(KC)], [wb, uTm], [pb])
            COPY("scalar", tkv.t[:], pb.t[:, :], [pb], [tkv])
            w = GROUPS[g3][0]
            for b in range(nsc):
                DMA(kvo[g3][b, w - 8:w, kv * 512:(kv + 1) * 512], tkv.t[b * 8:(b + 1) * 8, :],
                    reads=[tkv])
        return (loader, body)
    for kv, cbase in ((0, C_K), (1, C_V)):
        for g3 in range(3):
            items.append(mk_tkv(kv, cbase, g3))

    sacc_first = [True]

    def mk_gh(h, g):
        win, dil = GROUPS[g]
        gh = g * 4 + h
        sacc = saccs[h % 2]
        par = (h * 3 + g) % 2
        smk = smk_[par]
        pcount = [0]

        def pstep():
            pump()

        def loader():
            return load_blocks(rings, w_in, [cb + g * 512 + h * 128 for cb in (C_Q, C_K, C_V)])

        def body(wb):
            if g == 0:
                sacc_first[0] = True
            prologue()
            DMA(mk.t[:], masks_d[gh], writes=[mk])
            DMA(smk.t[:], smask_d[gh], writes=[smk])
            DMA(nmk.t[:], nmask_d[gh], writes=[nmk])
            COPY("gpsimd", mkh.t[:, 128:256], mk.t[:, 128:256], [mk], [mkh])
            TS("vector", mkh.t[:, 0:128], mk.t[:, 0:128], flag.t[:, 0:1], None, ALU.mult, None, [mk, flag], [mkh])
            for (t0, wd) in MAIN_TILES + [SAMP_TILE]:
                pb = proj_fm(wb, 384, 0, t0, wd)
                COPY("scalar", QT.t[:, t0 - M0:t0 - M0 + wd], pb.t[:, 0:wd], [pb], [QT])
                pstep()
            ktiles = [(t0, wd) for (t0, wd) in HALO_TILES if t0 + wd > HALO - win] + MAIN_TILES + [SAMP_TILE]
            for (t0, wd) in ktiles:
                pb = proj_fm(wb, 384, 128, t0, wd)
                COPY("scalar", KT.t[:, t0:t0 + wd], pb.t[:, 0:wd], [pb], [KT])
                if M0 <= t0 < S0 and t0 + wd > S0 - win:
                    lo = max(t0, S0 - win)
                    ks = kst[kst_i[0] % 2]; kst_i[0] += 1
                    COPY("vector", ks.t[:, 0:t0 + wd - lo], pb.t[:, lo - t0:wd], [pb], [ks])
                    DMA(kTo[g][h, :, lo - (S0 - win): t0 + wd - (S0 - win)], ks.t[:, 0:t0 + wd - lo], reads=[ks])
                pstep()
            nmb = 1 + MAIN // (128 * dil)
            base = HALO - win
            blocks = [(r, mb) for r in range(dil) for mb in range(nmb)]
            for q0 in range(0, len(blocks), 4):
                grp = blocks[q0:q0 + 4]
                pb = PS()
                mms = []
                for bi, (r, mb) in enumerate(grp):
                    tstart = base + mb * 128 * dil + r
                    for kc in range(KC):
                        mms.append((pb.t[:, bi * 128:(bi + 1) * 128], U(kc, tstart, 128, dil),
                                    wb.t[:, kc * 384 + 256: kc * 384 + 384], kc == 0, kc == KC - 1))
                MMG(mms, [wb, uTm, uTh], [pb])
                COPY("scalar", Vb.t[:, q0:q0 + len(grp), :],
                     pb.t[:, 0:len(grp) * 128].rearrange("p (a b) -> p a b", b=128), [pb], [Vb])
                need = [(bi, r, mb) for bi, (r, mb) in enumerate(grp)
                        if mb >= 1 and (mb * 128 * dil + base) + 127 * dil + r >= S0 - win]
                if need:
                    ks = kst[kst_i[0] % 2]; kst_i[0] += 1
                    COPY("vector", ks.t[:, 0:len(grp) * 128], pb.t[:, 0:len(grp) * 128], [pb], [ks])
                    for (bi, r, mb) in need:
                        ts_ = base + mb * 128 * dil + r - (S0 - win)
                        DMA(vo[g][ts_: ts_ + 127 * dil + 1: dil, h, :], ks.t[:, bi * 128:(bi + 1) * 128], reads=[ks])
                pstep()
            pb = PS()
            MMG([(pb.t[:, 0:128], U(kc, S0, 128), wb.t[:, kc * 384 + 256: kc * 384 + 384], kc == 0, kc == KC - 1)
                 for kc in range(KC)], [wb, uTm], [pb])
            COPY("scalar", Vs.t[:], pb.t[:, 0:128], [pb], [Vs])
            for r in range(dil):
                for mb in range(1, nmb):
                    qstart = (mb - 1) * 128 * dil + r
                    qcols = slice(qstart, qstart + 127 * dil + 1, dil)
                    kprev = base + (mb - 1) * 128 * dil + r
                    kown = base + mb * 128 * dil + r
                    bprev = r * nmb + mb - 1
                    i2 = cnt[0] % 2; cnt[0] += 1
                    pS = PS()
                    MMG([(pS.t[:, 0:128], KT.t[:, kprev: kprev + 127 * dil + 1: dil], QT.t[:, qcols], True, True),
                         (pS.t[:, 128:256], KT.t[:, kown: kown + 127 * dil + 1: dil], QT.t[:, qcols], False, True)],
                        [KT, QT], [pS])
                    ACT(Eb[i2].t[:], pS.t[:, 0:256], AF.Exp, [pS], [Eb[i2]], scale=QSCALE)
                    mm_ = mkh if mb == 1 else mk
                    TT("gpsimd" if i2 else "vector", Pb[i2].t[:], Eb[i2].t[:], mm_.t[:], ALU.mult, [Eb[i2], mm_], [Pb[i2]])
                    pO = PS()
                    MMG([(pO.t[:, 0:128], Vb.t[:, bprev, :], Pb[i2].t[:, 0:128], True, False),
                         (pO.t[:, 0:128], Vb.t[:, bprev + 1, :], Pb[i2].t[:, 128:256], False, True),
                         (pO.t[:, 128:256], ones_bf.t[:], Pb[i2].t[:, 0:128], False, False),
                         (pO.t[:, 128:256], ones_bf.t[:], Pb[i2].t[:, 128:256], False, True)],
                        [Vb, Pb[i2], ones_bf], [pO])
                    if g == 0:
                        COPY("vector", numacc.t[:, qcols], pO.t[:, 0:128], [pO], [numacc])
                        COPY("vector", denacc.t[:, qcols], pO.t[:, 128:256], [pO], [denacc])
                    else:
                        TT("vector", numacc.t[:, qcols], pO.t[:, 0:128], numacc.t[:, qcols], ALU.add, [pO, numacc], [numacc])
                        TT("vector", denacc.t[:, qcols], pO.t[:, 128:256], denacc.t[:, qcols], ALU.add, [pO, denacc], [denacc])
            QS = QT.t[:, MAIN:NO]
            pS = PS()
            MMG([(pS.t[:, 0:128], KT.t[:, S0:NT], QS, True, True)], [KT, QT], [pS])
            ACT(nsE.t[:], pS.t[:, 0:128], AF.Exp, [pS], [nsE], scale=QSCALE)
            TT("gpsimd", nsP.t[:], nsE.t[:], nmk.t[:], ALU.mult, [nsE, nmk], [nsP])
            MMG([(sacc.t[:, 0:128], Vs.t[:], nsP.t[:], sacc_first[0], False),
                 (sacc.t[:, 128:256], ones_bf.t[:], nsP.t[:], False, False)], [Vs, nsP, ones_bf], [sacc])
            sacc_first[0] = False
            pump(drain=True)
            if g == 0 and h > 0:
                fin_sample(h - 1)
            COPY("vector", QSs[par].t[:], QT.t[:, MAIN:NO], [QT], [QSs[par]])
            pieces_begin([mk_cache(h, g, b) for b in range(nsc)])
            if g == 2:
                fin_prompt(h)
        return (loader, body)

    def mk_cache(h, g, b):
        win, dil = GROUPS[g]
        nb = win // 128
        gh = g * 4 + h
        sacc = saccs[h % 2]
        par = (h * 3 + g) % 2
        smk = smk_[par]

        st_ = {}

        def dma():
            s_ = rings.stage()
            sK, sV = rings.subs[id(s_)]
            st_["s"] = (sK, sV)
            DMA(sK.t[:, 0:win].rearrange("p (a b) -> p a b", b=win), kTc[g][b, h].unsqueeze(1), writes=[sK])
            DMA(sV.t[:, 2048:2048 + nb * 128].rearrange("p (a b) -> p a b", b=128),
                kvc[g][b, :, 512 + h * 128: 512 + (h + 1) * 128].rearrange("(a p) d -> p a d", p=128), writes=[sV])

        def cast():
            sK, sV = st_["s"]
            dK, dV = cslots[rings_c.j % len(cslots)]
            rings_c.j += 1
            st_["d"] = (dK, dV)
            qk, qv = ("vector", "scalar") if (rings_c.j % 2 == 0) else ("scalar", "vector")
            COPY(qk, dK.t[:, 0:win], sK.t[:, 0:win], [sK], [dK])
            COPY(qv, dV.t[:, 2048:2048 + nb * 128], sV.t[:, 2048:2048 + nb * 128], [sV], [dV])

        def part1():
            cb_ = st_["d"][0]
            QSb = QSs[par]
            QS = QSb.t[:]
            i2 = pcnt[0] % 2; pcnt[0] += 1
            st_["i2"] = i2
            pS = PS()
            MMG([(pS.t[:, blk * 8:(blk + 1) * 8], cb_.t[:, blk * 128:(blk + 1) * 128], QS[:, b * 8:(b + 1) * 8],
                  blk == 0, True) for blk in range(nb)], [cb_, QSb], [pS])
            ACT(sE[i2].t[:, 0:nb * 8], pS.t[:, 0:nb * 8], AF.Exp, [pS], [sE[i2]], scale=QSCALE)
            TT("vector", sP[i2].t[:, 0:nb * 8], sE[i2].t[:, 0:nb * 8], smk.t[:, 0:nb * 8], ALU.mult,
               [sE[i2], smk], [sP[i2]])
            if nb > 1:
                P.op("vector", (lambda a, bb: (lambda e: e.tensor_reduce(
                    out=a, in_=bb, axis=mybir.AxisListType.X, op=ALU.add)))(
                    sPs[i2].t[:, 0:8], sP[i2].t[:, 0:nb * 8].rearrange("p (a t) -> p t a", t=8)),
                    [sP[i2]], [sPs[i2]])

        def part2():
            cb_ = st_["d"][1]
            i2 = st_["i2"]
            if nb > 1:
                srcs, onesx = sPs[i2], ones_f
            else:
                srcs, onesx = sP[i2], ones_bf
            sden = srcs.t[:, 0:8]
            mms = [(sacc.t[:, b * 8:(b + 1) * 8], cb_.t[:, 2048 + blk * 128: 2048 + (blk + 1) * 128],
                    sP[i2].t[:, blk * 8:(blk + 1) * 8], False, False) for blk in range(nb)]
            mms.append((sacc.t[:, 128 + b * 8:128 + (b + 1) * 8], onesx.t[:], sden, False, False))
            MMG(mms, [cb_, sP[i2], srcs, onesx], [sacc])
        return {"dma": dma, "cast": cast, "p1": part1, "p2": part2}

    def fin_prompt(h):
        for c0 in range(0, MAIN, 512):
            RECIP(rden.t[:], denacc.t[:, c0:c0 + 512], [denacc], [rden])
            TT("vector", attnT.t[:, h, c0:c0 + 512], numacc.t[:, c0:c0 + 512], rden.t[:], ALU.mult,
               [numacc, rden], [attnT])

    def fin_sample(h):
        sacc = saccs[h % 2]
        RECIP(rden.t[:, 0:128], sacc.t[:, 128:256], [sacc], [rden])
        TT("vector", attnT.t[:, h, MAIN:NO], sacc.t[:, 0:128], rden.t[:, 0:128], ALU.mult, [sacc, rden], [attnT])

    for h in range(4):
        for g in range(3):
            items.append(mk_gh(h, g))
    run_stream(items, 0)
    pump(drain=True)
    fin_sample(3)
    ps_n[0] = 6
    P.barrier()
    if kstop <= 2:
        return finish()

    R3 = Region(big, o_hi, ARENA_WORDS)
    hgT = R3.alloc("hgT", [128, 8, NO], BF16)
    o_after_hg = R3.top
    wbf_hg = R3.alloc("wbf_hg", [128, 4096], BF16)
    Kt_ = [R3.alloc("Kt%d" % i, [128, 512], BF16) for i in range(2)]
    Kt2_ = [R3.alloc("Kt2%d" % i, [128, 512], BF16) for i in range(2)]
    Qt_ = [R3.alloc("Qt%d" % i, [128, 512], BF16) for i in range(2)]
    sgT_ = [R3.alloc("sgT%d" % i, [128, 512]) for i in range(2)]
    Vh_ = [R3.alloc("Vh%d" % i, [128, 4, 128], BF16) for i in range(2)]
    Sall = R3.alloc("Sall", [128, 8, 128], BF16)
    Sf = [R3.alloc("Sf%d" % i, [128, 128]) for i in range(2)]
    S0f = R3.alloc("S0f", [128, 16, 128])
    S0b = R3.alloc("S0b", [128, 16, 128], BF16)
    Ktm4_ = [R3.alloc("Ktm4%d" % i, [128, 512], BF16) for i in range(2)]
    Ktmm = [R3.alloc("Ktmm%d" % i, [128, 128], BF16) for i in range(2)]
    hb_base = R3.top
    HB = [[R3.alloc("hB%d_%d" % (i, k), [128, 512]) for k in range(6)] for i in range(2)]
    hg_stage = big[:, hb_base:hb_base + 4096]
    hg_stage_bufs = HB[0] + HB[1][:2]

    def load_hg(hh):
        for bi, cb in enumerate((C_QH, C_FH, C_IH, C_OG)):
            DMA(hg_stage.rearrange("p (a b) -> p a b", b=512)[:, :, bi * 128:(bi + 1) * 128],
                wsrc(w_in, cb + hh * 128, 128), writes=hg_stage_bufs)
        COPY("vector" if hh % 2 else "scalar", wbf_hg.t[:], hg_stage, hg_stage_bufs, [wbf_hg])
        return wbf_hg
    dec_ = [R3.alloc("h_dec%d" % i, [128, 16]) for i in range(2)]
    Am4_ = [R3.alloc("Am4%d" % i, [128, 512], BF16) for i in range(2)]
    osq = R3.alloc("osq", [128, 512]); rstd = R3.alloc("rstd", [128, 512]); otmp = R3.alloc("otmp", [128, 512])
    tcount = [0]
    bcount = [0]

    psB = [psb[4], psb[5], psacc]
    psB_i = [0]

    def PSB():
        b = psB[psB_i[0] % 3]
        psB_i[0] += 1
        return b

    def proj_bank(pb, wb, row, woff, t0, wd):
        MMG([(pb.t[:, 0:wd], wb.t[:, kc * row + woff: kc * row + woff + 128], U(kc, t0, wd), kc == 0, kc == KC - 1)
             for kc in range(KC)], [wb, uTm, uTh], [pb])
        return pb

    def hg_head(hh, wb):
        DMA(S0f.t[:], state_d[:, hh].rearrange("b k v -> k b v"), writes=[S0f])
        issue_copies((3 * nsc + 7) // 8)
        COPY("gpsimd", S0b.t[:], S0f.t[:], [S0f], [S0b])
        st_ = {"sidx": 0}
        MEMSET("vector", Sf[0].t[:], 0.0, [Sf[0]])
        tiles = HALO_TILES + MAIN_TILES + [SAMP_TILE]

        def ctx(ti):
            t0, wd = tiles[ti]
            c = dict(t0=t0, wd=wd, halo=t0 < M0, samp=t0 >= S0, par=ti % 2)
            c["C"] = 8 if c["samp"] else 64
            c["nchk"] = wd // c["C"]
            c["nblk"] = wd // 128
            c["o0"] = t0 - M0
            return c

        def stageA(ti):
            c = ctx(ti)
            t0, wd, par, samp, halo = c["t0"], c["wd"], c["par"], c["samp"], c["halo"]
            B1, B2, B3, B4, B5, B6 = HB[par]
            Vh, sgT = Vh_[par], sgT_[par]
            pF = proj_bank(psb[0], wb, 512, 128, t0, wd)
            pV = psb[1]
            mms = []
            for bi in range(c["nblk"]):
                for kc in range(KC):
                    mms.append((pV.t[:, bi * 128:(bi + 1) * 128], U(kc, t0 + bi * 128, 128),
                                wb.t[:, kc * 512 + 256: kc * 512 + 384], kc == 0, kc == KC - 1))
            MMG(mms, [wb, uTm, uTh], [pV])
            ACT(B1.t[:, 0:wd], pF.t[:, 0:wd], AF.Exp, [pF], [B1], scale=-1.0)
            ACT(B2.t[:, 0:wd], B1.t[:, 0:wd], AF.Ln, [B1, lbT], [B2], scale=lbT.t[:, hh:hh + 1], bias=1.0)
            ACT(B3.t[:, 0:wd], B1.t[:, 0:wd], AF.Ln, [B1], [B3], bias=1.0)
            COPY("scalar", Vh.t[:, 0:c["nblk"], :], pV.t[:, 0:wd].rearrange("p (a b) -> p a b", b=128), [pV], [Vh])
            TT("gpsimd", B2.t[:, 0:wd], B2.t[:, 0:wd], B3.t[:, 0:wd], ALU.subtract, [B2, B3], [B2])
            P.op("vector", (lambda o_, d0, d1: (lambda e: e.tensor_tensor_scan(
                out=o_, data0=d0, data1=d1, initial=0.0, op0=ALU.mult, op1=ALU.add)))(
                B4.t[:, 0:wd], reset.t[:, 1 if samp else 0, 0:wd], B2.t[:, 0:wd]), [reset, B2], [B4])
            if not halo:
                pQ = proj_bank(psb[2], wb, 512, 0, t0, wd)
                pG = proj_bank(psb[3], wb, 512, 384, t0, wd)
                ACT(B5.t[:, 0:wd], pQ.t[:, 0:wd], AF.Exp, [pQ], [B5], scale=-1.0)
                ACT(B5.t[:, 0:wd], B5.t[:, 0:wd], AF.Ln, [B5], [B5], bias=1.0)
                ACT(B5.t[:, 0:wd], B5.t[:, 0:wd], AF.Exp, [B5], [B5], scale=-1.0)
                TT("vector", B5.t[:, 0:wd], pQ.t[:, 0:wd], B5.t[:, 0:wd], ALU.mult, [pQ, B5], [B5])
                ACT(B6.t[:, 0:wd], pG.t[:, 0:wd], AF.Exp, [pG], [B6], scale=-1.0)
                ACT(B6.t[:, 0:wd], B6.t[:, 0:wd], AF.Ln, [B6], [B6], bias=1.0)
                ACT(B6.t[:, 0:wd], B6.t[:, 0:wd], AF.Exp, [B6], [B6], scale=-1.0)
                STT(sgT.t[:, 0:wd], pG.t[:, 0:wd], hgw.t[:, 0:1], B6.t[:, 0:wd], ALU.mult, ALU.mult,
                    [pG, hgw, B6], [sgT])

        def stageB(ti):
            c = ctx(ti)
            wd, par, C, nchk, halo = c["wd"], c["par"], c["C"], c["nchk"], c["halo"]
            B1, B2, B3, B4, B5, B6 = HB[par]
            Kt, Kt2, Qt, dec = Kt_[par], Kt2_[par], Qt_[par], dec_[par]
            ACT(B1.t[:, 0:wd], B3.t[:, 0:wd], AF.Exp, [B3], [B1], scale=-1.0)
            ACT(B6.t[:, 0:wd], B4.t[:, 0:wd], AF.Exp, [B4], [B6], scale=-1.0)
            ACT(dec.t[:, 0:nchk], B4.t[:, C - 1:wd:C], AF.Exp, [B4], [dec])
            if not halo:
                ACT(B2.t[:, 0:wd], B4.t[:, 0:wd], AF.Exp, [B4], [B2])
            TS("vector", B3.t[:, 0:wd], B1.t[:, 0:wd], nlb1.t[:, hh:hh + 1], lb1.t[:, hh:hh + 1], ALU.mult, ALU.add,
               [B1, nlb1, lb1], [B3])
            TT("gpsimd", Kt.t[:, 0:wd], B3.t[:, 0:wd], B6.t[:, 0:wd], ALU.mult, [B3, B6], [Kt])
            TT("gpsimd", Kt2.t[:, 0:wd].rearrange("p (c t) -> p c t", t=C), Kt.t[:, 0:wd].rearrange("p (c t) -> p c t", t=C),
               dec.t[:, 0:nchk].unsqueeze(2).broadcast_to([128, nchk, C]), ALU.mult, [Kt, dec], [Kt2])
            if not halo:
                TT("vector", Qt.t[:, 0:wd], B5.t[:, 0:wd], B2.t[:, 0:wd], ALU.mult, [B5, B2], [Qt])

        def stageC(ti):
            c = ctx(ti)
            t0, wd, par, C, samp, halo, nblk, o0 = c["t0"], c["wd"], c["par"], c["C"], c["samp"], c["halo"], c["nblk"], c["o0"]
            Kt, Kt2, Qt, sgT, Vh, dec = Kt_[par], Kt2_[par], Qt_[par], sgT_[par], Vh_[par], dec_[par]
            Ktm4, Am4 = Ktm4_[par], Am4_[par]
            sidx = st_["sidx"]
            for bl in range(nblk):
                TRANSPOSE(pst.t[:, bl * 128:(bl + 1) * 128], Kt2.t[:, bl * 128:(bl + 1) * 128], ident_bf.t[:],
                          [Kt2, ident_bf], [pst])
            COPY("scalar", Ktm4.t[:, 0:wd], pst.t[:, 0:wd], [pst], [Ktm4])
            if not samp:
                pDs = []
                for cc in range(2):
                    pD = PSB()
                    MMG([(pD.t[:, bl * 128:(bl + 1) * 128], Ktm4.t[cc * 64:(cc + 1) * 64, bl * 128:(bl + 1) * 128],
                          Vh.t[cc * 64:(cc + 1) * 64, bl, :], bl == 0, True) for bl in range(nblk)], [Ktm4, Vh], [pD])
                    pDs.append(pD)
                for ch in range(2 * nblk):
                    if not halo:
                        COPY("gpsimd", Sall.t[:, ch, :], Sf[sidx].t[:], [Sf[sidx]], [Sall])
                    bl, cc = ch // 2, ch % 2
                    pD = pDs[cc]
                    STT(Sf[1 - sidx].t[:], Sf[sidx].t[:], dec.t[:, ch:ch + 1], pD.t[:, bl * 128:(bl + 1) * 128], ALU.mult, ALU.add,
                        [Sf[sidx], dec, pD], [Sf[1 - sidx]])
                    sidx = 1 - sidx
            else:
                for q4 in range(4):
                    pD = PSB()
                    for q in range(4):
                        cc = q4 * 4 + q
                        i2 = cc % 2
                        TS("vector" if cc % 2 else "gpsimd", Ktmm[i2].t[:], Ktm4.t[:, 0:128], oh.t[:, cc:cc + 1], None, ALU.mult, None,
                           [Ktm4, oh], [Ktmm[i2]])
                        MMG([(pD.t[:, q * 128:(q + 1) * 128], Ktmm[i2].t[:], Vh.t[:, 0, :], q == 0, True)], [Ktmm[i2], Vh], [pD])
                    for q in range(4):
                        cc = q4 * 4 + q
                        STT(S0f.t[:, cc, :], S0f.t[:, cc, :], dec.t[:, cc:cc + 1], pD.t[:, q * 128:(q + 1) * 128], ALU.mult, ALU.add,
                            [S0f, dec, pD], [S0f])
            if t0 + wd == M0:
                TS("vector", Sf[1 - sidx].t[:], Sf[sidx].t[:], flag.t[:, 0:1], None, ALU.mult, None,
                   [Sf[sidx], flag], [Sf[1 - sidx]])
                sidx = 1 - sidx
            if t0 + wd == S0:
                DMA(s_out[hh], Sf[sidx].t[:], reads=[Sf[sidx]])
            if samp:
                DMA(state_o[:, hh].rearrange("b k v -> k b v"), S0f.t[:], reads=[S0f])
            st_["sidx"] = sidx
            if halo:
                return
            pA = PSB()
            MMG([(pA.t[:, bl * 128:(bl + 1) * 128], Kt.t[:, bl * 128:(bl + 1) * 128], Qt.t[:, bl * 128:(bl + 1) * 128],
                  bl == 0, True) for bl in range(nblk)], [Kt, Qt], [pA])
            TT("vector", Am4.t[:, 0:wd].rearrange("p (a b) -> p a b", b=128), pA.t[:, 0:wd].rearrange("p (a b) -> p a b", b=128),
               hmask.t[:, (1 if samp else 0):(2 if samp else 1), :].broadcast_to([128, nblk, 128]), ALU.mult, [pA, hmask], [Am4])
            pO = PSB()
            mms = []
            for bl in range(nblk):
                c0 = bl * 128
                mms.append((pO.t[:, c0:c0 + 128], Vh.t[:, bl, :], Am4.t[:, c0:c0 + 128], bl == 0, False))
                if samp:
                    for cc in range(16):
                        mms.append((pO.t[:, cc * 8:(cc + 1) * 8], S0b.t[:, cc, :], Qt.t[:, cc * 8:(cc + 1) * 8], False, cc == 15))
                else:
                    for cc in range(2):
                        mms.append((pO.t[:, c0 + cc * 64:c0 + (cc + 1) * 64], Sall.t[:, bl * 2 + cc, :],
                                    Qt.t[:, c0 + cc * 64:c0 + (cc + 1) * 64], False, cc == 1))
            MMG(mms, [Vh, Am4, S0b, Sall, Qt], [pO])
            ACT(osq.t[:, 0:wd], pO.t[:, 0:wd], AF.Square, [pO], [osq])
            pM = PSB()
            MMG([(pM.t[:, 0:wd], ones_r.t[:], osq.t[:, 0:wd], True, True)], [ones_r, osq], [pM])
            ACT(rstd.t[:, 0:wd], pM.t[:, 0:wd], AF.Ln, [pM], [rstd], bias=RMS_EPS)
            ACT(rstd.t[:, 0:wd], rstd.t[:, 0:wd], AF.Exp, [rstd], [rstd], scale=-0.5)
            TT("vector", otmp.t[:, 0:wd], pO.t[:, 0:wd], rstd.t[:, 0:wd], ALU.mult, [pO, rstd], [otmp])
            TT("gpsimd", hgT.t[:, hh, o0:o0 + wd], otmp.t[:, 0:wd], sgT.t[:, 0:wd], ALU.mult, [otmp, sgT], [hgT])

        n_t = len(tiles)
        stageA(0)
        for ti in range(n_t):
            stageB(ti)
            if ti + 1 < n_t:
                stageA(ti + 1)
            stageC(ti)

    run_stream([((lambda hh=hh: load_hg(hh)), (lambda wb, hh=hh: hg_head(hh, wb))) for hh in range(8)], 0)
    issue_copies(len(copy_jobs))
    P.barrier()
    if kstop <= 3:
        return finish()

    mg_m = Region(big, o_uTh, o_hi).alloc("mg_m", [128, KC, MAIN], BF16)

    def MG(c8, o0, wd):
        return mg_m.t[:, c8, o0:o0 + wd] if o0 < MAIN else mg_s.t[:, c8, 0:wd]

    R4 = Region(big, o_after_hg, ARENA_WORDS)
    rings = Rings(R4, 2, 3)
    ea_ = [R4.alloc("ea%d" % i, [128, 512]) for i in range(2)]
    eb_ = [R4.alloc("eb%d" % i, [128, 512]) for i in range(2)]
    m1_ = [R4.alloc("m1%d" % i, [128, 512]) for i in range(2)]
    p3cnt = [0]

    def p3a_load(j):
        s_ = rings.stage()
        d_ = rings.bf[rings.j % len(rings.bf)]
        rings.j += 1
        for bi, (w_, c_) in enumerate(((w_in, C_GA + j * 128), (w_in, C_GB + j * 128), (w_bb, j * 128))):
            DMA(s_.t[:, 0:3072].rearrange("p (a b) -> p a b", b=384)[:, :, bi * 128:(bi + 1) * 128],
                wsrc(w_, c_, 128), writes=[s_])
        DMA(s_.t[:, 3072:3584].rearrange("p (a b) -> p a b", b=128), wsrc(w_ba, j * 128, 128, kc=4), writes=[s_])
        COPY("scalar" if j % 2 else "vector", d_.t[:, 0:3584], s_.t[:, 0:3584], [s_], [d_])
        return d_

    def p3a_body(j, wb):
        for (t0, wd) in MAIN_TILES + [SAMP_TILE]:
            o0 = t0 - M0
            pa = proj_fm(wb, 384, 0, t0, wd)
            pbb = proj_fm(wb, 384, 128, t0, wd)
            pBA = PS()
            MMG([(pBA.t[:, 0:wd], wb.t[:, 3072 + c4 * 128: 3072 + (c4 + 1) * 128], attnT.t[:, c4, o0:o0 + wd],
                  c4 == 0, c4 == 3) for c4 in range(4)], [wb, attnT], [pBA])
            pBB = PS()
            MMG([(pBB.t[:, 0:wd], wb.t[:, c8 * 384 + 256: c8 * 384 + 384], hgT.t[:, c8, o0:o0 + wd],
                  c8 == 0, c8 == 7) for c8 in range(8)], [wb, hgT], [pBB])
            i2 = p3cnt[0] % 2
            p3cnt[0] += 1
            ea, eb, m1 = ea_[i2], eb_[i2], m1_[i2]
            ACT(ea.t[:, 0:wd], pa.t[:, 0:wd], AF.Exp, [pa], [ea], scale=-1.0)
            ACT(ea.t[:, 0:wd], ea.t[:, 0:wd], AF.Ln, [ea], [ea], bias=1.0)
            ACT(ea.t[:, 0:wd], ea.t[:, 0:wd], AF.Exp, [ea], [ea], scale=-1.0)
            TT("vector", m1.t[:, 0:wd], pBA.t[:, 0:wd], ea.t[:, 0:wd], ALU.mult, [pBA, ea], [m1])
            ACT(eb.t[:, 0:wd], pbb.t[:, 0:wd], AF.Exp, [pbb], [eb], scale=-1.0)
            ACT(eb.t[:, 0:wd], eb.t[:, 0:wd], AF.Ln, [eb], [eb], bias=1.0)
            ACT(eb.t[:, 0:wd], eb.t[:, 0:wd], AF.Exp, [eb], [eb], scale=-1.0)
            TT("vector", eb.t[:, 0:wd], pBB.t[:, 0:wd], eb.t[:, 0:wd], ALU.mult, [pBB, eb], [eb])
            TT("gpsimd", MG(j, o0, wd), m1.t[:, 0:wd], eb.t[:, 0:wd], ALU.add, [m1, eb], [mg_m, mg_s])

    run_stream([((lambda j=j: p3a_load(j)), (lambda wb, j=j: p3a_body(j, wb))) for j in range(KC)], 2)
    P.barrier()
    if kstop <= 4:
        return finish()

    RL = Region(big, o_uTm, o_uTh)
    RH = Region(big, o_hi, ARENA_WORDS)
    hid = RL.alloc("hid", [128, 32, 512], BF16)
    h1 = RL.alloc("h1", [128, KC, 512])
    x1 = RH.alloc("x1", [128, KC, 512])
    u2 = RH.alloc("u2", [128, KC, 512], BF16)
    rings = Rings(RH, 2, 3)
    xres = [RH.alloc("xres%d" % i, [128, 512]) for i in range(2)]
    sq_ = [RH.alloc("sq%d" % i, [128, 512]) for i in range(2)]
    m2_ = RH.alloc("m2", [128, 512]); rs_ = RH.alloc("rs", [128, 512])
    tA = RH.alloc("tA", [128, 512]); tB = RH.alloc("tB", [128, 512]); hs_ = RH.alloc("hs", [128, 512])
    yst = [RH.alloc("yst%d" % i, [128, 512]) for i in range(2)]

    def layer_norm(src, wd, emit_j):
        pM = PS()
        MMG([(pM.t[:, 0:wd], ones_m.t[:], src.t[:, j, 0:wd], j == 0, j == KC - 1) for j in range(KC)], [ones_m, src], [pM])
        pV = PS()
        for j in range(KC):
            s2 = sq_[j % 2]
            ACT(s2.t[:, 0:wd], src.t[:, j, 0:wd], AF.Square, [src], [s2])
            MMG([(pV.t[:, 0:wd], ones_m.t[:], s2.t[:, 0:wd], j == 0, j == KC - 1)], [ones_m, s2], [pV])
        ACT(m2_.t[:, 0:wd], pM.t[:, 0:wd], AF.Square, [pM], [m2_])
        TT("vector", m2_.t[:, 0:wd], pV.t[:, 0:wd], m2_.t[:, 0:wd], ALU.subtract, [pV, m2_], [m2_])
        ACT(rs_.t[:, 0:wd], m2_.t[:, 0:wd], AF.Ln, [m2_], [rs_], bias=LN_EPS)
        ACT(rs_.t[:, 0:wd], rs_.t[:, 0:wd], AF.Exp, [rs_], [rs_], scale=-0.5)
        for j in range(KC):
            TT("vector", tA.t[:, 0:wd], src.t[:, j, 0:wd], pM.t[:, 0:wd], ALU.subtract, [src, pM], [tA])
            TT("gpsimd", tB.t[:, 0:wd], tA.t[:, 0:wd], rs_.t[:, 0:wd], ALU.mult, [tA, rs_], [tB])
            emit_j(j)

    items = []

    def p3b_tile(t0, wd):
        samp = t0 >= S0
        o0 = t0 - M0

        def emit1(j):
            affine("gpsimd", x1.t[:, j, 0:wd], tB.t[:, 0:wd], g1t, j, c1b, j, samp, [tB], [x1], tmp=tA)
            affine("gpsimd", u2.t[:, j, 0:wd], tB.t[:, 0:wd], gA2, j, bA2, j, samp, [tB], [u2], tmp=tA)

        def emit2(j):
            ys = yst[j % 2]
            affine("gpsimd", ys.t[:, 0:wd], tB.t[:, 0:wd], g2t, j, b2t, j, False, [tB], [ys])
            DMA(yT[j, :, o0:o0 + wd], ys.t[:, 0:wd], reads=[ys])

        def wo_body(k, wo_b):
            for j in range(4 * k, 4 * k + 4):
                xr = xres[j % 2]
                DMA(xr.t[:, 0:wd], xT[j, :, t0:t0 + wd], writes=[xr])
                pX = PS()
                MMG([(pX.t[:, 0:wd], wo_b.t[:, c8 * 512 + (j % 4) * 128: c8 * 512 + (j % 4 + 1) * 128], MG(c8, o0, wd),
                      c8 == 0, c8 == 7) for c8 in range(8)], [wo_b, mg_m, mg_s], [pX])
                if not samp:
                    STT(h1.t[:, j, 0:wd], pX.t[:, 0:wd], G1a.t[:, j, 0:1], xr.t[:, 0:wd], ALU.mult, ALU.add,
                        [pX, G1a, xr], [h1])
                else:
                    TT("vector", v3(tA.t[:, 0:wd], 8), v3(pX.t[:, 0:wd], 8), bc(G1a, j, 1, 16, 8), ALU.mult, [pX, G1a], [tA])
                    TT("gpsimd", h1.t[:, j, 0:wd], tA.t[:, 0:wd], xr.t[:, 0:wd], ALU.add, [tA, xr], [h1])
            if k == 1:
                layer_norm(h1, wd, emit1)
        for k in range(2):
            items.append(((lambda k=k: rings.load([(0, KC, 512, wsrc(w_out, k * 512, 512))])),
                          (lambda wb, k=k: wo_body(k, wb))))

        def wu_body(gq, wu):
            for fc in range(4 * gq, 4 * gq + 4):
                pH = PS()
                MMG([(pH.t[:, 0:wd], wu.t[:, kc * 512 + (fc % 4) * 128: kc * 512 + (fc % 4 + 1) * 128], u2.t[:, kc, 0:wd],
                      kc == 0, kc == KC - 1) for kc in range(KC)], [wu, u2], [pH])
                ACT(hs_.t[:, 0:wd], pH.t[:, 0:wd], AF.Identity, [pH, bup], [hs_], bias=bup.t[:, fc:fc + 1])
                STT(hid.t[:, fc, 0:wd], hs_.t[:, 0:wd], 0.0, hs_.t[:, 0:wd], ALU.max, ALU.mult, [hs_], [hid])
        for gq in range(8):
            items.append(((lambda gq=gq: rings.load([(0, KC, 512, wsrc(w_up, gq * 512, 512))])),
                          (lambda wb, gq=gq: wu_body(gq, wb))))

        def wd_body(j, wd_b):
            pF = PS()
            MMG([(pF.t[:, 0:wd], wd_b.t[:, c * 128:(c + 1) * 128], hid.t[:, c, 0:wd], c == 0, c == 31)
                 for c in range(32)], [wd_b, hid], [pF])
            if not samp:
                STT(h1.t[:, j, 0:wd], pF.t[:, 0:wd], G2a.t[:, j, 0:1], x1.t[:, j, 0:wd], ALU.mult, ALU.add,
                    [pF, G2a, x1], [h1])
            else:
                TT("vector", v3(tA.t[:, 0:wd], 8), v3(pF.t[:, 0:wd], 8), bc(G2a, j, 1, 16, 8), ALU.mult, [pF, G2a], [tA])
                TT("gpsimd", h1.t[:, j, 0:wd], tA.t[:, 0:wd], x1.t[:, j, 0:wd], ALU.add, [tA, x1], [h1])
            if j == KC - 1:
                layer_norm(h1, wd, emit2)
        for j in range(KC):
            items.append(((lambda j=j: rings.load([(0, 32, 128, w_down[:, j * 128:(j + 1) * 128].rearrange(
                "(k p) n -> p k n", p=128))])), (lambda wb, j=j: wd_body(j, wb))))

    for (t0, wd) in MAIN_TILES + [SAMP_TILE]:
        p3b_tile(t0, wd)
    run_stream(items, 2)

    return finish()


_NC_CACHE = {}
KSTOP = [99]
DEV = {"ncores": NCORES, "nsc": 16}


def _fm(v, n):
    return np.ascontiguousarray(np.asarray(v, np.float32).reshape(n, 128).T)


def kernel(x_prompt, x_sample, c_prompt, c_sample, cache_kv_w128, cache_kv_w512, cache_kv_w2048,
           state_hgrn, w_ada, b_ada, w_in, lb_param, hg_norm_w, w_branch_a, w_branch_b, w_out,
           ln1_g, ln1_b, w_up, b_up, w_down, b_down, ln2_g, ln2_b):
    f = lambda a: np.asarray(a, np.float32)
    x_prompt, x_sample, c_prompt, c_sample = f(x_prompt), f(x_sample), f(c_prompt), f(c_sample)
    caches = [f(cache_kv_w128), f(cache_kv_w512), f(cache_kv_w2048)]
    state_hgrn = f(state_hgrn)
    tabs = const_tables()
    common = dict(
        w_ada=np.ascontiguousarray(f(w_ada)[0]), b_ada=_fm(f(b_ada)[0], 48), w_in=np.ascontiguousarray(f(w_in)[0]),
        lbp=np.ascontiguousarray(f(lb_param).reshape(2, 8, 128).transpose(2, 0, 1)),
        hgw=np.ascontiguousarray(f(hg_norm_w)[0].reshape(128, 1)),
        w_ba=np.ascontiguousarray(f(w_branch_a)[0]), w_bb=np.ascontiguousarray(f(w_branch_b)[0]),
        w_out=np.ascontiguousarray(f(w_out)[0]),
        ln1g=_fm(f(ln1_g)[0], 8), ln1b=_fm(f(ln1_b)[0], 8), ln2g=_fm(f(ln2_g)[0], 8), ln2b=_fm(f(ln2_b)[0], 8),
        w_up=np.ascontiguousarray(f(w_up)[0]), bup=_fm(f(b_up)[0], 32),
        w_down=np.ascontiguousarray(f(w_down)[0]), bdown=_fm(f(b_down)[0], 8),
        masks=tabs["masks"], smask=tabs["smask"], nmask=tabs["nmask"], hmask=tabs["hmask"],
        reset=tabs["reset"], oh=tabs["oh"], ident=np.eye(128, dtype=np.float32),
    )
    in_maps = []
    for c in range(NCORES):
        b, half = c // 2, c % 2
        T0 = half * MAIN
        halo = x_prompt[b, T0 - HALO:T0] if half == 1 else np.zeros((HALO, D), np.float32)
        toks = np.concatenate([halo, x_prompt[b, T0:T0 + MAIN], x_sample[16 * c:16 * c + 16].reshape(SAMP, D)], axis=0)
        cs = np.concatenate([c_prompt[b:b + 1], c_sample[16 * c:16 * c + 16]], axis=0)
        m = dict(common)
        m["xT"] = np.ascontiguousarray(toks.T.reshape(KC, 128, NT))
        m["cT"] = np.ascontiguousarray(cs.T.reshape(KC, 128, 17).transpose(1, 0, 2))
        m["flag"] = np.full((128, 1), float(half), np.float32)
        for (w, _), cache in zip(GROUPS, caches):
            nsc = DEV["nsc"]
            cc = cache[0, 16 * c:16 * c + nsc]
            m["kv%d" % w] = np.ascontiguousarray(cc.reshape(nsc, w, 1024))
            m["kT%d" % w] = np.ascontiguousarray(cc[:, :, 0].transpose(0, 2, 3, 1))
        m["state"] = np.ascontiguousarray(state_hgrn[0, 16 * c:16 * c + 16])
        in_maps.append(m)
    if "nc" not in _NC_CACHE:
        _NC_CACHE["nc"] = build_nc(KSTOP[0], DEV["nsc"])
    ncr = DEV["ncores"]
    if DEV.get("trace"):
        res = run_bass_kernel_spmd(_NC_CACHE["nc"], in_maps[:ncr], core_ids=list(range(ncr)), trace=True)
        print("DEV exec_time_ns", res.exec_time_ns, flush=True)
    else:
        res = run_bass_kernel_spmd(_NC_CACHE["nc"], in_maps[:ncr], core_ids=list(range(ncr)))
    R = res.results
    nsc = DEV["nsc"]
    B, S = x_prompt.shape[0], x_prompt.shape[1]
    y_prompt = np.empty((B, S, D), np.float32)
    y_sample = np.empty((128, 8, D), np.float32)
    pk = [np.empty((1, B, w, 2, 4, 128), np.float32) for (w, _) in GROUPS]
    php = np.empty((1, B, 8, 128, 128), np.float32)
    sk = [np.empty((1, 128, w, 2, 4, 128), np.float32) for (w, _) in GROUPS]
    shs = np.empty((1, 128, 8, 128, 128), np.float32)
    for c in range(ncr):
        b, half = c // 2, c % 2
        T0 = half * MAIN
        yt = np.asarray(R[c]["yT"]).reshape(D, NO)
        y_prompt[b, T0:T0 + MAIN] = yt[:, :MAIN].T
        y_sample[16 * c:16 * c + 16] = yt[:, MAIN:].T.reshape(16, 8, D)
        for gi, (w, _) in enumerate(GROUPS):
            sk[gi][0, 16 * c:16 * c + nsc] = np.asarray(R[c]["kvo%d" % w]).reshape(nsc, w, 2, 4, 128)
            if half == 1:
                pk[gi][0, b, :, 0] = np.asarray(R[c]["kTo%d" % w]).transpose(2, 0, 1)
                pk[gi][0, b, :, 1] = np.asarray(R[c]["vo%d" % w])
        if half == 1:
            php[0, b] = np.asarray(R[c]["s_out"])
        shs[0, 16 * c:16 * c + 16] = np.asarray(R[c]["state_o"])
    return (y_prompt, y_sample, pk[0], pk[1], pk[2], php, sk[0], sk[1], sk[2], shs)
```

```python
import contextlib
import numpy as np
import concourse.bass as bass
import concourse.mybir as mybir
from concourse.bass_utils import run_bass_kernel_spmd

F32 = mybir.dt.float32
BF16 = mybir.dt.bfloat16
ALU = mybir.AluOpType
AF = mybir.ActivationFunctionType

SAME_ENGINE_SYNC = "raw"


def _need_sync(p, o, d):
    if p.is_dma or p.q != o.q:
        return True
    if p.q == "tensor":
        return False
    if SAME_ENGINE_SYNC is True:
        return True
    if SAME_ENGINE_SYNC == "raw":
        return d in o.raw
    return False
COMPUTE = ("tensor", "vector", "scalar", "gpsimd")

NCORES = 8
D = 1024
KC = 8
HALO = 2048
MAIN = 2048
SAMP = 128
NT = HALO + MAIN + SAMP
M0 = HALO
S0 = HALO + MAIN
NO = MAIN + SAMP
GROUPS = ((128, 1), (512, 4), (2048, 16))
ALPHA = 2.0 ** 0.25
LN_EPS = 1e-5 / (ALPHA * ALPHA)
RMS_EPS = 1e-6
QSCALE = 128.0 ** -0.5
C_Q, C_K, C_V = 0, 1536, 3072
C_QH, C_FH, C_IH, C_OG = 4608, 5632, 6656, 7680
C_GA, C_GB = 8704, 9728


def slope(g, h):
    return 2.0 ** (-8.0 * (g * 4 + h + 1) / 12.0)


class Buf:
    __slots__ = ("name", "t", "last_w", "readers", "excl", "co_w", "w_deps")

    def __init__(self, name, t=None, excl=False):
        self.name = name
        self.t = t
        self.excl = excl
        self.last_w = None
        self.readers = []
        self.co_w = []
        self.w_deps = None

    def __getitem__(self, k):
        return self.t[k]


class Op:
    __slots__ = ("q", "fn", "deps", "raw", "is_dma", "signal", "sigval", "sem", "idx", "prev")

    def __init__(self, q, fn, is_dma):
        self.q = q
        self.fn = fn
        self.deps = set()
        self.raw = set()
        self.is_dma = is_dma
        self.signal = False
        self.sigval = None
        self.sem = None
        self.prev = 0


class Prog:
    def __init__(self, nc):
        self.nc = nc
        self.ops = []

    def op(self, q, fn, reads=(), writes=(), dma=False):
        o = Op(q, fn, dma)
        o.idx = len(self.ops)
        for b in reads:
            if b.last_w is not None:
                o.deps.add(b.last_w)
                o.raw.add(b.last_w)
                for c in b.co_w:
                    o.deps.add(c)
                    o.raw.add(c)
            if b.excl:
                for r in b.readers:
                    if self.ops[r].q != q:
                        o.deps.add(r)
        cow = {}
        for b in writes:
            if (dma and b.last_w is not None and not b.readers and not b.excl and b.w_deps is not None
                    and self.ops[b.last_w].is_dma and b not in reads):
                o.deps |= b.w_deps
                cow[id(b)] = True
            else:
                wd = set(b.readers)
                if b.last_w is not None:
                    wd.add(b.last_w)
                    wd.update(b.co_w)
                o.deps |= wd
                cow[id(b)] = wd
        for b in reads:
            b.readers.append(o.idx)
        for b in writes:
            if cow[id(b)] is True:
                b.co_w.append(b.last_w)
            else:
                b.co_w = []
                b.w_deps = cow[id(b)]
            b.last_w = o.idx
            b.readers = []
        o.deps.discard(o.idx)
        self.ops.append(o)
        return o

    def barrier(self):
        self.ops.append(None)

    def emit(self, final_wait_q="sync"):
        nc = self.nc
        ops = self.ops
        lastq = {}
        for o in ops:
            if o is None:
                for q_, lo in lastq.items():
                    lo.signal = True
                continue
            if not o.is_dma:
                lastq[o.q] = o
        for o in ops:
            if o is None:
                continue
            for d in o.deps:
                p = ops[d]
                if _need_sync(p, o, d):
                    p.signal = True
        for o in ops:
            if o is not None and o.is_dma:
                o.signal = True
        stack = contextlib.ExitStack()
        qsem = {q: stack.enter_context(nc.semaphore("s_" + q)) for q in COMPUTE}
        pool_sizes = {"sync": 16, "gpsimd": 8, "scalar": 8}
        dpool = {q: [stack.enter_context(nc.semaphore("d_%s%d" % (q, i))) for i in range(n)]
                 for q, n in pool_sizes.items()}
        dcount = {q: [0] * n for q, n in pool_sizes.items()}
        dnext = {q: 0 for q in pool_sizes}
        qcount = {q: 0 for q in COMPUTE}
        snaps = {}
        for oi, o in enumerate(ops):
            if o is None:
                snaps[oi] = (dict(qcount), {q: list(v) for q, v in dcount.items()})
                continue
            if not o.signal:
                continue
            if o.is_dma:
                i = dnext[o.q]
                dnext[o.q] = (i + 1) % pool_sizes[o.q]
                o.sem = dpool[o.q][i]
                o.prev = dcount[o.q][i]
                dcount[o.q][i] += 16
                o.sigval = dcount[o.q][i]
            else:
                o.sem = qsem[o.q]
                qcount[o.q] += 1
                o.sigval = qcount[o.q]
        with nc.Block() as block:
            def make(qname):
                def body(eng):
                    known = {}
                    for oi, o in enumerate(ops):
                        if o is None:
                            qc, dc = snaps[oi]
                            for q2 in COMPUTE:
                                if qc[q2] > known.get(id(qsem[q2]), 0):
                                    eng.wait_ge(qsem[q2], qc[q2])
                                    known[id(qsem[q2])] = qc[q2]
                            for q2, n in pool_sizes.items():
                                for i in range(n):
                                    if dc[q2][i] > known.get(id(dpool[q2][i]), 0):
                                        eng.wait_ge(dpool[q2][i], dc[q2][i])
                                        known[id(dpool[q2][i])] = dc[q2][i]
                            continue
                        if o.q != qname:
                            continue
                        for d in sorted(o.deps):
                            p = ops[d]
                            if not p.signal:
                                continue
                            if not _need_sync(p, o, d):
                                continue
                            key = id(p.sem)
                            if known.get(key, 0) >= p.sigval:
                                continue
                            eng.wait_ge(p.sem, p.sigval)
                            known[key] = p.sigval
                        if o.is_dma and o.prev > 0 and known.get(id(o.sem), 0) < o.prev:
                            eng.wait_ge(o.sem, o.prev)
                            known[id(o.sem)] = o.prev
                        inst = o.fn(eng)
                        if o.signal:
                            inst.then_inc(o.sem, 16 if o.is_dma else 1)
                    if qname == final_wait_q:
                        for q2, n in pool_sizes.items():
                            for i in range(n):
                                if dcount[q2][i] > 0:
                                    eng.wait_ge(dpool[q2][i], dcount[q2][i])
                        for q in COMPUTE:
                            if qcount[q] > 0:
                                eng.wait_ge(qsem[q], qcount[q])
                return body
            block.sync(make("sync"))
            block.scalar(make("scalar"))
            block.vector(make("vector"))
            block.gpsimd(make("gpsimd"))
            block.tensor(make("tensor"))
        stack.close()


def const_tables():
    j = np.arange(128)[:, None].astype(np.float64)
    i = np.arange(128)[None, :].astype(np.float64)
    masks = np.zeros((12, 128, 256), np.float32)
    smask = np.zeros((12, 128, 128), np.float32)
    nmask = np.zeros((12, 128, 128), np.float32)
    for g, (win, dil) in enumerate(GROUPS):
        for h in range(4):
            sl = slope(g, h)
            prev = np.where(i <= j, np.exp(-sl * dil * (128 + i - j)), 0.0)
            own = np.where(i >= j, np.exp(-sl * dil * (i - j)), 0.0)
            masks[g * 4 + h, :, :128] = prev
            masks[g * 4 + h, :, 128:] = own
            nb = win // 128
            p = np.arange(128)[:, None]
            for blk in range(nb):
                t = np.arange(8)[None, :]
                idx = blk * 128 + p
                diff = win + t - idx
                ok = (diff % dil == 0) & (diff >= 0) & (diff <= win)
                smask[g * 4 + h, :, blk * 8:(blk + 1) * 8] = np.where(ok, np.exp(-sl * diff), 0.0)
            kp = np.arange(128)[:, None]
            qp = np.arange(128)[None, :]
            dj = (qp % 8) - (kp % 8)
            ok = (kp // 8 == qp // 8) & (dj >= 0) & (dj % dil == 0)
            nmask[g * 4 + h] = np.where(ok, np.exp(-sl * dj), 0.0)
    s = np.arange(128)[:, None]
    t = np.arange(128)[None, :]
    hmask = np.zeros((2, 128, 128), np.float32)
    hmask[0] = ((s // 64 == t // 64) & (s <= t)).astype(np.float32)
    hmask[1] = ((s // 8 == t // 8) & (s <= t)).astype(np.float32)
    reset = np.ones((2, 128, 512), np.float32)
    reset[0, :, ::64] = 0.0
    reset[1, :, ::8] = 0.0
    oh = (np.arange(128)[:, None] // 8 == np.arange(16)[None, :]).astype(np.float32)
    return dict(masks=masks, smask=smask, nmask=nmask, hmask=hmask, reset=reset, oh=oh)


ARENA_WORDS = 53100


class Region:
    def __init__(self, big, lo, hi):
        self.big, self.lo, self.hi, self.top = big, lo, hi, lo

    def alloc(self, name, shape, dt=F32):
        assert shape[0] == 128
        n = 1
        for s_ in shape[1:]:
            n *= s_
        words = n if dt == F32 else (n + 1) // 2
        words = (words + 7) // 8 * 8
        assert self.top + words <= self.hi, (name, self.top, words, self.hi)
        ap = self.big[:, self.top:self.top + words]
        if dt != F32:
            ap = ap.bitcast(dt)
        ap = ap[:, 0:n]
        if len(shape) == 3:
            ap = ap.rearrange("p (a b) -> p a b", b=shape[2])
        self.top += words
        return Buf(name, ap)


def build_nc(kstop=99, nsc=16):
    nc = bass.Bass("TRN2", target_bir_lowering=False)
    P = Prog(nc)
    st = contextlib.ExitStack()

    def finish():
        P.emit()
        st.close()
        return nc

    def din(name, shape):
        return nc.dram_tensor(name, list(shape), F32, kind="ExternalInput").ap()

    def dout(name, shape):
        return nc.dram_tensor(name, list(shape), F32, kind="ExternalOutput").ap()

    xT = din("xT", [KC, 128, NT])
    cT = din("cT", [128, KC, 17])
    flag_d = din("flag", [128, 1])
    w_ada = din("w_ada", [D, 6144])
    b_ada = din("b_ada", [128, 48])
    w_in = din("w_in", [D, 10752])
    lbp = din("lbp", [128, 2, 8])
    hgw_d = din("hgw", [128, 1])
    w_ba = din("w_ba", [512, D])
    w_bb = din("w_bb", [D, D])
    w_out = din("w_out", [D, D])
    ln1g_d = din("ln1g", [128, 8]); ln1b_d = din("ln1b", [128, 8])
    ln2g_d = din("ln2g", [128, 8]); ln2b_d = din("ln2b", [128, 8])
    w_up = din("w_up", [D, 4096]); bup_d = din("bup", [128, 32])
    w_down = din("w_down", [4096, D]); bdown_d = din("bdown", [128, 8])
    masks_d = din("masks", [12, 128, 256])
    smask_d = din("smask", [12, 128, 128])
    nmask_d = din("nmask", [12, 128, 128])
    hmask_d = din("hmask", [2, 128, 128])
    reset_d = din("reset", [2, 128, 512])
    oh_d = din("oh", [128, 16])
    ident_d = din("ident", [128, 128])
    kvc = [din("kv%d" % w, [nsc, w, 1024]) for (w, _) in GROUPS]
    kTc = [din("kT%d" % w, [nsc, 4, 128, w]) for (w, _) in GROUPS]
    state_d = din("state", [16, 8, 128, 128])

    yT = dout("yT", [KC, 128, NO])
    kTo = [dout("kTo%d" % w, [4, 128, w]) for (w, _) in GROUPS]
    vo = [dout("vo%d" % w, [w, 4, 128]) for (w, _) in GROUPS]
    s_out = dout("s_out", [8, 128, 128])
    kvo = [dout("kvo%d" % w, [nsc, w, 1024]) for (w, _) in GROUPS]
    state_o = dout("state_o", [16, 8, 128, 128])

    big = st.enter_context(nc.sbuf_tensor("big", [128, ARENA_WORDS], F32))

    def psum(name, shape, dt=F32):
        return Buf(name, st.enter_context(nc.psum_tensor(name, list(shape), dt)), excl=True)

    def DMA(out, in_, reads=(), writes=(), q="sync"):
        P.op(q, lambda e: e.dma_start(out=out, in_=in_), reads, writes, dma=True)

    def ACT(out, in_, func, reads, writes, scale=None, bias=None):
        kw = {}
        if scale is not None:
            kw["scale"] = scale
        if bias is not None:
            kw["bias"] = bias
        P.op("scalar", lambda e: e.activation(out=out, in_=in_, func=func, **kw), reads, writes)

    def TT(q, out, in0, in1, op, reads, writes):
        P.op(q, lambda e: e.tensor_tensor(out=out, in0=in0, in1=in1, op=op), reads, writes)

    def TS(q, out, in0, s1, s2, op0, op1, reads, writes):
        if op1 is None:
            P.op(q, lambda e: e.tensor_scalar(out=out, in0=in0, scalar1=s1, scalar2=None, op0=op0), reads, writes)
        else:
            P.op(q, lambda e: e.tensor_scalar(out=out, in0=in0, scalar1=s1, scalar2=s2, op0=op0, op1=op1), reads, writes)

    def STT(out, in0, scalar, in1, op0, op1, reads, writes):
        P.op("vector", lambda e: e.scalar_tensor_tensor(out=out, in0=in0, scalar=scalar, in1=in1, op0=op0, op1=op1),
             reads, writes)

    def COPY(q, out, in_, reads, writes):
        if q == "scalar":
            P.op(q, lambda e: e.activation(out=out, in_=in_, func=AF.Copy), reads, writes)
        else:
            P.op(q, lambda e: e.tensor_copy(out=out, in_=in_), reads, writes)

    def RECIP(out, in_, reads, writes):
        P.op("vector", lambda e: e.reciprocal(out=out, in_=in_), reads, writes)

    def MEMSET(q, out, val, writes):
        P.op(q, lambda e: e.memset(out, val), (), writes)

    def MMG(mms, reads, writes):
        mms = list(mms)

        def fn(e):
            inst = None
            for (o, l, r, s_, p_) in mms:
                inst = e.matmul(o, lhsT=l, rhs=r, start=s_, stop=p_, skip_group_check=True)
            return inst
        P.op("tensor", fn, reads, writes)

    def TRANSPOSE(out, in_, ident, reads, writes):
        P.op("tensor", lambda e: e.transpose(out, in_, ident), reads, writes)

    psb = [psum("ps%d" % i, [128, 512]) for i in range(6)]
    psacc = psum("psacc", [128, 512])
    pst = psum("pst", [128, 1024], BF16)
    ps_i = [0]
    ps_n = [6]

    def PS():
        b = psb[ps_i[0] % ps_n[0]]
        ps_i[0] += 1
        return b

    RA = Region(big, 0, 4608)
    o_uTm = 4608
    o_attn = o_uTm + 8704
    o_uTh = o_attn + 4352
    o_hi = o_uTh + 8192
    RB = Region(big, o_uTm, o_hi)
    uTm = RB.alloc("uTm", [128, KC, NO], BF16)
    attnT = RB.alloc("attnT", [128, 4, NO], BF16)
    uTh = RB.alloc("uTh", [128, KC, HALO], BF16)
    assert RB.top == o_hi

    def U(kc, t0, n, step=1):
        if t0 < M0:
            assert t0 + (n - 1) * step < M0
            return uTh.t[:, kc, t0: t0 + (n - 1) * step + 1: step]
        a = t0 - M0
        return uTm.t[:, kc, a: a + (n - 1) * step + 1: step]

    def sb(name, shape, dt=F32):
        return RA.alloc(name, shape, dt)

    modT = sb("modT", [128, 48, 17])
    A1 = sb("A1", [128, 8, 17]); A2 = sb("A2", [128, 8, 17])
    G1a = sb("G1a", [128, 8, 17]); G2a = sb("G2a", [128, 8, 17])
    c1b = sb("c1b", [128, 8, 17]); g1t = sb("g1t", [128, 8, 17])
    gA2 = sb("gA2", [128, 8, 17]); bA2 = sb("bA2", [128, 8, 17])
    g2t = sb("g2t", [128, 8, 17]); b2t = sb("b2t", [128, 8, 17])
    flag = sb("flag_s", [128, 1])
    lbT = sb("lbT", [128, 8]); lb1 = sb("lb1", [128, 8]); nlb1 = sb("nlb1", [128, 8])
    hgw = sb("hgw_s", [128, 1])
    ln1g = sb("ln1g_s", [128, 8]); ln1b = sb("ln1b_s", [128, 8])
    ln2g = sb("ln2g_s", [128, 8]); ln2b = sb("ln2b_s", [128, 8])
    bup = sb("bup_s", [128, 32]); bdown = sb("bdown_s", [128, 8])
    ones_bf = sb("ones_bf", [128, 128], BF16)
    ones_m = sb("ones_m", [128, 128])
    ones_r = sb("ones_r", [128, 128])
    ident_bf = sb("ident_bf", [128, 128], BF16)
    hmask = sb("hmask_s", [128, 2, 128])
    reset = sb("reset_s", [128, 2, 512])
    oh = sb("oh_s", [128, 16])
    mg_s = sb("mg_s", [128, KC, 128], BF16)
    ones_f = sb("ones_f", [128, 128])

    class Rings:
        def __init__(self, R, nst, nbf):
            self.st = [R.alloc("wst%d" % i, [128, 4096]) for i in range(nst)]
            self.bf = [R.alloc("wbf%d" % i, [128, 4096], BF16) for i in range(nbf)]
            self.i = 0
            self.j = 0
            self.subs = {}

        def W(self, s_):
            return [s_] + list(self.subs.get(id(s_), ()))

        def stage(self):
            s_ = self.st[self.i % len(self.st)]
            self.i += 1
            return s_

        def load(self, srcs, castq=None):
            s_ = self.stage()
            d_ = self.bf[self.j % len(self.bf)]
            self.j += 1
            tot = 0
            for (off, a, b, ap) in srcs:
                DMA(s_.t[:, off:off + a * b].rearrange("p (a b) -> p a b", b=b), ap, writes=self.W(s_))
                tot = max(tot, off + a * b)
            q = castq or ("scalar" if (self.j % 2 == 0) else "vector")
            COPY(q, d_.t[:, 0:tot], s_.t[:, 0:tot], self.W(s_), [d_])
            return d_

    def run_stream(items, lookahead):
        loaded = {}
        n = len(items)
        for i in range(min(lookahead, n)):
            loaded[i] = items[i][0]()
        for i in range(n):
            if i + lookahead < n:
                loaded[i + lookahead] = items[i + lookahead][0]()
            items[i][1](loaded.pop(i))
        assert not loaded

    def wsrc(w, c0, ncols, kc=KC, r0=0):
        return w[r0:r0 + kc * 128, c0:c0 + ncols].rearrange("(k p) n -> p k n", p=128)

    def wblocks(w, cols, kc=KC, width=128):
        nb_ = len(cols)
        return [(0, None, None, None)] and [
            (bi, c) for bi, c in enumerate(cols)], nb_ * width

    def load_blocks(rings, w, cols, kc=KC, width=128, castq=None):
        s_ = rings.stage()
        d_ = rings.bf[rings.j % len(rings.bf)]
        rings.j += 1
        row = len(cols) * width
        tot = kc * row
        for bi, c in enumerate(cols):
            DMA(s_.t[:, 0:tot].rearrange("p (a b) -> p a b", b=row)[:, :, bi * width:(bi + 1) * width],
                wsrc(w, c, width, kc=kc), writes=rings.W(s_))
        q = castq or ("scalar" if (rings.j % 2 == 0) else "vector")
        COPY(q, d_.t[:, 0:tot], s_.t[:, 0:tot], rings.W(s_), [d_])
        return d_

    def proj_fm(wb, row, woff, t0, wd):
        pb = PS()
        MMG([(pb.t[:, 0:wd], wb.t[:, kc * row + woff: kc * row + woff + 128], U(kc, t0, wd), kc == 0, kc == KC - 1)
             for kc in range(KC)], [wb, uTm, uTh], [pb])
        return pb

    def bc(tab, j, lo, n, rep):
        return tab.t[:, j, lo:lo + n].unsqueeze(2).broadcast_to([128, n, rep])

    def v3(ap, rep):
        return ap.rearrange("p (s t) -> p s t", t=rep)

    def affine(q2, out, in_, stab, sj, btab, bj, sample, reads, writes, tmp=None):
        if not sample:
            ACT(out, in_, AF.Identity, list(reads) + [stab, btab], writes, scale=stab.t[:, sj, 0:1], bias=btab.t[:, bj, 0:1])
        else:
            TT("vector", v3(tmp.t[:, 0:128], 8), v3(in_, 8), bc(stab, sj, 1, 16, 8), ALU.mult, list(reads) + [stab], [tmp])
            TT(q2, v3(out, 8), v3(tmp.t[:, 0:128], 8), bc(btab, bj, 1, 16, 8), ALU.add, [tmp, btab], writes)

    MAIN_TILES = [(M0 + i * 512, 512) for i in range(4)]
    HALO_TILES = [(i * 512, 512) for i in range(4)]
    SAMP_TILE = (S0, 128)

    R0 = Region(big, o_hi, ARENA_WORDS)
    for (dst, src) in ((flag, flag_d), (hgw, hgw_d), (ln1g, ln1g_d), (ln1b, ln1b_d), (ln2g, ln2g_d),
                       (ln2b, ln2b_d), (bup, bup_d), (bdown, bdown_d), (oh, oh_d)):
        DMA(dst.t[:], src, writes=[dst])
    DMA(hmask.t[:], hmask_d.rearrange("a p n -> p a n"), writes=[hmask])
    DMA(reset.t[:], reset_d.rearrange("a p n -> p a n"), writes=[reset])
    MEMSET("vector", ones_bf.t[:], 1.0, [ones_bf])
    MEMSET("vector", ones_m.t[:], 1.0 / 1024.0, [ones_m])
    MEMSET("vector", ones_r.t[:], 1.0 / 128.0, [ones_r])
    MEMSET("vector", ones_f.t[:], 1.0, [ones_f])
    identf = R0.alloc("identf", [128, 128])
    DMA(identf.t[:], ident_d, writes=[identf])
    COPY("vector", ident_bf.t[:], identf.t[:], [identf], [ident_bf])

    kvo_bufs = [Buf("kvo%d" % g) for g in range(3)]
    copy_jobs = []
    for b in range(nsc):
        for g, (w, _) in enumerate(GROUPS):
            copy_jobs.append((g, b, w))

    def issue_copies(n):
        for _ in range(n):
            if copy_jobs:
                g, b, w = copy_jobs.pop(0)
                DMA(kvo[g][b, 0:w - 8, :], kvc[g][b, 8:w, :], q="sync")

    lbs = R0.alloc("lbs", [128, 2, 8])
    DMA(lbs.t[:], lbp, writes=[lbs])
    TT("vector", lbT.t[:], lbs.t[:, 1, :], lbs.t[:, 0, :], ALU.subtract, [lbs], [lbT])
    ACT(lbT.t[:], lbT.t[:], AF.Exp, [lbT], [lbT])
    TS("vector", lbT.t[:], lbT.t[:], 1.0, None, ALU.add, None, [lbT], [lbT])
    RECIP(lbT.t[:], lbT.t[:], [lbT], [lbT])
    TS("vector", lb1.t[:], lbT.t[:], -1.0, 1.0, ALU.mult, ALU.add, [lbT], [lb1])
    TS("vector", nlb1.t[:], lb1.t[:], -1.0, None, ALU.mult, None, [lb1], [nlb1])

    scT = R0.alloc("scT", [128, KC, 17])
    sct = R0.alloc("sct", [128, KC, 17])
    DMA(scT.t[:], cT, writes=[scT])
    ACT(sct.t[:], scT.t[:], AF.Exp, [scT], [sct], scale=-1.0)
    TS("vector", sct.t[:], sct.t[:], 1.0, None, ALU.add, None, [sct], [sct])
    RECIP(sct.t[:], sct.t[:], [sct], [sct])
    TT("vector", scT.t[:], scT.t[:], sct.t[:], ALU.mult, [scT, sct], [scT])
    bada = R0.alloc("bada", [128, 48])
    DMA(bada.t[:], b_ada, writes=[bada])
    wada = [R0.alloc("wada%d" % i, [128, 4096]) for i in range(2)]
    for ng in range(12):
        s_ = wada[ng % 2]
        DMA(s_.t[:, 0:4096].rearrange("p (a b) -> p a b", b=512), wsrc(w_ada, ng * 512, 512), writes=[s_])
        pb = PS()
        mms = []
        for jj in range(4):
            for kc in range(KC):
                mms.append((pb.t[:, jj * 17:(jj + 1) * 17], s_.t[:, kc * 512 + jj * 128: kc * 512 + (jj + 1) * 128],
                            scT.t[:, kc, :], kc == 0 and jj == 0, kc == KC - 1))
        MMG(mms, [s_, scT], [pb])
        for jj in range(4):
            j = ng * 4 + jj
            TS("vector", modT.t[:, j, :], pb.t[:, jj * 17:(jj + 1) * 17], bada.t[:, j:j + 1], None, ALU.add, None,
               [pb, bada], [modT])
    TS("vector", A1.t[:], modT.t[:, 8:16, :], 1.0, None, ALU.add, None, [modT], [A1])
    TS("vector", A2.t[:], modT.t[:, 32:40, :], 1.0, None, ALU.add, None, [modT], [A2])
    TS("vector", G1a.t[:], modT.t[:, 16:24, :], 1.0 / ALPHA, None, ALU.mult, None, [modT], [G1a])
    TS("vector", G2a.t[:], modT.t[:, 40:48, :], 1.0 / ALPHA, None, ALU.mult, None, [modT], [G2a])

    def bcp(v):
        return v.t[:].unsqueeze(2).broadcast_to([128, 8, 17])
    TS("vector", g1t.t[:], A1.t[:], 0.0, None, ALU.mult, None, [A1], [g1t])
    TT("vector", g1t.t[:], g1t.t[:], bcp(ln1g), ALU.add, [g1t, ln1g], [g1t])
    TT("vector", c1b.t[:], G2a.t[:], bcp(bdown), ALU.mult, [G2a, bdown], [c1b])
    TT("vector", c1b.t[:], c1b.t[:], bcp(ln1b), ALU.add, [c1b, ln1b], [c1b])
    TT("vector", gA2.t[:], A2.t[:], bcp(ln1g), ALU.mult, [A2, ln1g], [gA2])
    TT("vector", bA2.t[:], A2.t[:], bcp(ln1b), ALU.mult, [A2, ln1b], [bA2])
    TT("vector", bA2.t[:], bA2.t[:], modT.t[:, 24:32, :], ALU.add, [bA2, modT], [bA2])
    TS("vector", g2t.t[:], A1.t[:], 0.0, None, ALU.mult, None, [A1], [g2t])
    TT("vector", b2t.t[:], g2t.t[:], bcp(ln2b), ALU.add, [g2t, ln2b], [b2t])
    TT("vector", g2t.t[:], g2t.t[:], bcp(ln2g), ALU.add, [g2t, ln2g], [g2t])

    xst = [R0.alloc("xst%d" % i, [128, NT]) for i in range(2)]
    tmpA = R0.alloc("tmpA", [128, 512])
    for kc in range(KC):
        xs = xst[kc % 2]
        DMA(xs.t[:, 0:2048], xT[kc, :, 0:2048], writes=[xs])
        DMA(xs.t[:, 2048:NT], xT[kc, :, 2048:NT], writes=[xs])
        for t0 in range(0, S0, 1024):
            affine("gpsimd", U(kc, t0, 1024), xs.t[:, t0:t0 + 1024], A1, kc, modT, kc, False, [xs], [uTm, uTh])
        affine("gpsimd", U(kc, S0, 128), xs.t[:, S0:NT], A1, kc, modT, kc, True, [xs], [uTm], tmp=tmpA)
    P.barrier()
    if kstop <= 1:
        return finish()

    R2 = Region(big, o_hi, ARENA_WORDS)
    rings = Rings(R2, 2, 1)
    rings_c = Rings(R2, 0, 2)
    rings_c.stage = rings.stage
    for s_ in rings.st:
        rings.subs[id(s_)] = (Buf(s_.name + "K", s_.t), Buf(s_.name + "V", s_.t))
    cslots = [(Buf(d_.name + "K", d_.t), Buf(d_.name + "V", d_.t)) for d_ in rings_c.bf]
    QT = R2.alloc("QT", [128, NO], BF16)
    QSs = [R2.alloc("QSs%d" % i, [128, 128], BF16) for i in range(2)]
    KT = R2.alloc("KT", [128, NT], BF16)
    Vb = R2.alloc("Vb", [128, 33, 128], BF16)
    Vs = R2.alloc("Vs", [128, 128], BF16)
    numacc = R2.alloc("numacc", [128, MAIN]); denacc = R2.alloc("denacc", [128, MAIN])
    mk = R2.alloc("mk", [128, 256]); mkh = R2.alloc("mkh", [128, 256])
    smk_ = [R2.alloc("smk%d" % i, [128, 128]) for i in range(2)]; nmk = R2.alloc("nmk", [128, 128])
    Eb = [R2.alloc("Eb%d" % i, [128, 256], BF16) for i in range(2)]
    Pb = [R2.alloc("Pb%d" % i, [128, 256], BF16) for i in range(2)]
    kst = [R2.alloc("kst%d" % i, [128, 512]) for i in range(2)]
    kst_i = [0]
    sE = [R2.alloc("sE%d" % i, [128, 128]) for i in range(2)]
    sP = [R2.alloc("sP%d" % i, [128, 128], BF16) for i in range(2)]
    sPs = [R2.alloc("sPs%d" % i, [128, 8]) for i in range(2)]
    tkv = kst[0]
    rden = kst[1]
    cnt = [0]
    import collections
    nsE = R2.alloc("nsE", [128, 128])
    nsP = R2.alloc("nsP", [128, 128], BF16)
    plist = []
    pidx = {"dma": 0, "cast": 0, "p1": 0, "p2": 0}
    pcnt = [0]

    def pieces_begin(new):
        assert pidx["p1"] == len(plist) and pidx["p2"] == len(plist)
        plist[:] = new
        for k_ in pidx:
            pidx[k_] = 0

    def prologue():
        n = len(plist)
        while pidx["dma"] < min(2, n):
            plist[pidx["dma"]]["dma"](); pidx["dma"] += 1

    def pump(drain=False):
        while True:
            n = len(plist)
            j = pidx["p1"]
            if j >= n:
                if pidx["p2"] < n:
                    plist[pidx["p2"]]["p2"](); pidx["p2"] += 1
                return
            while pidx["dma"] < min(j + 2, n):
                plist[pidx["dma"]]["dma"](); pidx["dma"] += 1
            if pidx["cast"] <= j:
                plist[j]["cast"](); pidx["cast"] = j + 1
            plist[j]["p1"](); pidx["p1"] = j + 1
            if j >= 1:
                plist[j - 1]["p2"](); pidx["p2"] = j
            if j + 1 < n:
                plist[j + 1]["cast"](); pidx["cast"] = j + 2
            if j + 2 < n:
                plist[j + 2]["dma"](); pidx["dma"] = j + 3
            if not drain:
                return
    saccs = [psacc, psb[5]]

    items = []
    ps_n[0] = 5

    def mk_tkv(kv, cbase, g3):
        st_ = {}

        def dma():
            s_ = rings.stage()
            st_["s"] = s_
            DMA(s_.t[:, 0:4096].rearrange("p (a b) -> p a b", b=512), wsrc(w_in, cbase + g3 * 512, 512),
                writes=rings.W(s_))

        def loader():
            s_ = st_["s"]
            d_ = rings.bf[rings.j % len(rings.bf)]
            rings.j += 1
            q = "scalar" if (rings.j % 2 == 0) else "vector"
            COPY(q, d_.t[:, 0:4096], s_.t[:, 0:4096], rings.W(s_), [d_])
            return d_

        def body(wb):
            pb = PS()
            MMG([(pb.t[:, :], U(kc, S0, 128), wb.t[:, kc * 512:(kc + 1) * 512], kc == 0, kc == KC - 1)
                 for kc in range(KC)], [wb, uTm], [pb])
            COPY("scalar", tkv.t[:], pb.t[:, :], [pb], [tkv])
            w = GROUPS[g3][0]
            for b in range(nsc):
                DMA(kvo[g3][b, w - 8:w, kv * 512:(kv + 1) * 512], tkv.t[b * 8:(b + 1) * 8, :],
                    reads=[tkv])
        return (dma, loader, body)
    tkv_items = [mk_tkv(kv, cbase, g3) for kv, cbase in ((0, C_K), (1, C_V)) for g3 in range(3)]
    tkv_items[0][0]()
    for i_, (_, ld_, bd_) in enumerate(tkv_items):
        if i_ + 1 < len(tkv_items):
            tkv_items[i_ + 1][0]()
        bd_(ld_())

    sacc_first = [True]

    def mk_gh(h, g):
        win, dil = GROUPS[g]
        gh = g * 4 + h
        sacc = saccs[h % 2]
        par = (h * 3 + g) % 2
        smk = smk_[par]
        pcount = [0]

        def pstep():
            pump()

        def loader():
            return load_blocks(rings, w_in, [cb + g * 512 + h * 128 for cb in (C_Q, C_K, C_V)])

        def body(wb):
            if g == 0:
                sacc_first[0] = True
            prologue()
            DMA(mk.t[:], masks_d[gh], writes=[mk])
            DMA(smk.t[:], smask_d[gh], writes=[smk])
            DMA(nmk.t[:], nmask_d[gh], writes=[nmk])
            COPY("gpsimd", mkh.t[:, 128:256], mk.t[:, 128:256], [mk], [mkh])
            TS("vector", mkh.t[:, 0:128], mk.t[:, 0:128], flag.t[:, 0:1], None, ALU.mult, None, [mk, flag], [mkh])
            for (t0, wd) in MAIN_TILES + [SAMP_TILE]:
                pb = proj_fm(wb, 384, 0, t0, wd)
                COPY("scalar", QT.t[:, t0 - M0:t0 - M0 + wd], pb.t[:, 0:wd], [pb], [QT])
                pstep()
            ktiles = [(t0, wd) for (t0, wd) in HALO_TILES if t0 + wd > HALO - win] + MAIN_TILES + [SAMP_TILE]
            for (t0, wd) in ktiles:
                pb = proj_fm(wb, 384, 128, t0, wd)
                COPY("scalar", KT.t[:, t0:t0 + wd], pb.t[:, 0:wd], [pb], [KT])
                if M0 <= t0 < S0 and t0 + wd > S0 - win:
                    lo = max(t0, S0 - win)
                    ks = kst[kst_i[0] % 2]; kst_i[0] += 1
                    COPY("vector", ks.t[:, 0:t0 + wd - lo], pb.t[:, lo - t0:wd], [pb], [ks])
                    DMA(kTo[g][h, :, lo - (S0 - win): t0 + wd - (S0 - win)], ks.t[:, 0:t0 + wd - lo], reads=[ks])
                pstep()
            nmb = 1 + MAIN // (128 * dil)
            base = HALO - win
            blocks = [(r, mb) for r in range(dil) for mb in range(nmb)]
            for q0 in range(0, len(blocks), 4):
                grp = blocks[q0:q0 + 4]
                pb = PS()
                mms = []
                for bi, (r, mb) in enumerate(grp):
                    tstart = base + mb * 128 * dil + r
                    for kc in range(KC):
                        mms.append((pb.t[:, bi * 128:(bi + 1) * 128], U(kc, tstart, 128, dil),
                                    wb.t[:, kc * 384 + 256: kc * 384 + 384], kc == 0, kc == KC - 1))
                MMG(mms, [wb, uTm, uTh], [pb])
                COPY("scalar", Vb.t[:, q0:q0 + len(grp), :],
                     pb.t[:, 0:len(grp) * 128].rearrange("p (a b) -> p a b", b=128), [pb], [Vb])
                need = [(bi, r, mb) for bi, (r, mb) in enumerate(grp)
                        if mb >= 1 and (mb * 128 * dil + base) + 127 * dil + r >= S0 - win]
                if need:
                    ks = kst[kst_i[0] % 2]; kst_i[0] += 1
                    COPY("vector", ks.t[:, 0:len(grp) * 128], pb.t[:, 0:len(grp) * 128], [pb], [ks])
                    for (bi, r, mb) in need:
                        ts_ = base + mb * 128 * dil + r - (S0 - win)
                        DMA(vo[g][ts_: ts_ + 127 * dil + 1: dil, h, :], ks.t[:, bi * 128:(bi + 1) * 128], reads=[ks])
                pstep()
            pb = PS()
            MMG([(pb.t[:, 0:128], U(kc, S0, 128), wb.t[:, kc * 384 + 256: kc * 384 + 384], kc == 0, kc == KC - 1)
                 for kc in range(KC)], [wb, uTm], [pb])
            COPY("scalar", Vs.t[:], pb.t[:, 0:128], [pb], [Vs])
            for r in range(dil):
                for mb in range(1, nmb):
                    qstart = (mb - 1) * 128 * dil + r
                    qcols = slice(qstart, qstart + 127 * dil + 1, dil)
                    kprev = base + (mb - 1) * 128 * dil + r
                    kown = base + mb * 128 * dil + r
                    bprev = r * nmb + mb - 1
                    i2 = cnt[0] % 2; cnt[0] += 1
                    pS = PS()
                    MMG([(pS.t[:, 0:128], KT.t[:, kprev: kprev + 127 * dil + 1: dil], QT.t[:, qcols], True, True),
                         (pS.t[:, 128:256], KT.t[:, kown: kown + 127 * dil + 1: dil], QT.t[:, qcols], False, True)],
                        [KT, QT], [pS])
                    ACT(Eb[i2].t[:], pS.t[:, 0:256], AF.Exp, [pS], [Eb[i2]], scale=QSCALE)
                    mm_ = mkh if mb == 1 else mk
                    TT("gpsimd" if i2 else "vector", Pb[i2].t[:], Eb[i2].t[:], mm_.t[:], ALU.mult, [Eb[i2], mm_], [Pb[i2]])
                    pO = PS()
                    MMG([(pO.t[:, 0:128], Vb.t[:, bprev, :], Pb[i2].t[:, 0:128], True, False),
                         (pO.t[:, 0:128], Vb.t[:, bprev + 1, :], Pb[i2].t[:, 128:256], False, True),
                         (pO.t[:, 128:256], ones_bf.t[:], Pb[i2].t[:, 0:128], False, False),
                         (pO.t[:, 128:256], ones_bf.t[:], Pb[i2].t[:, 128:256], False, True)],
                        [Vb, Pb[i2], ones_bf], [pO])
                    if g == 0:
                        COPY("vector", numacc.t[:, qcols], pO.t[:, 0:128], [pO], [numacc])
                        COPY("vector", denacc.t[:, qcols], pO.t[:, 128:256], [pO], [denacc])
                    else:
                        TT("vector", numacc.t[:, qcols], pO.t[:, 0:128], numacc.t[:, qcols], ALU.add, [pO, numacc], [numacc])
                        TT("vector", denacc.t[:, qcols], pO.t[:, 128:256], denacc.t[:, qcols], ALU.add, [pO, denacc], [denacc])
            QS = QT.t[:, MAIN:NO]
            pS = PS()
            MMG([(pS.t[:, 0:128], KT.t[:, S0:NT], QS, True, True)], [KT, QT], [pS])
            ACT(nsE.t[:], pS.t[:, 0:128], AF.Exp, [pS], [nsE], scale=QSCALE)
            TT("gpsimd", nsP.t[:], nsE.t[:], nmk.t[:], ALU.mult, [nsE, nmk], [nsP])
            MMG([(sacc.t[:, 0:128], Vs.t[:], nsP.t[:], sacc_first[0], False),
                 (sacc.t[:, 128:256], ones_bf.t[:], nsP.t[:], False, False)], [Vs, nsP, ones_bf], [sacc])
            sacc_first[0] = False
            pump(drain=True)
            if g == 0 and h > 0:
                fin_sample(h - 1)
            COPY("vector", QSs[par].t[:], QT.t[:, MAIN:NO], [QT], [QSs[par]])
            pieces_begin([mk_cache(h, g, b) for b in range(nsc)])
            if g == 2:
                fin_prompt(h)
        return (loader, body)

    def mk_cache(h, g, b):
        win, dil = GROUPS[g]
        nb = win // 128
        gh = g * 4 + h
        sacc = saccs[h % 2]
        par = (h * 3 + g) % 2
        smk = smk_[par]

        st_ = {}

        def dma():
            s_ = rings.stage()
            sK, sV = rings.subs[id(s_)]
            st_["s"] = (sK, sV)
            DMA(sK.t[:, 0:win].rearrange("p (a b) -> p a b", b=win), kTc[g][b, h].unsqueeze(1), writes=[sK])
            DMA(sV.t[:, 2048:2048 + nb * 128].rearrange("p (a b) -> p a b", b=128),
                kvc[g][b, :, 512 + h * 128: 512 + (h + 1) * 128].rearrange("(a p) d -> p a d", p=128), writes=[sV])

        def cast():
            sK, sV = st_["s"]
            dK, dV = cslots[rings_c.j % len(cslots)]
            rings_c.j += 1
            st_["d"] = (dK, dV)
            qk, qv = ("vector", "scalar") if (rings_c.j % 2 == 0) else ("scalar", "vector")
            COPY(qk, dK.t[:, 0:win], sK.t[:, 0:win], [sK], [dK])
            COPY(qv, dV.t[:, 2048:2048 + nb * 128], sV.t[:, 2048:2048 + nb * 128], [sV], [dV])

        def part1():
            cb_ = st_["d"][0]
            QSb = QSs[par]
            QS = QSb.t[:]
            i2 = pcnt[0] % 2; pcnt[0] += 1
            st_["i2"] = i2
            pS = PS()
            MMG([(pS.t[:, blk * 8:(blk + 1) * 8], cb_.t[:, blk * 128:(blk + 1) * 128], QS[:, b * 8:(b + 1) * 8],
                  blk == 0, True) for blk in range(nb)], [cb_, QSb], [pS])
            ACT(sE[i2].t[:, 0:nb * 8], pS.t[:, 0:nb * 8], AF.Exp, [pS], [sE[i2]], scale=QSCALE)
            TT("vector", sP[i2].t[:, 0:nb * 8], sE[i2].t[:, 0:nb * 8], smk.t[:, 0:nb * 8], ALU.mult,
               [sE[i2], smk], [sP[i2]])
            if nb > 1:
                P.op("vector", (lambda a, bb: (lambda e: e.tensor_reduce(
                    out=a, in_=bb, axis=mybir.AxisListType.X, op=ALU.add)))(
                    sPs[i2].t[:, 0:8], sP[i2].t[:, 0:nb * 8].rearrange("p (a t) -> p t a", t=8)),
                    [sP[i2]], [sPs[i2]])

        def part2():
            cb_ = st_["d"][1]
            i2 = st_["i2"]
            if nb > 1:
                srcs, onesx = sPs[i2], ones_f
            else:
                srcs, onesx = sP[i2], ones_bf
            sden = srcs.t[:, 0:8]
            mms = [(sacc.t[:, b * 8:(b + 1) * 8], cb_.t[:, 2048 + blk * 128: 2048 + (blk + 1) * 128],
                    sP[i2].t[:, blk * 8:(blk + 1) * 8], False, False) for blk in range(nb)]
            mms.append((sacc.t[:, 128 + b * 8:128 + (b + 1) * 8], onesx.t[:], sden, False, False))
            MMG(mms, [cb_, sP[i2], srcs, onesx], [sacc])
        return {"dma": dma, "cast": cast, "p1": part1, "p2": part2}

    def fin_prompt(h):
        for c0 in range(0, MAIN, 512):
            RECIP(rden.t[:], denacc.t[:, c0:c0 + 512], [denacc], [rden])
            TT("vector", attnT.t[:, h, c0:c0 + 512], numacc.t[:, c0:c0 + 512], rden.t[:], ALU.mult,
               [numacc, rden], [attnT])

    def fin_sample(h):
        sacc = saccs[h % 2]
        RECIP(rden.t[:, 0:128], sacc.t[:, 128:256], [sacc], [rden])
        TT("vector", attnT.t[:, h, MAIN:NO], sacc.t[:, 0:128], rden.t[:, 0:128], ALU.mult, [sacc, rden], [attnT])

    for h in range(4):
        for g in range(3):
            items.append(mk_gh(h, g))
    run_stream(items, 0)
    pump(drain=True)
    fin_sample(3)
    ps_n[0] = 6
    P.barrier()
    if kstop <= 2:
        return finish()

    R3 = Region(big, o_hi, ARENA_WORDS)
    hgT = R3.alloc("hgT", [128, 8, NO], BF16)
    o_after_hg = R3.top
    wbf_hg = R3.alloc("wbf_hg", [128, 4096], BF16)
    Kt_ = [R3.alloc("Kt%d" % i, [128, 512], BF16) for i in range(2)]
    Kt2_ = [R3.alloc("Kt2%d" % i, [128, 512], BF16) for i in range(2)]
    Qt_ = [R3.alloc("Qt%d" % i, [128, 512], BF16) for i in range(2)]
    sgT_ = [R3.alloc("sgT%d" % i, [128, 512]) for i in range(2)]
    Vh_ = [R3.alloc("Vh%d" % i, [128, 4, 128], BF16) for i in range(2)]
    Sall = R3.alloc("Sall", [128, 8, 128], BF16)
    Sf = [R3.alloc("Sf%d" % i, [128, 128]) for i in range(2)]
    S0f = R3.alloc("S0f", [128, 16, 128])
    S0b = R3.alloc("S0b", [128, 16, 128], BF16)
    Ktm4_ = [R3.alloc("Ktm4%d" % i, [128, 512], BF16) for i in range(2)]
    Ktmm = [R3.alloc("Ktmm%d" % i, [128, 128], BF16) for i in range(2)]
    hb_base = R3.top
    HB = [[R3.alloc("hB%d_%d" % (i, k), [128, 512]) for k in range(6)] for i in range(2)]
    hg_stage = big[:, hb_base:hb_base + 4096]
    hg_stage_bufs = HB[0] + HB[1][:2]

    def load_hg(hh):
        for bi, cb in enumerate((C_QH, C_FH, C_IH, C_OG)):
            DMA(hg_stage.rearrange("p (a b) -> p a b", b=512)[:, :, bi * 128:(bi + 1) * 128],
                wsrc(w_in, cb + hh * 128, 128), writes=hg_stage_bufs)
        COPY("vector" if hh % 2 else "scalar", wbf_hg.t[:], hg_stage, hg_stage_bufs, [wbf_hg])
        return wbf_hg
    dec_ = [R3.alloc("h_dec%d" % i, [128, 16]) for i in range(2)]
    Am4_ = [R3.alloc("Am4%d" % i, [128, 512], BF16) for i in range(2)]
    osq = R3.alloc("osq", [128, 512]); rstd = R3.alloc("rstd", [128, 512]); otmp = R3.alloc("otmp", [128, 512])
    tcount = [0]
    bcount = [0]

    psB = [psb[4], psb[5], psacc]
    psB_i = [0]

    def PSB():
        b = psB[psB_i[0] % 3]
        psB_i[0] += 1
        return b

    def proj_bank(pb, wb, row, woff, t0, wd):
        MMG([(pb.t[:, 0:wd], wb.t[:, kc * row + woff: kc * row + woff + 128], U(kc, t0, wd), kc == 0, kc == KC - 1)
             for kc in range(KC)], [wb, uTm, uTh], [pb])
        return pb

    def hg_head(hh, wb):
        DMA(S0f.t[:], state_d[:, hh].rearrange("b k v -> k b v"), writes=[S0f])
        issue_copies((3 * nsc + 7) // 8)
        COPY("gpsimd", S0b.t[:], S0f.t[:], [S0f], [S0b])
        st_ = {"sidx": 0}
        MEMSET("vector", Sf[0].t[:], 0.0, [Sf[0]])
        tiles = HALO_TILES + MAIN_TILES + [SAMP_TILE]

        def ctx(ti):
            t0, wd = tiles[ti]
            c = dict(t0=t0, wd=wd, halo=t0 < M0, samp=t0 >= S0, par=ti % 2)
            c["C"] = 8 if c["samp"] else 64
            c["nchk"] = wd // c["C"]
            c["nblk"] = wd // 128
            c["o0"] = t0 - M0
            return c

        def stageA(ti):
            c = ctx(ti)
            t0, wd, par, samp, halo = c["t0"], c["wd"], c["par"], c["samp"], c["halo"]
            B1, B2, B3, B4, B5, B6 = HB[par]
            Vh, sgT = Vh_[par], sgT_[par]
            pF = proj_bank(psb[0], wb, 512, 128, t0, wd)
            pV = psb[1]
            mms = []
            for bi in range(c["nblk"]):
                for kc in range(KC):
                    mms.append((pV.t[:, bi * 128:(bi + 1) * 128], U(kc, t0 + bi * 128, 128),
                                wb.t[:, kc * 512 + 256: kc * 512 + 384], kc == 0, kc == KC - 1))
            MMG(mms, [wb, uTm, uTh], [pV])
            ACT(B1.t[:, 0:wd], pF.t[:, 0:wd], AF.Exp, [pF], [B1], scale=-1.0)
            ACT(B2.t[:, 0:wd], B1.t[:, 0:wd], AF.Ln, [B1, lbT], [B2], scale=lbT.t[:, hh:hh + 1], bias=1.0)
            ACT(B3.t[:, 0:wd], B1.t[:, 0:wd], AF.Ln, [B1], [B3], bias=1.0)
            COPY("scalar", Vh.t[:, 0:c["nblk"], :], pV.t[:, 0:wd].rearrange("p (a b) -> p a b", b=128), [pV], [Vh])
            TT("gpsimd", B2.t[:, 0:wd], B2.t[:, 0:wd], B3.t[:, 0:wd], ALU.subtract, [B2, B3], [B2])
            P.op("vector", (lambda o_, d0, d1: (lambda e: e.tensor_tensor_scan(
                out=o_, data0=d0, data1=d1, initial=0.0, op0=ALU.mult, op1=ALU.add)))(
                B4.t[:, 0:wd], reset.t[:, 1 if samp else 0, 0:wd], B2.t[:, 0:wd]), [reset, B2], [B4])
            if not halo:
                pQ = proj_bank(psb[2], wb, 512, 0, t0, wd)
                pG = proj_bank(psb[3], wb, 512, 384, t0, wd)
                ACT(B5.t[:, 0:wd], pQ.t[:, 0:wd], AF.Exp, [pQ], [B5], scale=-1.0)
                ACT(B5.t[:, 0:wd], B5.t[:, 0:wd], AF.Ln, [B5], [B5], bias=1.0)
                ACT(B5.t[:, 0:wd], B5.t[:, 0:wd], AF.Exp, [B5], [B5], scale=-1.0)
                TT("vector", B5.t[:, 0:wd], pQ.t[:, 0:wd], B5.t[:, 0:wd], ALU.mult, [pQ, B5], [B5])
                ACT(B6.t[:, 0:wd], pG.t[:, 0:wd], AF.Exp, [pG], [B6], scale=-1.0)
                ACT(B6.t[:, 0:wd], B6.t[:, 0:wd], AF.Ln, [B6], [B6], bias=1.0)
                ACT(B6.t[:, 0:wd], B6.t[:, 0:wd], AF.Exp, [B6], [B6], scale=-1.0)
                STT(sgT.t[:, 0:wd], pG.t[:, 0:wd], hgw.t[:, 0:1], B6.t[:, 0:wd], ALU.mult, ALU.mult,
                    [pG, hgw, B6], [sgT])

        def stageB(ti):
            c = ctx(ti)
            wd, par, C, nchk, halo = c["wd"], c["par"], c["C"], c["nchk"], c["halo"]
            B1, B2, B3, B4, B5, B6 = HB[par]
            Kt, Kt2, Qt, dec = Kt_[par], Kt2_[par], Qt_[par], dec_[par]
            ACT(B1.t[:, 0:wd], B3.t[:, 0:wd], AF.Exp, [B3], [B1], scale=-1.0)
            ACT(B6.t[:, 0:wd], B4.t[:, 0:wd], AF.Exp, [B4], [B6], scale=-1.0)
            ACT(dec.t[:, 0:nchk], B4.t[:, C - 1:wd:C], AF.Exp, [B4], [dec])
            if not halo:
                ACT(B2.t[:, 0:wd], B4.t[:, 0:wd], AF.Exp, [B4], [B2])
            TS("vector", B3.t[:, 0:wd], B1.t[:, 0:wd], nlb1.t[:, hh:hh + 1], lb1.t[:, hh:hh + 1], ALU.mult, ALU.add,
               [B1, nlb1, lb1], [B3])
            TT("gpsimd", Kt.t[:, 0:wd], B3.t[:, 0:wd], B6.t[:, 0:wd], ALU.mult, [B3, B6], [Kt])
            TT("gpsimd", Kt2.t[:, 0:wd].rearrange("p (c t) -> p c t", t=C), Kt.t[:, 0:wd].rearrange("p (c t) -> p c t", t=C),
               dec.t[:, 0:nchk].unsqueeze(2).broadcast_to([128, nchk, C]), ALU.mult, [Kt, dec], [Kt2])
            if not halo:
                TT("vector", Qt.t[:, 0:wd], B5.t[:, 0:wd], B2.t[:, 0:wd], ALU.mult, [B5, B2], [Qt])

        def stageC(ti):
            c = ctx(ti)
            t0, wd, par, C, samp, halo, nblk, o0 = c["t0"], c["wd"], c["par"], c["C"], c["samp"], c["halo"], c["nblk"], c["o0"]
            Kt, Kt2, Qt, sgT, Vh, dec = Kt_[par], Kt2_[par], Qt_[par], sgT_[par], Vh_[par], dec_[par]
            Ktm4, Am4 = Ktm4_[par], Am4_[par]
            sidx = st_["sidx"]
            for bl in range(nblk):
                TRANSPOSE(pst.t[:, bl * 128:(bl + 1) * 128], Kt2.t[:, bl * 128:(bl + 1) * 128], ident_bf.t[:],
                          [Kt2, ident_bf], [pst])
            COPY("scalar", Ktm4.t[:, 0:wd], pst.t[:, 0:wd], [pst], [Ktm4])
            if not samp:
                pDs = []
                for cc in range(2):
                    pD = PSB()
                    MMG([(pD.t[:, bl * 128:(bl + 1) * 128], Ktm4.t[cc * 64:(cc + 1) * 64, bl * 128:(bl + 1) * 128],
                          Vh.t[cc * 64:(cc + 1) * 64, bl, :], bl == 0, True) for bl in range(nblk)], [Ktm4, Vh], [pD])
                    pDs.append(pD)
                for ch in range(2 * nblk):
                    if not halo:
                        COPY("gpsimd", Sall.t[:, ch, :], Sf[sidx].t[:], [Sf[sidx]], [Sall])
                    bl, cc = ch // 2, ch % 2
                    pD = pDs[cc]
                    STT(Sf[1 - sidx].t[:], Sf[sidx].t[:], dec.t[:, ch:ch + 1], pD.t[:, bl * 128:(bl + 1) * 128], ALU.mult, ALU.add,
                        [Sf[sidx], dec, pD], [Sf[1 - sidx]])
                    sidx = 1 - sidx
            else:
                for q4 in range(4):
                    pD = PSB()
                    for q in range(4):
                        cc = q4 * 4 + q
                        i2 = cc % 2
                        TS("vector" if cc % 2 else "gpsimd", Ktmm[i2].t[:], Ktm4.t[:, 0:128], oh.t[:, cc:cc + 1], None, ALU.mult, None,
                           [Ktm4, oh], [Ktmm[i2]])
                        MMG([(pD.t[:, q * 128:(q + 1) * 128], Ktmm[i2].t[:], Vh.t[:, 0, :], q == 0, True)], [Ktmm[i2], Vh], [pD])
                    for q in range(4):
                        cc = q4 * 4 + q
                        STT(S0f.t[:, cc, :], S0f.t[:, cc, :], dec.t[:, cc:cc + 1], pD.t[:, q * 128:(q + 1) * 128], ALU.mult, ALU.add,
                            [S0f, dec, pD], [S0f])
            if t0 + wd == M0:
                TS("vector", Sf[1 - sidx].t[:], Sf[sidx].t[:], flag.t[:, 0:1], None, ALU.mult, None,
                   [Sf[sidx], flag], [Sf[1 - sidx]])
                sidx = 1 - sidx
            if t0 + wd == S0:
                DMA(s_out[hh], Sf[sidx].t[:], reads=[Sf[sidx]])
            if samp:
                DMA(state_o[:, hh].rearrange("b k v -> k b v"), S0f.t[:], reads=[S0f])
            st_["sidx"] = sidx
            if halo:
                return
            pA = PSB()
            MMG([(pA.t[:, bl * 128:(bl + 1) * 128], Kt.t[:, bl * 128:(bl + 1) * 128], Qt.t[:, bl * 128:(bl + 1) * 128],
                  bl == 0, True) for bl in range(nblk)], [Kt, Qt], [pA])
            TT("vector", Am4.t[:, 0:wd].rearrange("p (a b) -> p a b", b=128), pA.t[:, 0:wd].rearrange("p (a b) -> p a b", b=128),
               hmask.t[:, (1 if samp else 0):(2 if samp else 1), :].broadcast_to([128, nblk, 128]), ALU.mult, [pA, hmask], [Am4])
            pO = PSB()
            mms = []
            for bl in range(nblk):
                c0 = bl * 128
                mms.append((pO.t[:, c0:c0 + 128], Vh.t[:, bl, :], Am4.t[:, c0:c0 + 128], bl == 0, False))
                if samp:
                    for cc in range(16):
                        mms.append((pO.t[:, cc * 8:(cc + 1) * 8], S0b.t[:, cc, :], Qt.t[:, cc * 8:(cc + 1) * 8], False, cc == 15))
                else:
                    for cc in range(2):
                        mms.append((pO.t[:, c0 + cc * 64:c0 + (cc + 1) * 64], Sall.t[:, bl * 2 + cc, :],
                                    Qt.t[:, c0 + cc * 64:c0 + (cc + 1) * 64], False, cc == 1))
            MMG(mms, [Vh, Am4, S0b, Sall, Qt], [pO])
            ACT(osq.t[:, 0:wd], pO.t[:, 0:wd], AF.Square, [pO], [osq])
            pM = PSB()
            MMG([(pM.t[:, 0:wd], ones_r.t[:], osq.t[:, 0:wd], True, True)], [ones_r, osq], [pM])
            ACT(rstd.t[:, 0:wd], pM.t[:, 0:wd], AF.Ln, [pM], [rstd], bias=RMS_EPS)
            ACT(rstd.t[:, 0:wd], rstd.t[:, 0:wd], AF.Exp, [rstd], [rstd], scale=-0.5)
            TT("vector", otmp.t[:, 0:wd], pO.t[:, 0:wd], rstd.t[:, 0:wd], ALU.mult, [pO, rstd], [otmp])
            TT("gpsimd", hgT.t[:, hh, o0:o0 + wd], otmp.t[:, 0:wd], sgT.t[:, 0:wd], ALU.mult, [otmp, sgT], [hgT])

        n_t = len(tiles)
        stageA(0)
        for ti in range(n_t):
            stageB(ti)
            if ti + 1 < n_t:
                stageA(ti + 1)
            stageC(ti)

    run_stream([((lambda hh=hh: load_hg(hh)), (lambda wb, hh=hh: hg_head(hh, wb))) for hh in range(8)], 0)
    issue_copies(len(copy_jobs))
    P.barrier()
    if kstop <= 3:
        return finish()

    mg_m = Region(big, o_uTh, o_hi).alloc("mg_m", [128, KC, MAIN], BF16)

    def MG(c8, o0, wd):
        return mg_m.t[:, c8, o0:o0 + wd] if o0 < MAIN else mg_s.t[:, c8, 0:wd]

    R4 = Region(big, o_after_hg, ARENA_WORDS)
    rings = Rings(R4, 2, 3)
    ea_ = [R4.alloc("ea%d" % i, [128, 512]) for i in range(2)]
    eb_ = [R4.alloc("eb%d" % i, [128, 512]) for i in range(2)]
    m1_ = [R4.alloc("m1%d" % i, [128, 512]) for i in range(2)]
    p3cnt = [0]

    def p3a_load(j):
        s_ = rings.stage()
        d_ = rings.bf[rings.j % len(rings.bf)]
        rings.j += 1
        for bi, (w_, c_) in enumerate(((w_in, C_GA + j * 128), (w_in, C_GB + j * 128), (w_bb, j * 128))):
            DMA(s_.t[:, 0:3072].rearrange("p (a b) -> p a b", b=384)[:, :, bi * 128:(bi + 1) * 128],
                wsrc(w_, c_, 128), writes=[s_])
        DMA(s_.t[:, 3072:3584].rearrange("p (a b) -> p a b", b=128), wsrc(w_ba, j * 128, 128, kc=4), writes=[s_])
        COPY("scalar" if j % 2 else "vector", d_.t[:, 0:3584], s_.t[:, 0:3584], [s_], [d_])
        return d_

    def p3a_body(j, wb):
        for (t0, wd) in MAIN_TILES + [SAMP_TILE]:
            o0 = t0 - M0
            pa = proj_fm(wb, 384, 0, t0, wd)
            pbb = proj_fm(wb, 384, 128, t0, wd)
            pBA = PS()
            MMG([(pBA.t[:, 0:wd], wb.t[:, 3072 + c4 * 128: 3072 + (c4 + 1) * 128], attnT.t[:, c4, o0:o0 + wd],
                  c4 == 0, c4 == 3) for c4 in range(4)], [wb, attnT], [pBA])
            pBB = PS()
            MMG([(pBB.t[:, 0:wd], wb.t[:, c8 * 384 + 256: c8 * 384 + 384], hgT.t[:, c8, o0:o0 + wd],
                  c8 == 0, c8 == 7) for c8 in range(8)], [wb, hgT], [pBB])
            i2 = p3cnt[0] % 2
            p3cnt[0] += 1
            ea, eb, m1 = ea_[i2], eb_[i2], m1_[i2]
            ACT(ea.t[:, 0:wd], pa.t[:, 0:wd], AF.Exp, [pa], [ea], scale=-1.0)
            ACT(ea.t[:, 0:wd], ea.t[:, 0:wd], AF.Ln, [ea], [ea], bias=1.0)
            ACT(ea.t[:, 0:wd], ea.t[:, 0:wd], AF.Exp, [ea], [ea], scale=-1.0)
            TT("vector", m1.t[:, 0:wd], pBA.t[:, 0:wd], ea.t[:, 0:wd], ALU.mult, [pBA, ea], [m1])
            ACT(eb.t[:, 0:wd], pbb.t[:, 0:wd], AF.Exp, [pbb], [eb], scale=-1.0)
            ACT(eb.t[:, 0:wd], eb.t[:, 0:wd], AF.Ln, [eb], [eb], bias=1.0)
            ACT(eb.t[:, 0:wd], eb.t[:, 0:wd], AF.Exp, [eb], [eb], scale=-1.0)
            TT("vector", eb.t[:, 0:wd], pBB.t[:, 0:wd], eb.t[:, 0:wd], ALU.mult, [pBB, eb], [eb])
            TT("gpsimd", MG(j, o0, wd), m1.t[:, 0:wd], eb.t[:, 0:wd], ALU.add, [m1, eb], [mg_m, mg_s])

    run_stream([((lambda j=j: p3a_load(j)), (lambda wb, j=j: p3a_body(j, wb))) for j in range(KC)], 2)
    P.barrier()
    if kstop <= 4:
        return finish()

    RL = Region(big, o_uTm, o_uTh)
    RH = Region(big, o_hi, ARENA_WORDS)
    hid = RL.alloc("hid", [128, 32, 512], BF16)
    h1 = RL.alloc("h1", [128, KC, 512])
    x1 = RH.alloc("x1", [128, KC, 512])
    u2 = RH.alloc("u2", [128, KC, 512], BF16)
    rings = Rings(RH, 2, 3)
    xres = [RH.alloc("xres%d" % i, [128, 512]) for i in range(2)]
    sq_ = [RH.alloc("sq%d" % i, [128, 512]) for i in range(2)]
    m2_ = RH.alloc("m2", [128, 512]); rs_ = RH.alloc("rs", [128, 512])
    tA = RH.alloc("tA", [128, 512]); tB = RH.alloc("tB", [128, 512]); hs_ = RH.alloc("hs", [128, 512])
    yst = [RH.alloc("yst%d" % i, [128, 512]) for i in range(2)]

    def layer_norm(src, wd, emit_j):
        pM = PS()
        MMG([(pM.t[:, 0:wd], ones_m.t[:], src.t[:, j, 0:wd], j == 0, j == KC - 1) for j in range(KC)], [ones_m, src], [pM])
        pV = PS()
        for j in range(KC):
            s2 = sq_[j % 2]
            ACT(s2.t[:, 0:wd], src.t[:, j, 0:wd], AF.Square, [src], [s2])
            MMG([(pV.t[:, 0:wd], ones_m.t[:], s2.t[:, 0:wd], j == 0, j == KC - 1)], [ones_m, s2], [pV])
        ACT(m2_.t[:, 0:wd], pM.t[:, 0:wd], AF.Square, [pM], [m2_])
        TT("vector", m2_.t[:, 0:wd], pV.t[:, 0:wd], m2_.t[:, 0:wd], ALU.subtract, [pV, m2_], [m2_])
        ACT(rs_.t[:, 0:wd], m2_.t[:, 0:wd], AF.Ln, [m2_], [rs_], bias=LN_EPS)
        ACT(rs_.t[:, 0:wd], rs_.t[:, 0:wd], AF.Exp, [rs_], [rs_], scale=-0.5)
        for j in range(KC):
            TT("vector", tA.t[:, 0:wd], src.t[:, j, 0:wd], pM.t[:, 0:wd], ALU.subtract, [src, pM], [tA])
            TT("gpsimd", tB.t[:, 0:wd], tA.t[:, 0:wd], rs_.t[:, 0:wd], ALU.mult, [tA, rs_], [tB])
            emit_j(j)

    items = []

    def p3b_tile(t0, wd):
        samp = t0 >= S0
        o0 = t0 - M0

        def emit1(j):
            affine("gpsimd", x1.t[:, j, 0:wd], tB.t[:, 0:wd], g1t, j, c1b, j, samp, [tB], [x1], tmp=tA)
            affine("gpsimd", u2.t[:, j, 0:wd], tB.t[:, 0:wd], gA2, j, bA2, j, samp, [tB], [u2], tmp=tA)

        def emit2(j):
            ys = yst[j % 2]
            affine("gpsimd", ys.t[:, 0:wd], tB.t[:, 0:wd], g2t, j, b2t, j, False, [tB], [ys])
            DMA(yT[j, :, o0:o0 + wd], ys.t[:, 0:wd], reads=[ys])

        def wo_body(k, wo_b):
            for j in range(4 * k, 4 * k + 4):
                xr = xres[j % 2]
                DMA(xr.t[:, 0:wd], xT[j, :, t0:t0 + wd], writes=[xr])
                pX = PS()
                MMG([(pX.t[:, 0:wd], wo_b.t[:, c8 * 512 + (j % 4) * 128: c8 * 512 + (j % 4 + 1) * 128], MG(c8, o0, wd),
                      c8 == 0, c8 == 7) for c8 in range(8)], [wo_b, mg_m, mg_s], [pX])
                if not samp:
                    STT(h1.t[:, j, 0:wd], pX.t[:, 0:wd], G1a.t[:, j, 0:1], xr.t[:, 0:wd], ALU.mult, ALU.add,
                        [pX, G1a, xr], [h1])
                else:
                    TT("vector", v3(tA.t[:, 0:wd], 8), v3(pX.t[:, 0:wd], 8), bc(G1a, j, 1, 16, 8), ALU.mult, [pX, G1a], [tA])
                    TT("gpsimd", h1.t[:, j, 0:wd], tA.t[:, 0:wd], xr.t[:, 0:wd], ALU.add, [tA, xr], [h1])
            if k == 1:
                layer_norm(h1, wd, emit1)
        for k in range(2):
            items.append(((lambda k=k: rings.load([(0, KC, 512, wsrc(w_out, k * 512, 512))])),
                          (lambda wb, k=k: wo_body(k, wb))))

        def wu_body(gq, wu):
            for fc in range(4 * gq, 4 * gq + 4):
                pH = PS()
                MMG([(pH.t[:, 0:wd], wu.t[:, kc * 512 + (fc % 4) * 128: kc * 512 + (fc % 4 + 1) * 128], u2.t[:, kc, 0:wd],
                      kc == 0, kc == KC - 1) for kc in range(KC)], [wu, u2], [pH])
                ACT(hs_.t[:, 0:wd], pH.t[:, 0:wd], AF.Identity, [pH, bup], [hs_], bias=bup.t[:, fc:fc + 1])
                STT(hid.t[:, fc, 0:wd], hs_.t[:, 0:wd], 0.0, hs_.t[:, 0:wd], ALU.max, ALU.mult, [hs_], [hid])
        for gq in range(8):
            items.append(((lambda gq=gq: rings.load([(0, KC, 512, wsrc(w_up, gq * 512, 512))])),
                          (lambda wb, gq=gq: wu_body(gq, wb))))

        def wd_body(j, wd_b):
            pF = PS()
            MMG([(pF.t[:, 0:wd], wd_b.t[:, c * 128:(c + 1) * 128], hid.t[:, c, 0:wd], c == 0, c == 31)
                 for c in range(32)], [wd_b, hid], [pF])
            if not samp:
                STT(h1.t[:, j, 0:wd], pF.t[:, 0:wd], G2a.t[:, j, 0:1], x1.t[:, j, 0:wd], ALU.mult, ALU.add,
                    [pF, G2a, x1], [h1])
            else:
                TT("vector", v3(tA.t[:, 0:wd], 8), v3(pF.t[:, 0:wd], 8), bc(G2a, j, 1, 16, 8), ALU.mult, [pF, G2a], [tA])
                TT("gpsimd", h1.t[:, j, 0:wd], tA.t[:, 0:wd], x1.t[:, j, 0:wd], ALU.add, [tA, x1], [h1])
            if j == KC - 1:
                layer_norm(h1, wd, emit2)
        for j in range(KC):
            items.append(((lambda j=j: rings.load([(0, 32, 128, w_down[:, j * 128:(j + 1) * 128].rearrange(
                "(k p) n -> p k n", p=128))])), (lambda wb, j=j: wd_body(j, wb))))

    for (t0, wd) in MAIN_TILES + [SAMP_TILE]:
        p3b_tile(t0, wd)
    run_stream(items, 2)

    return finish()


_NC_CACHE = {}
KSTOP = [99]
DEV = {"ncores": NCORES, "nsc": 16}


def _fm(v, n):
    return np.ascontiguousarray(np.asarray(v, np.float32).reshape(n, 128).T)


def kernel(x_prompt, x_sample, c_prompt, c_sample, cache_kv_w128, cache_kv_w512, cache_kv_w2048,
           state_hgrn, w_ada, b_ada, w_in, lb_param, hg_norm_w, w_branch_a, w_branch_b, w_out,
           ln1_g, ln1_b, w_up, b_up, w_down, b_down, ln2_g, ln2_b):
    f = lambda a: np.asarray(a, np.float32)
    x_prompt, x_sample, c_prompt, c_sample = f(x_prompt), f(x_sample), f(c_prompt), f(c_sample)
    caches = [f(cache_kv_w128), f(cache_kv_w512), f(cache_kv_w2048)]
    state_hgrn = f(state_hgrn)
    tabs = const_tables()
    common = dict(
        w_ada=np.ascontiguousarray(f(w_ada)[0]), b_ada=_fm(f(b_ada)[0], 48), w_in=np.ascontiguousarray(f(w_in)[0]),
        lbp=np.ascontiguousarray(f(lb_param).reshape(2, 8, 128).transpose(2, 0, 1)),
        hgw=np.ascontiguousarray(f(hg_norm_w)[0].reshape(128, 1)),
        w_ba=np.ascontiguousarray(f(w_branch_a)[0]), w_bb=np.ascontiguousarray(f(w_branch_b)[0]),
        w_out=np.ascontiguousarray(f(w_out)[0]),
        ln1g=_fm(f(ln1_g)[0], 8), ln1b=_fm(f(ln1_b)[0], 8), ln2g=_fm(f(ln2_g)[0], 8), ln2b=_fm(f(ln2_b)[0], 8),
        w_up=np.ascontiguousarray(f(w_up)[0]), bup=_fm(f(b_up)[0], 32),
        w_down=np.ascontiguousarray(f(w_down)[0]), bdown=_fm(f(b_down)[0], 8),
        masks=tabs["masks"], smask=tabs["smask"], nmask=tabs["nmask"], hmask=tabs["hmask"],
        reset=tabs["reset"], oh=tabs["oh"], ident=np.eye(128, dtype=np.float32),
    )
    in_maps = []
    for c in range(NCORES):
        b, half = c // 2, c % 2
        T0 = half * MAIN
        halo = x_prompt[b, T0 - HALO:T0] if half == 1 else np.zeros((HALO, D), np.float32)
        toks = np.concatenate([halo, x_prompt[b, T0:T0 + MAIN], x_sample[16 * c:16 * c + 16].reshape(SAMP, D)], axis=0)
        cs = np.concatenate([c_prompt[b:b + 1], c_sample[16 * c:16 * c + 16]], axis=0)
        m = dict(common)
        m["xT"] = np.ascontiguousarray(toks.T.reshape(KC, 128, NT))
        m["cT"] = np.ascontiguousarray(cs.T.reshape(KC, 128, 17).transpose(1, 0, 2))
        m["flag"] = np.full((128, 1), float(half), np.float32)
        for (w, _), cache in zip(GROUPS, caches):
            nsc = DEV["nsc"]
            cc = cache[0, 16 * c:16 * c + nsc]
            m["kv%d" % w] = np.ascontiguousarray(cc.reshape(nsc, w, 1024))
            m["kT%d" % w] = np.ascontiguousarray(cc[:, :, 0].transpose(0, 2, 3, 1))
        m["state"] = np.ascontiguousarray(state_hgrn[0, 16 * c:16 * c + 16])
        in_maps.append(m)
    if "nc" not in _NC_CACHE:
        _NC_CACHE["nc"] = build_nc(KSTOP[0], DEV["nsc"])
    ncr = DEV["ncores"]
    if DEV.get("trace"):
        res = run_bass_kernel_spmd(_NC_CACHE["nc"], in_maps[:ncr], core_ids=list(range(ncr)), trace=True)
        print("DEV exec_time_ns", res.exec_time_ns, flush=True)
    else:
        res = run_bass_kernel_spmd(_NC_CACHE["nc"], in_maps[:ncr], core_ids=list(range(ncr)))
    R = res.results
    nsc = DEV["nsc"]
    B, S = x_prompt.shape[0], x_prompt.shape[1]
    y_prompt = np.empty((B, S, D), np.float32)
    y_sample = np.empty((128, 8, D), np.float32)
    pk = [np.empty((1, B, w, 2, 4, 128), np.float32) for (w, _) in GROUPS]
    php = np.empty((1, B, 8, 128, 128), np.float32)
    sk = [np.empty((1, 128, w, 2, 4, 128), np.float32) for (w, _) in GROUPS]
    shs = np.empty((1, 128, 8, 128, 128), np.float32)
    for c in range(ncr):
        b, half = c // 2, c % 2
        T0 = half * MAIN
        yt = np.asarray(R[c]["yT"]).reshape(D, NO)
        y_prompt[b, T0:T0 + MAIN] = yt[:, :MAIN].T
        y_sample[16 * c:16 * c + 16] = yt[:, MAIN:].T.reshape(16, 8, D)
        for gi, (w, _) in enumerate(GROUPS):
            sk[gi][0, 16 * c:16 * c + nsc] = np.asarray(R[c]["kvo%d" % w]).reshape(nsc, w, 2, 4, 128)
            if half == 1:
                pk[gi][0, b, :, 0] = np.asarray(R[c]["kTo%d" % w]).transpose(2, 0, 1)
                pk[gi][0, b, :, 1] = np.asarray(R[c]["vo%d" % w])
        if half == 1:
            php[0, b] = np.asarray(R[c]["s_out"])
        shs[0, 16 * c:16 * c + 16] = np.asarray(R[c]["state_o"])
    return (y_prompt, y_sample, pk[0], pk[1], pk[2], php, sk[0], sk[1], sk[2], shs)
```
